# Optimizing a Trainium2 kernel written in Bass

```python
import jax, jax.numpy as jnp
from jax import lax
import numpy as np

D_MODEL = 1024
BATCH = 8
SEQ = 4096
DEPTH = 1

GRID_W = 64
CTX_LEN = 256
N_Q_HEADS = 8
N_KV_HEADS = 2
HEAD_DIM = 64
GROUP = N_Q_HEADS // N_KV_HEADS
ATTN_DIM = N_Q_HEADS * HEAD_DIM
KV_DIM = N_KV_HEADS * HEAD_DIM
CONV_DIM = D_MODEL - ATTN_DIM
MIX_DIM = ATTN_DIM + CONV_DIM
CONV_K = 3
OFF_K = ATTN_DIM
OFF_V = OFF_K + KV_DIM
OFF_B = OFF_V + KV_DIM
OFF_C = OFF_B + CONV_DIM
OFF_U = OFF_C + CONV_DIM
MIX_IN_DIM = OFF_U + CONV_DIM
D_FF = 2816
N_MOD = 9
ROPE_THETA = 10000.0
Q_BLOCK = 128
ATTN_SCALE = HEAD_DIM ** -0.5
EPS = 1e-6

kernel_name = "hybrid_dit_attn_shortconv_macaron"


def rmsnorm(x, g):
    x32 = x.astype(jnp.float32)
    y = x32 * lax.rsqrt(jnp.mean(x32 * x32, axis=-1, keepdims=True) + EPS)
    return (y * g.astype(jnp.float32)).astype(x.dtype)


def modulate(h, shift, scale):
    return h * (1 + scale) + shift


def swiglu(h, w_in, w_out):
    gate, up = jnp.split(h @ w_in, 2, axis=-1)
    return (jax.nn.silu(gate) * up) @ w_out


def axial_rope_tables(n_tokens):
    rows = n_tokens // GRID_W
    row = jnp.repeat(jnp.arange(rows, dtype=jnp.float32), GRID_W)
    col = jnp.tile(jnp.arange(GRID_W, dtype=jnp.float32), rows)
    n_freq = HEAD_DIM // 4
    inv = ROPE_THETA ** (-jnp.arange(n_freq, dtype=jnp.float32) / n_freq)
    ang = jnp.concatenate([row[:, None] * inv, col[:, None] * inv], axis=-1)
    return jnp.cos(ang)[:, None, :], jnp.sin(ang)[:, None, :]


def apply_rope(x, cos, sin):
    xr = x.astype(jnp.float32).reshape(x.shape[:-1] + (HEAD_DIM // 2, 2))
    x0, x1 = xr[..., 0], xr[..., 1]
    out = jnp.stack([x0 * cos - x1 * sin, x0 * sin + x1 * cos], axis=-1)
    return out.reshape(x.shape).astype(x.dtype)


def gqa_softmax(q, k, v):
    s = jnp.einsum('bqkgd,bskd->bkgqs', q, k, preferred_element_type=jnp.float32) * ATTN_SCALE
    p = jax.nn.softmax(s, axis=-1).astype(v.dtype)
    return jnp.einsum('bkgqs,bskd->bqkgd', p, v)


def latent_attention(q, k, v, kc, vc):
    b, s = q.shape[:2]
    nb = s // Q_BLOCK
    k_all = jnp.concatenate([kc, k], axis=1)
    v_all = jnp.concatenate([vc, v], axis=1)
    qb = q.reshape(b, nb, Q_BLOCK, N_KV_HEADS, GROUP, HEAD_DIM).transpose(1, 0, 2, 3, 4, 5)
    o = lax.map(lambda qi: gqa_softmax(qi, k_all, v_all), qb)
    return o.transpose(1, 0, 2, 3, 4, 5).reshape(b, s, ATTN_DIM)


def context_attention(q, kc, vc):
    b, l = q.shape[:2]
    o = gqa_softmax(q.reshape(b, l, N_KV_HEADS, GROUP, HEAD_DIM), kc, vc)
    return o.reshape(b, l, ATTN_DIM)


def short_conv3(u, w):
    up = jnp.pad(u, ((0, 0), (1, 1), (0, 0)))
    return up[:, :-2] * w[0] + up[:, 1:-1] * w[1] + up[:, 2:] * w[2]


def split_heads(p, q_norm, k_norm):
    b, l = p.shape[:2]
    q = rmsnorm(p[..., :OFF_K].reshape(b, l, N_Q_HEADS, HEAD_DIM), q_norm)
    k = rmsnorm(p[..., OFF_K:OFF_V].reshape(b, l, N_KV_HEADS, HEAD_DIM), k_norm)
    v = p[..., OFF_V:OFF_B].reshape(b, l, N_KV_HEADS, HEAD_DIM)
    gb, gc, u = p[..., OFF_B:OFF_C], p[..., OFF_C:OFF_U], p[..., OFF_U:]
    return q, k, v, gb, gc, u


def setup_inputs(seed: int = 0) -> dict:
    key = jax.random.key(seed)
    ks = jax.random.split(key, 24)
    f32 = jnp.float32
    nrm = lambda k, shape, s: jax.random.normal(k, shape, f32) * s
    gain = lambda k, shape: 1.0 + 0.05 * jax.random.normal(k, shape, f32)
    return {
        "x": nrm(ks[0], (BATCH, SEQ, D_MODEL), 1.0),
        "c": nrm(ks[1], (BATCH, D_MODEL), 1.0),
        "ctx": nrm(ks[2], (BATCH, CTX_LEN, D_MODEL), 1.0),
        "c_ctx": nrm(ks[3], (D_MODEL,), 1.0),
        "w_ada": nrm(ks[4], (DEPTH, D_MODEL, N_MOD * D_MODEL), 0.5 * D_MODEL ** -0.5),
        "b_ada": nrm(ks[5], (DEPTH, N_MOD * D_MODEL), 0.02),
        "norm_ffn1": gain(ks[6], (DEPTH, D_MODEL)),
        "w_ffn1_in": nrm(ks[7], (DEPTH, D_MODEL, 2 * D_FF), D_MODEL ** -0.5),
        "w_ffn1_out": nrm(ks[8], (DEPTH, D_FF, D_MODEL), D_FF ** -0.5),
        "norm_mix": gain(ks[9], (DEPTH, D_MODEL)),
        "w_mix_in": nrm(ks[10], (DEPTH, D_MODEL, MIX_IN_DIM), D_MODEL ** -0.5),
        "q_norm": gain(ks[11], (DEPTH, HEAD_DIM)),
        "k_norm": gain(ks[12], (DEPTH, HEAD_DIM)),
        "conv_w": nrm(ks[13], (DEPTH, CONV_K, CONV_DIM), CONV_K ** -0.5),
        "attn_out_norm": gain(ks[14], (DEPTH, ATTN_DIM)),
        "conv_out_norm": gain(ks[15], (DEPTH, CONV_DIM)),
        "w_mix_out": nrm(ks[16], (DEPTH, MIX_DIM, D_MODEL), MIX_DIM ** -0.5),
        "norm_ffn2": gain(ks[17], (DEPTH, D_MODEL)),
        "w_ffn2_in": nrm(ks[18], (DEPTH, D_MODEL, 2 * D_FF), D_MODEL ** -0.5),
        "w_ffn2_out": nrm(ks[19], (DEPTH, D_FF, D_MODEL), D_FF ** -0.5),
        "final_norm": gain(ks[20], (D_MODEL,)),
    }


def reference(x, c, ctx, c_ctx, w_ada, b_ada, norm_ffn1, w_ffn1_in, w_ffn1_out,
              norm_mix, w_mix_in, q_norm, k_norm, conv_w, attn_out_norm, conv_out_norm,
              w_mix_out, norm_ffn2, w_ffn2_in, w_ffn2_out, final_norm):
    b, s, d = x.shape
    cos, sin = axial_rope_tables(s)

    for l in range(DEPTH):
        update_ctx = l < DEPTH - 1
        m = (jax.nn.silu(c) @ w_ada[l] + b_ada[l]).reshape(b, 1, N_MOD, d)
        mc = (jax.nn.silu(c_ctx) @ w_ada[l] + b_ada[l]).reshape(1, 1, N_MOD, d)

        x = x + 0.5 * m[:, :, 2] * swiglu(modulate(rmsnorm(x, norm_ffn1[l]), m[:, :, 0], m[:, :, 1]),
                                          w_ffn1_in[l], w_ffn1_out[l])
        ctx = ctx + 0.5 * mc[:, :, 2] * swiglu(modulate(rmsnorm(ctx, norm_ffn1[l]), mc[:, :, 0], mc[:, :, 1]),
                                               w_ffn1_in[l], w_ffn1_out[l])

        px = modulate(rmsnorm(x, norm_mix[l]), m[:, :, 3], m[:, :, 4]) @ w_mix_in[l]
        pc = modulate(rmsnorm(ctx, norm_mix[l]), mc[:, :, 3], mc[:, :, 4]) @ w_mix_in[l]
        q, k, v, gb, gc, u = split_heads(px, q_norm[l], k_norm[l])
        qc, kc, vc, gbc, gcc, uc = split_heads(pc, q_norm[l], k_norm[l])

        a = latent_attention(apply_rope(q, cos, sin), apply_rope(k, cos, sin), v, kc, vc)
        sc = gb * short_conv3(gc * u, conv_w[l])
        mix = jnp.concatenate([rmsnorm(a, attn_out_norm[l]), rmsnorm(sc, conv_out_norm[l])], axis=-1)
        x = x + m[:, :, 5] * (mix @ w_mix_out[l])

        if update_ctx:
            ac = context_attention(qc, kc, vc)
            scc = gbc * short_conv3(gcc * uc, conv_w[l])
            mixc = jnp.concatenate([rmsnorm(ac, attn_out_norm[l]), rmsnorm(scc, conv_out_norm[l])], axis=-1)
            ctx = ctx + mc[:, :, 5] * (mixc @ w_mix_out[l])
            ctx = ctx + 0.5 * mc[:, :, 8] * swiglu(modulate(rmsnorm(ctx, norm_ffn2[l]), mc[:, :, 6], mc[:, :, 7]),
                                                   w_ffn2_in[l], w_ffn2_out[l])

        x = x + 0.5 * m[:, :, 8] * swiglu(modulate(rmsnorm(x, norm_ffn2[l]), m[:, :, 6], m[:, :, 7]),
                                          w_ffn2_in[l], w_ffn2_out[l])

    return rmsnorm(x, final_norm)
```

```python
import numpy as np
from contextlib import ExitStack
import concourse.bass as bass
import concourse.mybir as mybir
from concourse.bass_utils import run_bass_kernel_spmd

F32 = mybir.dt.float32
BF16 = mybir.dt.bfloat16
AF = mybir.ActivationFunctionType
ALU = mybir.AluOpType

D = 1024
S = 4096
CTX = 256
DFF = 2816
NF = 22
EPS = 1e-6
TB = 512
NB = S // TB
NKC = 34
NS = 5
SLOT = 3072
SEM_LIMIT = 30000

U_F1 = 0
U_F2 = 30
U_Q = 60
U_CV = 62
U_MO = 66
NU = 69

NPAR = 48 + 72


class Buf:
    __slots__ = ("name", "w", "rs", "psum")

    def __init__(self, name="", psum=False):
        self.name = name
        self.w = None
        self.rs = []
        self.psum = psum


class DSem:
    def __init__(self, sem):
        self.sem = sem
        self.count = 0


class Eng:
    def __init__(self, name, h, sems, is_pe=False):
        self.name = name
        self.h = h
        self.sems = list(sems)
        self.sem = self.sems.pop(0)
        self.n = 0
        self.known = {}
        self.is_pe = is_pe
        self.skip_waw = False
        self.mine = set([id(self.sem)])

    def wait(self, ev):
        if ev is None:
            return
        sem, val = ev
        k = id(sem)
        if self.known.get(k, 0) >= val:
            return
        self.h.wait_ge(sem, val)
        self.known[k] = val

    def tick(self, inst):
        if self.n >= SEM_LIMIT:
            self.sem = self.sems.pop(0)
            self.mine.add(id(self.sem))
            self.n = 0
        self.n += 1
        inst.then_inc(self.sem, 1)
        return (self.sem, self.n)


class KB:
    def __init__(self, nc, mk_sem):
        self.nc = nc
        self.pe = Eng("pe", nc.tensor, [mk_sem("pe%d" % i) for i in range(2)], is_pe=True)
        self.act = Eng("act", nc.scalar, [mk_sem("act%d" % i) for i in range(2)])
        self.dve = Eng("dve", nc.vector, [mk_sem("dve%d" % i) for i in range(3)])
        self.pool = Eng("pool", nc.gpsimd, [mk_sem("pool%d" % i) for i in range(2)])
        self.sp = Eng("sp", nc.sync, [mk_sem("sp%d" % i) for i in range(1)])
        self.act.skip_waw = True

    def _deps(self, eng, reads, writes):
        def need(ev):
            if ev is None:
                return
            if eng.is_pe and id(ev[0]) in eng.mine:
                return
            eng.wait(ev)
        for b in reads:
            need(b.w)
            if b.psum:
                for ev in b.rs:
                    if id(ev[0]) not in eng.mine:
                        need(ev)
        for b in writes:
            if not (eng.skip_waw and b.w is not None and id(b.w[0]) in eng.mine):
                need(b.w)
            for ev in b.rs:
                need(ev)

    def _commit(self, ev, reads, writes):
        for b in reads:
            b.rs.append(ev)
            if len(b.rs) > 48:
                best = {}
                for s, v in b.rs:
                    if id(s) not in best or best[id(s)][1] < v:
                        best[id(s)] = (s, v)
                b.rs = list(best.values())
        for b in writes:
            b.w = ev
            b.rs = []

    def op(self, eng, fn, reads=(), writes=()):
        self._deps(eng, reads, writes)
        inst = fn()
        ev = eng.tick(inst)
        self._commit(ev, reads, writes)
        return ev

    def dma(self, eng, dsem, out, in_, reads=(), writes=(), **kw):
        self._deps(eng, reads, writes)
        inst = eng.h.dma_start(out=out, in_=in_, **kw)
        dsem.count += 16
        inst.then_inc(dsem.sem, 16)
        ev = (dsem.sem, dsem.count)
        self._commit(ev, reads, writes)
        return ev


QDBG = [None]
NFILL = [0]
PFLIM = [10 ** 9]


def build_program(debug=False, stop_after=None, limit=4, blimit=9):
    nc = bass.Bass("TRN2", target_bir_lowering=False)

    def din(name, shape, dt=F32):
        return nc.dram_tensor(name, list(shape), dt, kind="ExternalInput").ap()

    x_d = din("x", [S, D])
    ctx_d = din("ctx", [CTX, D])
    cc_d = din("cc", [128, 2, 8])
    wada_d = din("w_ada", [D, 9 * D])
    bada_d = din("b_ada", [1, 9 * D])
    w1i_d = din("w1i", [D, 2 * DFF])
    w1o_d = din("w1o", [DFF, D])
    wmi_d = din("wmi", [D, 2304])
    wmo_d = din("wmo", [D, D])
    w2i_d = din("w2i", [D, 2 * DFF])
    w2o_d = din("w2o", [DFF, D])
    par_d = din("params", [128, NPAR])
    fn_d = din("fnorm", [1, D])
    cm_d = din("cmat", [128, 640])
    rope_d = din("rope", [128, 2, S])
    out_d = nc.dram_tensor("out", [S, D], F32, kind="ExternalOutput").ap()

    skind = "ExternalOutput" if debug else "Internal"
    wscr_d = nc.dram_tensor("wscr", [NU, 128, SLOT], BF16, kind="Internal").ap()
    x1s_d = nc.dram_tensor("x1s", [S, D], F32, kind=skind).ap()
    h2s_d = nc.dram_tensor("h2s", [128, 8, S], BF16, kind="Internal").ap()
    if debug:
        dbg_kt = nc.dram_tensor("dbg_kt", [128, NKC * 128], F32, kind="ExternalOutput").ap()
        dbg_va = nc.dram_tensor("dbg_va", [128, NKC * 192], F32, kind="ExternalOutput").ap()
        dbg_mt = nc.dram_tensor("dbg_mt", [128, 144], F32, kind="ExternalOutput").ap()
        dbg_g = nc.dram_tensor("dbg_g", [128, 4 * D], F32, kind="ExternalOutput").ap()
        dbg_mix = nc.dram_tensor("dbg_mix", [128, 8 * TB], F32, kind="ExternalOutput").ap()
        dbg_x2 = nc.dram_tensor("dbg_x2", [TB, D], F32, kind="ExternalOutput").ap()

    with ExitStack() as es:
        def sb(name, shape, dt):
            return es.enter_context(nc.sbuf_tensor("sb_" + name, list(shape), dt))

        def mk_sem(name):
            return es.enter_context(nc.semaphore("sem_" + name))

        kb = KB(nc, mk_sem)
        PE, ACT, DVE, POOL, SP = kb.pe, kb.act, kb.dve, kb.pool, kb.sp
        V = nc.vector
        A_ = nc.scalar
        T_ = nc.tensor
        G_ = nc.gpsimd

        KT = sb("KT", [128, NKC * 128], BF16)
        VA = sb("VA", [128, NKC, 192], BF16)
        Wkv = sb("Wkv", [128, 8, 256], BF16)
        cmat = sb("cmat", [128, 640], F32)
        identb = sb("identb", [128, 128], BF16)
        par = sb("par", [128, NPAR], F32)
        mods = sb("mods", [128, 10, 8], F32)
        mT = sb("mT", [128, 72, 2], F32)
        Gt = [sb("G%d" % i, [128, D], F32) for i in range(4)]
        fnbc = sb("fnbc", [128, D], F32)
        ring = [sb("ring%d" % i, [128, SLOT], BF16) for i in range(NS)]
        XB = [sb("XB%d" % i, [128, 4, D], F32) for i in range(2)]
        xn = sb("xn", [128, 4, D], BF16)
        hT = sb("hT", [128, 8, TB], BF16)
        h2T = sb("h2T", [128, 8, TB + 4], BF16)
        actT = sb("actT", [128, NF, TB], BF16)
        TT = [sb("T%d" % i, [128, 512], F32) for i in range(8)]
        ropeT = sb("ropeT", [128, 2, TB], F32)
        QT = sb("QT", [128, 4, TB], BF16)
        PTe = [sb("PTe%d" % i, [128, 1024], BF16) for i in range(3)]
        mixT = sb("mixT", [128, 8, TB], BF16)
        junk = sb("junk", [128, D], BF16)
        stg1 = sb("stg1", [128, SLOT], BF16)
        stat = sb("stat", [128, 16], F32)
        Sada = sb("Sada", [128, 8, 33], F32)
        cct = sb("cct", [128, 2, 8], F32)
        epsT = sb("epsT", [128, 1], F32)

        PS = [es.enter_context(nc.psum_tensor("PS%d" % i, [128, 1024], F32)) for i in range(4)]
        PSb = [[Buf("PS%d_%d" % (i, h), psum=True) for h in range(2)] for i in range(4)]

        def psh(i, h, n=512):
            return PS[i][:, h * 512:h * 512 + n]

        B = {}

        def bf(name):
            if name not in B:
                B[name] = Buf(name)
            return B[name]

        ringb = [Buf("ring%d" % i) for i in range(NS)]
        ring_ds = [DSem(mk_sem("ringl%d" % i)) for i in range(NS)]
        ring_dc = [DSem(mk_sem("ringc%d" % i)) for i in range(NS)]
        wkv_ld = DSem(mk_sem("wkvl"))
        stg_dc = [DSem(mk_sem("stgc%d" % i)) for i in range(2)]
        stg_ss = [DSem(mk_sem("stgs%d" % i)) for i in range(2)]
        ring_ss = [DSem(mk_sem("rings%d" % i)) for i in range(NS)]
        scrb = [Buf("scr%d" % u) for u in range(NU)]
        XBb = [[Buf("XB%d_%d" % (i, t)) for t in range(4)] for i in range(2)]
        XB_ld = [DSem(mk_sem("xbl%d" % i)) for i in range(2)]
        XB_st = [DSem(mk_sem("xbs%d" % i)) for i in range(2)]
        TTb = [Buf("T%d" % i) for i in range(8)]
        misc_ld = DSem(mk_sem("miscl"))
        misc_st = DSem(mk_sem("miscs"))
        ada_ld = [DSem(mk_sem("adal%d" % i)) for i in range(2)]
        h2_ld = DSem(mk_sem("h2l"))
        h2_st = DSem(mk_sem("h2s"))
        rope_ld = DSem(mk_sem("ropel"))
        dbg_st = DSem(mk_sem("dbgs"))
        dbg_s2 = [DSem(mk_sem("dbgs2_%d" % i)) for i in range(2)]
        x1sb = [Buf("x1s%d" % b) for b in range(NB)]
        h2sb = [Buf("h2s%d" % b) for b in range(NB)]

        ident_f = cmat[:, 0:128]
        Rm = cmat[:, 128:256]
        bd64 = cmat[:, 256:384]
        o512 = cmat[:, 384:512]
        ones_m = cmat[:, 512:640]

        g1 = par[:, 0:8]
        gm = par[:, 8:16]
        g2 = par[:, 16:24]
        qg = par[:, 24:25]
        kg = par[:, 25:26]
        ga = par[:, 26:30]
        gcv = par[:, 30:34]
        cw = par[:, 34:46]
        bT = par[:, 48:120]

        M_A1, M_B1, M_A1c, M_B1c, M_A2, M_B2, M_A2c, M_B2c, M_A3, M_B3 = range(10)

        def mod(i):
            return mods[:, i, :]

        kb.dma(SP, misc_ld, par[:], par_d[:, :], writes=[bf("par")])
        kb.dma(SP, misc_ld, cmat[:], cm_d[:, :], writes=[bf("cmat")])
        kb.dma(SP, misc_ld, cct[:], cc_d[:, :, :], writes=[bf("cct")])
        kb.dma(SP, misc_ld, fnbc[:], fn_d.partition_broadcast(128), writes=[bf("fnbc")])
        GSPEC = ((0, 2, 0.5, "G0"), (1, 2, 0.5, "G1c"), (2, 5, 1.0, "G2"), (3, 8, 0.5, "G3"))
        for gi, v, sc, nm in GSPEC:
            kb.dma(SP, misc_ld, Gt[gi][:], bada_d[0:1, v * D:(v + 1) * D].partition_broadcast(128), writes=[bf(nm)])
        for gi, v, sc, nm in GSPEC:
            bf(nm).w = (misc_ld.sem, misc_ld.count)
        for nm in ("par", "cmat", "cct", "fnbc"):
            bf(nm).w = (misc_ld.sem, misc_ld.count)

        kb.op(DVE, lambda: V.tensor_copy(out=identb[:], in_=ident_f), reads=[bf("cmat")], writes=[bf("identb")])
        kb.op(DVE, lambda: V.memset(Sada[:], 0.0), writes=[bf("Sada")])
        kb.op(DVE, lambda: V.memset(VA[:, :, 64:128], 1.0), writes=[bf("VAones")])
        kb.op(DVE, lambda: V.memset(stat[:], 0.0), writes=[bf("stat")])
        kb.op(DVE, lambda: V.memset(epsT[:], EPS), writes=[bf("epsT")])
        kb.op(ACT, lambda: A_.activation(out=Sada[:, :, 0], in_=cct[:, 0, :], func=AF.Silu),
              reads=[bf("cct")], writes=[bf("Sada")])
        kb.op(ACT, lambda: A_.activation(out=Sada[:, :, 32], in_=cct[:, 1, :], func=AF.Silu),
              reads=[bf("cct")], writes=[bf("Sada")])

        kb.dma(POOL, wkv_ld, Wkv[:, :, :],
               wmi_d[:, 512:768].rearrange("(kc p) c -> p kc c", p=128), writes=[bf("Wkv")])

        def cast_unit(u, sl, slb, dsem):
            w = [slb]

            def cd(out, in_):
                kb.dma(POOL, dsem, out, in_, writes=w)
            if u < U_Q:
                wi, wo = (w1i_d, w1o_d) if u < U_F2 else (w2i_d, w2o_d)
                r = u - (U_F1 if u < U_F2 else U_F2)
                if r < NF:
                    f = r
                    slv = sl[:, 0:2048].rearrange("p (k c) -> p k c", c=256)
                    cd(slv[:, :, 0:128], wi[:, f * 128:(f + 1) * 128].rearrange("(kc p) c -> p kc c", p=128))
                    cd(slv[:, :, 128:256],
                       wi[:, DFF + f * 128:DFF + (f + 1) * 128].rearrange("(kc p) c -> p kc c", p=128))
                else:
                    o = r - NF
                    for fi in range(3):
                        f = o * 3 + fi
                        if f < NF:
                            cd(sl[:, fi * 1024:(fi + 1) * 1024], wo[f * 128:(f + 1) * 128, :])
            elif u < U_CV:
                i = u - U_Q
                slv = sl[:, 0:2048].rearrange("p (k c) -> p k c", c=256)
                for jj in range(2):
                    j = i * 2 + jj
                    cd(slv[:, :, jj * 128:jj * 128 + 64],
                       wmi_d[:, j * 64:(j + 1) * 64].rearrange("(kc p) c -> p kc c", p=128))
                    cd(slv[:, :, jj * 128 + 64:jj * 128 + 128],
                       wmi_d[:, (j + 4) * 64:(j + 5) * 64].rearrange("(kc p) c -> p kc c", p=128))
            elif u < U_MO:
                c = u - U_CV
                slv = sl[:, 0:3072].rearrange("p (k c) -> p k c", c=384)
                for gi_, off in enumerate((768, 1280, 1792)):
                    cd(slv[:, :, gi_ * 128:(gi_ + 1) * 128],
                       wmi_d[:, off + c * 128:off + (c + 1) * 128].rearrange("(kc p) c -> p kc c", p=128))
            else:
                o = u - U_MO
                for ki in range(3):
                    k = o * 3 + ki
                    if k >= 8:
                        continue
                    dst = sl[:, ki * 1024:(ki + 1) * 1024]
                    if k < 4:
                        cd(dst[0:64, :], wmo_d[k * 64:(k + 1) * 64, :])
                        cd(dst[64:128, :], wmo_d[(k + 4) * 64:(k + 5) * 64, :])
                    else:
                        cd(dst, wmo_d[512 + (k - 4) * 128:512 + (k - 3) * 128, :])

        def usize(u):
            if u < U_Q:
                r = u - (U_F1 if u < U_F2 else U_F2)
                if r < NF:
                    return 2048
                return 1024 * min(3, NF - (r - NF) * 3)
            if u < U_CV:
                return 2048
            if u < U_MO:
                return 3072
            return 1024 * min(3, 8 - (u - U_MO) * 3)

        def store_unit(u, sl, slb, dsem):
            n = usize(u)
            kb.dma(POOL, dsem, wscr_d[u, :, 0:n], sl[:, 0:n], reads=[slb], writes=[scrb[u]])

        stg = [mixT[:, :, :].rearrange("p a b -> p (a b)"), stg1[:, :]]
        stgb = [bf("mixT"), Buf("stg1")]
        bg_state = [U_F2, 0]

        def bg_cast(n_units):
            for _ in range(n_units):
                u = bg_state[0]
                if u >= NU:
                    return
                k = bg_state[1] % 2
                cast_unit(u, stg[k], stgb[k], stg_dc[k])
                store_unit(u, stg[k], stgb[k], stg_ss[k])
                bg_state[0] += 1
                bg_state[1] += 1

        NPRE = U_F2
        pre_u = [0]
        st_u = [0]

        def precast_issue(n):
            for _ in range(n):
                if pre_u[0] < NPRE:
                    u = pre_u[0]
                    cast_unit(u, ring[u % NS], ringb[u % NS], ring_dc[u % NS])
                    pre_u[0] += 1

        def precast_store(n):
            for _ in range(n):
                if st_u[0] < pre_u[0]:
                    u = st_u[0]
                    n_ = usize(u)
                    kb.dma(SP, ring_ss[u % NS], wscr_d[u, :, 0:n_], ring[u % NS][:, 0:n_],
                           reads=[ringb[u % NS]], writes=[scrb[u]])
                    st_u[0] += 1

        precast_issue(NS)

        for gi, v, sc, nm in GSPEC:
            if sc != 1.0:
                kb.op(DVE, lambda gi=gi, sc=sc: V.tensor_scalar(out=Gt[gi][:], in0=Gt[gi][:], scalar1=sc, scalar2=None,
                                                                op0=ALU.mult), reads=[bf(nm)], writes=[bf(nm)])
        PMT = PS[1][:, 0:144]
        ada_first = True
        for q in range(18):
            xb = XB[q % 2]
            xbv = xb[:, :, :].rearrange("p a (b c) -> p (a b) c", c=512)
            kb.dma(SP, ada_ld[q % 2], xbv,
                   wada_d[:, q * 512:(q + 1) * 512].rearrange("(kc p) c -> p kc c", p=128),
                   writes=XBb[q % 2])
            precast_store(2)
            precast_issue(2)
            pm = PS[0][0:33, 0:512]

            def mm_ada(xbv=xbv, pm=pm):
                last = None
                for kc in range(8):
                    last = T_.matmul(pm, lhsT=Sada[:, kc, :], rhs=xbv[:, kc, :], start=(kc == 0), stop=(kc == 7))
                return last
            kb.op(PE, mm_ada, reads=XBb[q % 2] + [bf("Sada")], writes=[PSb[0][0]])
            mrow = TT[q % 2]
            kb.op(ACT, lambda mrow=mrow, pm=pm: A_.activation(out=mrow[0:33, :], in_=pm, func=AF.Copy),
                  reads=[PSb[0][0]], writes=[TTb[q % 2]])

            def mm_tr(mrow=mrow, q=q):
                last = None
                for jj in range(4):
                    j = q * 4 + jj
                    for w in range(2):
                        last = T_.matmul(PS[1][:, j * 2 + w:j * 2 + w + 1],
                                         lhsT=mrow[32 * w:32 * w + 1, jj * 128:(jj + 1) * 128],
                                         rhs=ones_m[32 * w:32 * w + 1, 0:1], start=True, stop=True)
                return last
            kb.op(PE, mm_tr, reads=[TTb[q % 2], bf("cmat")], writes=[PSb[1][0]])
            v = q // 2
            if v in (2, 5, 8):
                gi = {2: 0, 5: 2, 8: 3}[v]
                sc = 1.0 if v == 5 else 0.5
                half = q % 2
                pb = psh(2, 0)
                kb.op(PE, lambda mrow=mrow, pb=pb: T_.matmul(pb, lhsT=ones_m[0:1, :], rhs=mrow[0:1, :],
                                                          start=True, stop=True),
                      reads=[TTb[q % 2], bf("cmat")], writes=[PSb[2][0]])
                kb.op(DVE, lambda pb=pb, gi=gi, half=half, sc=sc: V.scalar_tensor_tensor(
                    out=Gt[gi][:, half * 512:(half + 1) * 512], in0=pb, scalar=sc,
                    in1=Gt[gi][:, half * 512:(half + 1) * 512], op0=ALU.mult, op1=ALU.add),
                    reads=[PSb[2][0], bf("G%d" % gi)], writes=[bf("G%d" % gi)])
                if v == 2:
                    pb2 = psh(2, 1)
                    kb.op(PE, lambda mrow=mrow, pb2=pb2: T_.matmul(pb2, lhsT=ones_m[32:33, :], rhs=mrow[32:33, :],
                                                                start=True, stop=True),
                          reads=[TTb[q % 2], bf("cmat")], writes=[PSb[2][1]])
                    kb.op(DVE, lambda pb2=pb2, half=half: V.scalar_tensor_tensor(
                        out=Gt[1][:, half * 512:(half + 1) * 512], in0=pb2, scalar=0.5,
                        in1=Gt[1][:, half * 512:(half + 1) * 512], op0=ALU.mult, op1=ALU.add),
                        reads=[PSb[2][1], bf("G1c")], writes=[bf("G1c")])
        while st_u[0] < NPRE:
            precast_store(1)
            precast_issue(1)
        kb.op(DVE, lambda: V.tensor_tensor(out=mT[:, :, 0], in0=PS[1][:, 0:144:2], in1=bT, op=ALU.add),
              reads=[PSb[1][0], bf("par")], writes=[bf("mT")])
        kb.op(DVE, lambda: V.tensor_tensor(out=mT[:, :, 1], in0=PS[1][:, 1:144:2], in1=bT, op=ALU.add),
              reads=[PSb[1][0], bf("par")], writes=[bf("mT")])

        def mk_mod(ai, bi, vs, vsh, g, w):
            kb.op(DVE, lambda: V.scalar_tensor_tensor(out=mod(ai), in0=mT[:, vs * 8:(vs + 1) * 8, w], scalar=1.0,
                                                      in1=g, op0=ALU.add, op1=ALU.mult),
                  reads=[bf("mT"), bf("par")], writes=[bf("mods")])
            kb.op(DVE, lambda: V.tensor_copy(out=mod(bi), in_=mT[:, vsh * 8:(vsh + 1) * 8, w]),
                  reads=[bf("mT")], writes=[bf("mods")])
        mk_mod(M_A1, M_B1, 1, 0, g1, 0)
        mk_mod(M_A1c, M_B1c, 1, 0, g1, 1)
        mk_mod(M_A2, M_B2, 4, 3, gm, 0)
        mk_mod(M_A2c, M_B2c, 4, 3, gm, 1)
        mk_mod(M_A3, M_B3, 7, 6, g2, 0)

        if debug:
            kb.dma(POOL, dbg_st, dbg_mt[:, :], mT[:, :, :].rearrange("p a b -> p (a b)"), reads=[bf("mT")])
            for i in range(4):
                kb.dma(POOL, dbg_st, dbg_g[:, i * D:(i + 1) * D], Gt[i][:], reads=[bf("G0"), bf("G1c"), bf("G2"), bf("G3")])

        class WStream:
            def __init__(self):
                self.sched = []
                self.nl = 0
                self.nu = 0
                self.rel = 0

            def done(self, n=1):
                self.rel += n

            def prefetch(self, upto):
                while self.nl < min(upto, len(self.sched)):
                    i = self.nl
                    u = self.sched[i]
                    s = i % NS
                    n = usize(u)
                    kb.dma(SP, ring_ds[s], ring[s][:, 0:n], wscr_d[u, :, 0:n], reads=[scrb[u]], writes=[ringb[s]])
                    self.nl += 1

            def acquire(self, u):
                i = self.nu
                assert self.sched[i] == u, (i, u, self.sched[i])
                assert i < self.rel + NS
                self.prefetch(min(self.rel + NS, PFLIM[0]))
                self.nu += 1
                return ring[i % NS], ringb[i % NS]

        ws = WStream()
        ffn1_units = list(range(U_F1, U_F1 + 30))
        ffn2_units = list(range(U_F2, U_F2 + 30))
        for b in range(NB + 1):
            ws.sched += ffn1_units
        for b in range(NB):
            ws.sched += [U_Q, U_Q + 1] + list(range(U_CV, U_CV + 4)) + list(range(U_MO, U_MO + 3)) + ffn2_units

        def rms_to_T(xi, NT, Ai, Bi, dstT, col0, dstb):
            TBk = NT * 128
            for t in range(NT):
                kb.op(ACT, lambda t=t: A_.activation(out=junk[:], in_=XB[xi][:, t, :], func=AF.Square,
                                                     scale=1.0 / 32.0, accum_out=stat[:, t:t + 1]),
                      reads=[XBb[xi][t]], writes=[bf("ss%d" % t)])
                kb.op(ACT, lambda t=t: A_.activation(out=stat[:, t:t + 1], in_=stat[:, t:t + 1], func=AF.Ln,
                                                     bias=epsT[:, 0:1]),
                      reads=[bf("ss%d" % t), bf("epsT")], writes=[bf("ss%d" % t)])
                kb.op(ACT, lambda t=t: A_.activation(out=stat[:, 4 + t:5 + t], in_=stat[:, t:t + 1], func=AF.Exp,
                                                     scale=-0.5),
                      reads=[bf("ss%d" % t)], writes=[bf("rstd%d" % t)])
                if t % 2 == 0:
                    kb.op(ACT, lambda t=t: A_.activation(out=xn[:, t, :], in_=XB[xi][:, t, :], func=AF.Copy,
                                                         scale=stat[:, 4 + t:5 + t]),
                          reads=[XBb[xi][t], bf("rstd%d" % t)], writes=[bf("xn%d" % t)])
                else:
                    kb.op(DVE, lambda t=t: V.tensor_scalar(out=xn[:, t, :], in0=XB[xi][:, t, :],
                                                           scalar1=stat[:, 4 + t:5 + t], scalar2=None, op0=ALU.mult),
                          reads=[XBb[xi][t], bf("rstd%d" % t)], writes=[bf("xn%d" % t)])
            for kc in range(8):
                pi, ph = 2 + kc // 4, (kc % 4) // 2
                ptv = PS[pi].bitcast(BF16)
                base = (kc % 4) * 512

                def tr(kc=kc, ptv=ptv, base=base):
                    last = None
                    for t in range(NT):
                        last = T_.transpose(out=ptv[:, base + t * 128:base + (t + 1) * 128],
                                            in_=xn[:, t, kc * 128:(kc + 1) * 128], identity=identb[:])
                    return last
                kb.op(PE, tr, reads=[bf("xn%d" % t) for t in range(NT)] + [bf("identb")], writes=[PSb[pi][ph]])
                src = ptv[:, base:base + TBk]
                dst = dstT[:, kc, col0:col0 + TBk]
                if kc % 2 == 0:
                    kb.op(ACT, lambda src=src, dst=dst, kc=kc: A_.activation(
                        out=dst, in_=src, func=AF.Identity, scale=mod(Ai)[:, kc:kc + 1], bias=mod(Bi)[:, kc:kc + 1]),
                        reads=[PSb[pi][ph], bf("mods")], writes=[dstb[kc]])
                else:
                    kb.op(DVE, lambda src=src, dst=dst, kc=kc: V.tensor_scalar(
                        out=dst, in0=src, scalar1=mod(Ai)[:, kc:kc + 1], scalar2=mod(Bi)[:, kc:kc + 1],
                        op0=ALU.mult, op1=ALU.add),
                        reads=[PSb[pi][ph], bf("mods")], writes=[dstb[kc]])

        hTb = [bf("hT%d" % kc) for kc in range(8)]
        h2Tb = [bf("h2T%d" % kc) for kc in range(8)]

        def ffn(xi, NT, Ai, Bi, gi, ubase, do_norm=True, on_pool=True):
            TBk = NT * 128
            if do_norm:
                rms_to_T(xi, NT, Ai, Bi, hT, 0, hTb)
            for f in range(NF):
                sl, slb = ws.acquire(ubase + f)
                pg = PS[f % 2][:, 0:TBk]
                pu = PS[f % 2][:, 512:512 + TBk]

                def mm1(sl=sl, pg=pg, pu=pu):
                    last = None
                    for kc in range(8):
                        T_.matmul(pg, lhsT=sl[:, kc * 256:kc * 256 + 128], rhs=hT[:, kc, 0:TBk],
                                  start=(kc == 0), stop=(kc == 7))
                    for kc in range(8):
                        last = T_.matmul(pu, lhsT=sl[:, kc * 256 + 128:kc * 256 + 256], rhs=hT[:, kc, 0:TBk],
                                         start=(kc == 0), stop=(kc == 7))
                    return last
                kb.op(PE, mm1, reads=[slb] + hTb, writes=[PSb[f % 2][0], PSb[f % 2][1]])
                sg = TT[f % 2]
                kb.op(ACT, lambda sg=sg, pg=pg: A_.activation(out=sg[:, 0:TBk], in_=pg, func=AF.Silu),
                      reads=[PSb[f % 2][0]], writes=[TTb[f % 2]])
                kb.op(DVE, lambda sg=sg, pu=pu, f=f: V.tensor_tensor(out=actT[:, f, 0:TBk], in0=sg[:, 0:TBk], in1=pu,
                                                                     op=ALU.mult),
                      reads=[TTb[f % 2], PSb[f % 2][1]], writes=[bf("act%d" % f)])
                ws.done()
            accb = [PSb[t][h] for t in range(NT) for h in range(2)]
            for o in range(8):
                sl, slb = ws.acquire(ubase + NF + o)
                fs = [o * 3 + fi for fi in range(3) if o * 3 + fi < NF]

                def mm2(sl=sl, fs=fs, o=o):
                    last = None
                    for fi, f in enumerate(fs):
                        for t in range(NT):
                            for h in range(2):
                                last = T_.matmul(psh(t, h), lhsT=actT[:, f, t * 128:(t + 1) * 128],
                                                 rhs=sl[:, fi * 1024 + h * 512:fi * 1024 + (h + 1) * 512],
                                                 start=(f == 0), stop=(f == NF - 1))
                    return last
                kb.op(PE, mm2, reads=[slb] + [bf("act%d" % f) for f in fs], writes=accb)
                ws.done()
            k = 0
            for t in range(NT):
                for h in range(2):
                    tmp = TT[2 + k % 4]
                    tb = TTb[2 + k % 4]
                    kb.op(DVE, lambda tmp=tmp, t=t, h=h: V.tensor_tensor(
                        out=tmp[:, :], in0=psh(t, h), in1=Gt[gi][:, h * 512:(h + 1) * 512], op=ALU.mult),
                        reads=[PSb[t][h], bf("G%s" % ("1c" if gi == 1 else str(gi)))], writes=[tb])
                    if on_pool:
                        kb.op(POOL, lambda tmp=tmp, t=t, h=h: G_.tensor_tensor(
                            out=XB[xi][:, t, h * 512:(h + 1) * 512], in0=XB[xi][:, t, h * 512:(h + 1) * 512],
                            in1=tmp[:, :], op=ALU.add),
                            reads=[tb, XBb[xi][t]], writes=[XBb[xi][t]])
                    else:
                        kb.op(DVE, lambda tmp=tmp, t=t, h=h: V.tensor_tensor(
                            out=XB[xi][:, t, h * 512:(h + 1) * 512], in0=XB[xi][:, t, h * 512:(h + 1) * 512],
                            in1=tmp[:, :], op=ALU.add),
                            reads=[tb, XBb[xi][t]], writes=[XBb[xi][t]])
                    k += 1

        HN_SETS = [
            dict(raw=2, sq=3, kn=4, t1=5, ps=3),
            dict(raw=6, sq=7, kn=0, t1=1, ps=1),
        ]

        def headnorm_gen(psrc, psb, n, gsc, dst, dstb, with_rope, on_pool=True, tset=0):
            ts = HN_SETS[tset]
            raw, rawb = TT[ts["raw"]], TTb[ts["raw"]]
            sq, sqb = TT[ts["sq"]], TTb[ts["sq"]]
            kn, knb = TT[ts["kn"]], TTb[ts["kn"]]
            t1, t1b = TT[ts["t1"]], TTb[ts["t1"]]
            pi = ts["ps"]
            pss, pssb = PS[pi][:, 0:n], PSb[pi][0]
            prot, protb = PS[pi][:, 512:512 + n], PSb[pi][1]
            kb.op(DVE, lambda: V.tensor_copy(out=raw[:, 0:n], in_=psrc), reads=[psb], writes=[rawb])
            yield
            kb.op(ACT, lambda: A_.activation(out=sq[:, 0:n], in_=psrc, func=AF.Square), reads=[psb], writes=[sqb])
            yield
            kb.op(PE, lambda: T_.matmul(pss, lhsT=bd64, rhs=sq[:, 0:n], start=True, stop=True),
                  reads=[sqb, bf("cmat")], writes=[pssb])
            yield
            kb.op(ACT, lambda: A_.activation(out=sq[:, 0:n], in_=pss, func=AF.Ln, bias=epsT[:, 0:1]),
                  reads=[pssb, bf("epsT")], writes=[sqb])
            yield
            kb.op(ACT, lambda: A_.activation(out=sq[:, 0:n], in_=sq[:, 0:n], func=AF.Exp, scale=-0.5),
                  reads=[sqb], writes=[sqb])
            yield
            if not with_rope:
                kb.op(DVE, lambda: V.scalar_tensor_tensor(out=dst, in0=raw[:, 0:n], scalar=gsc, in1=sq[:, 0:n],
                                                          op0=ALU.mult, op1=ALU.mult),
                      reads=[rawb, sqb, bf("par")], writes=[dstb])
                yield
                return
            kb.op(DVE, lambda: V.scalar_tensor_tensor(out=kn[:, 0:n], in0=raw[:, 0:n], scalar=gsc, in1=sq[:, 0:n],
                                                      op0=ALU.mult, op1=ALU.mult),
                  reads=[rawb, sqb, bf("par")], writes=[knb])
            yield
            kb.op(PE, lambda: T_.matmul(prot, lhsT=Rm, rhs=kn[:, 0:n], start=True, stop=True),
                  reads=[knb, bf("cmat")], writes=[protb])
            yield
            if on_pool:
                kb.op(POOL, lambda: G_.tensor_tensor(out=t1[:, 0:n], in0=kn[:, 0:n], in1=ropeT[:, 0, 0:n], op=ALU.mult),
                      reads=[knb, bf("rope")], writes=[t1b])
            else:
                kb.op(DVE, lambda: V.tensor_tensor(out=t1[:, 0:n], in0=kn[:, 0:n], in1=ropeT[:, 0, 0:n], op=ALU.mult),
                      reads=[knb, bf("rope")], writes=[t1b])
            yield
            kb.op(DVE, lambda: V.tensor_tensor(out=raw[:, 0:n], in0=prot, in1=ropeT[:, 1, 0:n], op=ALU.mult),
                  reads=[protb, bf("rope")], writes=[rawb])
            yield
            kb.op(DVE, lambda: V.tensor_tensor(out=dst, in0=t1[:, 0:n], in1=raw[:, 0:n], op=ALU.add),
                  reads=[t1b, rawb], writes=[dstb])
            yield

        def headnorm_rope(psrc, psb, n, gsc, dst, dstb, with_rope, on_pool=True):
            for _ in headnorm_gen(psrc, psb, n, gsc, dst, dstb, with_rope, on_pool=on_pool, tset=0):
                pass

        def kv_stage(NT, kc0, with_rope):
            TBk = NT * 128
            pk = PS[0][:, 0:TBk]

            def mmk():
                last = None
                for kc in range(8):
                    last = T_.matmul(pk, lhsT=Wkv[:, kc, 0:128], rhs=h2T[:, kc, 2:2 + TBk],
                                     start=(kc == 0), stop=(kc == 7))
                return last
            kb.op(PE, mmk, reads=[bf("Wkv")] + h2Tb, writes=[PSb[0][0]])
            pv = PS[0][:, 512:512 + TBk]

            def mmv():
                last = None
                for t in range(NT):
                    for kc in range(8):
                        last = T_.matmul(pv[:, t * 128:(t + 1) * 128], lhsT=h2T[:, kc, 2 + t * 128:2 + (t + 1) * 128],
                                         rhs=Wkv[:, kc, 128:256], start=(kc == 0), stop=(kc == 7))
                return last
            kb.op(PE, mmv, reads=[bf("Wkv")] + h2Tb, writes=[PSb[0][1]])
            pv3 = pv.rearrange("p (t c) -> p t c", c=128)
            kb.op(ACT, lambda: A_.activation(out=VA[:, kc0:kc0 + NT, 0:64], in_=pv3[:, :, 0:64], func=AF.Copy),
                  reads=[PSb[0][1]], writes=[bf("VA")])
            kb.op(DVE, lambda: V.tensor_copy(out=VA[:, kc0:kc0 + NT, 128:192], in_=pv3[:, :, 64:128]),
                  reads=[PSb[0][1]], writes=[bf("VA")])
            headnorm_rope(pk, PSb[0][0], TBk, kg, KT[:, kc0 * 128:kc0 * 128 + TBk], bf("KT"), with_rope, on_pool=False)

        def load_x(src_ap, xi, NT, b_reads=()):
            kb.dma(SP, XB_ld[xi], XB[xi][:, 0:NT, :], src_ap.rearrange("(t p) d -> p t d", p=128),
                   reads=list(b_reads), writes=XBb[xi][0:NT])

        if limit >= 2:
            load_x(ctx_d[:, :], 0, 2)
            load_x(x_d[0:TB, :], 1, 4)
            rms_to_T(0, 2, M_A1c, M_B1c, hT, 0, hTb)
            ffn(0, 2, M_A1c, M_B1c, 1, U_F1, do_norm=False, on_pool=False)
            if limit >= 3:
                rms_to_T(1, 4, M_A1, M_B1, hT, 0, hTb)
            rms_to_T(0, 2, M_A2c, M_B2c, h2T, 2, h2Tb)
            kv_stage(2, 0, False)
        for b in range(NB if limit >= 3 else 0):
            xi = (b + 1) % 2
            if b + 1 < NB:
                load_x(x_d[(b + 1) * TB:(b + 2) * TB, :], 1 - xi, 4)
            kb.dma(SP, rope_ld, ropeT[:, :, :], rope_d[:, :, b * TB:(b + 1) * TB], writes=[bf("rope")])
            if b < 4:
                bg_cast(10)
            ffn(xi, 4, M_A1, M_B1, 0, U_F1, do_norm=False, on_pool=False)
            kb.dma(POOL, XB_st[xi], x1s_d[b * TB:(b + 1) * TB, :].rearrange("(t p) d -> p t d", p=128),
                   XB[xi][:, :, :], reads=XBb[xi], writes=[x1sb[b]])
            if b + 1 < NB:
                rms_to_T(1 - xi, 4, M_A1, M_B1, hT, 0, hTb)
            rms_to_T(xi, 4, M_A2, M_B2, h2T, 2, h2Tb)
            kb.dma(POOL, h2_st, h2s_d[:, :, b * TB:(b + 1) * TB], h2T[:, :, 2:2 + TB], reads=h2Tb, writes=[h2sb[b]])
            kv_stage(4, 2 + b * 4, True)

        for b in range(NB):
            xi = (b + 1) % 2
            x1sb[b].w = (XB_st[xi].sem, XB_st[xi].count)
            h2sb[b].w = (h2_st.sem, h2_st.count)
        if debug:
            for c in range(NKC * 128 // 256):
                tt = TT[c % 2]
                kb.op(DVE, lambda c=c, tt=tt: V.tensor_copy(out=tt[:, 0:256], in_=KT[:, c * 256:(c + 1) * 256]),
                      reads=[bf("KT")], writes=[TTb[c % 2]])
                kb.dma(POOL, dbg_s2[c % 2], dbg_kt[:, c * 256:(c + 1) * 256], tt[:, 0:256], reads=[TTb[c % 2]])
            vaf = VA[:, :, :].rearrange("p a b -> p (a b)")
            for c in range(NKC * 192 // 384):
                tt = TT[c % 2]
                kb.op(DVE, lambda c=c, tt=tt: V.tensor_copy(out=tt[:, 0:384], in_=vaf[:, c * 384:(c + 1) * 384]),
                      reads=[bf("VA"), bf("VAones")], writes=[TTb[c % 2]])
                kb.dma(POOL, dbg_s2[c % 2], dbg_va[:, c * 384:(c + 1) * 384], tt[:, 0:384], reads=[TTb[c % 2]])

        def q_stage(b, extra=None):
            for i in range(2):
                sl, slb = ws.acquire(U_Q + i)
                gens = []
                for jj in range(2):
                    j = i * 2 + jj
                    pq = PS[0][:, jj * 512:jj * 512 + TB]

                    def mmq(sl=sl, jj=jj, pq=pq):
                        last = None
                        for kc in range(8):
                            last = T_.matmul(pq, lhsT=sl[:, kc * 256 + jj * 128:kc * 256 + (jj + 1) * 128],
                                             rhs=h2T[:, kc, 2:2 + TB], start=(kc == 0), stop=(kc == 7))
                        return last
                    kb.op(PE, mmq, reads=[slb] + h2Tb, writes=[PSb[0][jj]])
                    gens.append(headnorm_gen(pq, PSb[0][jj], TB, qg, QT[:, j, :], bf("QT%d" % j), True, tset=jj))
                if extra is not None and i == 0:
                    gens.append(extra)
                live = list(gens)
                while live:
                    for g in list(live):
                        try:
                            next(g)
                        except StopIteration:
                            live.remove(g)
                ws.done()

        def attn_stage(b):
            pending = {}
            for t in range(4):
                po = [psh(2, 0), psh(2, 1)]
                pob = [PSb[2][0], PSb[2][1]]

                def qk(kc, t=t):
                    pi = kc % 2
                    def f():
                        T_.matmul(psh(pi, 0), lhsT=KT[0:64, kc * 128:(kc + 1) * 128],
                                  rhs=QT[0:64, :, t * 128:(t + 1) * 128], start=True, stop=True)
                        return T_.matmul(psh(pi, 1), lhsT=KT[64:128, kc * 128:(kc + 1) * 128],
                                         rhs=QT[64:128, :, t * 128:(t + 1) * 128], start=True, stop=True)
                    kb.op(PE, f, reads=[bf("KT")] + [bf("QT%d" % j_) for j_ in range(4)], writes=[PSb[pi][0], PSb[pi][1]])

                def ex(kc):
                    pi = kc % 2
                    pe_ = kc % 3
                    kb.op(ACT, lambda: A_.activation(out=PTe[pe_][:, :], in_=PS[pi][:, :], func=AF.Exp, scale=0.125),
                          reads=[PSb[pi][0], PSb[pi][1]], writes=[bf("PTe%d" % pe_)])

                def pv(kc):
                    pe_ = kc % 3
                    def f():
                        T_.matmul(po[0], lhsT=VA[:, kc, 0:128], rhs=PTe[pe_][:, 0:512],
                                  start=(kc == 0), stop=(kc == NKC - 1))
                        return T_.matmul(po[1], lhsT=VA[:, kc, 64:192], rhs=PTe[pe_][:, 512:1024],
                                         start=(kc == 0), stop=(kc == NKC - 1))
                    kb.op(PE, f, reads=[bf("VA"), bf("VAones"), bf("PTe%d" % pe_)], writes=pob)
                def epi_parts(t=t):
                    o0, o1, rc, at, sq, rs = TT[0], TT[1], TT[2], TT[3], TT[4], TT[5]
                    pss = PS[3][:, 0:128]

                    def p1():
                        kb.op(DVE, lambda: V.reciprocal(out=rc[0:64, :], in_=o0[64:128, :]), reads=[TTb[0]], writes=[TTb[2]])
                        kb.op(DVE, lambda: V.reciprocal(out=rc[64:128, :], in_=o1[0:64, :]), reads=[TTb[1]], writes=[TTb[2]])
                        kb.op(DVE, lambda: V.tensor_tensor(out=at[0:64, :], in0=o0[0:64, :], in1=rc[0:64, :], op=ALU.mult),
                              reads=[TTb[0], TTb[2]], writes=[TTb[3]])
                        kb.op(DVE, lambda: V.tensor_tensor(out=at[64:128, :], in0=o1[64:128, :], in1=rc[64:128, :],
                                                           op=ALU.mult), reads=[TTb[1], TTb[2]], writes=[TTb[3]])

                    def p2():
                        kb.op(DVE, lambda: V.tensor_tensor(out=sq[:, :], in0=at[:, :], in1=at[:, :], op=ALU.mult),
                              reads=[TTb[3]], writes=[TTb[4]])

                    def p3():
                        def mss():
                            last = None
                            for j in range(4):
                                last = T_.matmul(pss, lhsT=o512, rhs=sq[:, j * 128:(j + 1) * 128],
                                                 start=(j == 0), stop=(j == 3))
                            return last
                        kb.op(PE, mss, reads=[TTb[4], bf("cmat")], writes=[PSb[3][0]])

                    def p4():
                        kb.op(ACT, lambda: A_.activation(out=rs[:, 0:128], in_=pss, func=AF.Ln, bias=epsT[:, 0:1]),
                              reads=[PSb[3][0], bf("epsT")], writes=[TTb[5]])
                        kb.op(ACT, lambda: A_.activation(out=rs[:, 0:128], in_=rs[:, 0:128], func=AF.Exp, scale=-0.5),
                              reads=[TTb[5]], writes=[TTb[5]])

                    def p5():
                        for j in range(4):
                            kb.op(DVE, lambda j=j: V.scalar_tensor_tensor(
                                out=mixT[:, j, t * 128:(t + 1) * 128], in0=at[:, j * 128:(j + 1) * 128],
                                scalar=ga[:, j:j + 1], in1=rs[:, 0:128], op0=ALU.mult, op1=ALU.mult),
                                reads=[TTb[3], TTb[5], bf("par")], writes=[bf("mixT")])
                    return {2: p1, 4: p2, 14: p3, 20: p4, 26: p5}

                qk(0)
                qk(1)
                for kc in range(NKC):
                    ex(kc)
                    if kc + 2 < NKC:
                        qk(kc + 2)
                    pv(kc)
                    if kc in pending:
                        pending[kc]()
                pending.clear()
                kb.op(DVE, lambda: V.tensor_copy(out=TT[0][:, :], in_=po[0]), reads=[pob[0]], writes=[TTb[0]])
                kb.op(DVE, lambda: V.tensor_copy(out=TT[1][:, :], in_=po[1]), reads=[pob[1]], writes=[TTb[1]])
                pending.update(epi_parts())
                if t == 3:
                    for k_ in sorted(pending):
                        pending[k_]()
                    pending.clear()

        def conv_stage(b):
            SC = [TT[0], TT[1], TT[2], TT[3]]
            it = 0
            for c in range(4):
                sl, slb = ws.acquire(U_CV + c)
                for s in range(2):
                    par_ = it % 2
                    it += 1
                    if par_ == 0:
                        pgb, pgbb = PS[0][:, 0:256], PSb[0][0]
                        pgc, pgcb = PS[0][:, 512:512 + 258], PSb[0][1]
                        puu, puub = PS[1][:, 0:258], PSb[1][0]
                        zu, zub, y, yb = TT[4], TTb[4], TT[5], TTb[5]
                    else:
                        pgb, pgbb = PS[2][:, 0:256], PSb[2][0]
                        pgc, pgcb = PS[2][:, 512:512 + 258], PSb[2][1]
                        puu, puub = PS[1][:, 512:512 + 258], PSb[1][1]
                        zu, zub, y, yb = TT[6], TTb[6], TT[7], TTb[7]

                    def mmc(sl=sl, s=s, pgb=pgb, pgc=pgc, puu=puu):
                        last = None
                        for kc in range(8):
                            T_.matmul(pgb, lhsT=sl[:, kc * 384:kc * 384 + 128],
                                      rhs=h2T[:, kc, 2 + s * 256:2 + s * 256 + 256], start=(kc == 0), stop=(kc == 7))
                        for kc in range(8):
                            T_.matmul(pgc, lhsT=sl[:, kc * 384 + 128:kc * 384 + 256],
                                      rhs=h2T[:, kc, 1 + s * 256:1 + s * 256 + 258], start=(kc == 0), stop=(kc == 7))
                        for kc in range(8):
                            last = T_.matmul(puu, lhsT=sl[:, kc * 384 + 256:kc * 384 + 384],
                                             rhs=h2T[:, kc, 1 + s * 256:1 + s * 256 + 258], start=(kc == 0), stop=(kc == 7))
                        return last
                    kb.op(PE, mmc, reads=[slb] + h2Tb, writes=[pgbb, pgcb, puub])
                    kb.op(ACT, lambda zu=zu, puu=puu: A_.activation(out=zu[:, 0:258], in_=puu, func=AF.Copy),
                          reads=[puub], writes=[zub])
                    kb.op(DVE, lambda zu=zu, pgc=pgc: V.tensor_tensor(out=zu[:, 0:258], in0=pgc, in1=zu[:, 0:258],
                                                                      op=ALU.mult),
                          reads=[pgcb, zub], writes=[zub])
                    kb.op(ACT, lambda c=c, zu=zu, y=y: A_.activation(out=y[:, 0:256], in_=zu[:, 1:257], func=AF.Copy,
                                                                     scale=cw[:, c * 3 + 1:c * 3 + 2]),
                          reads=[zub, bf("par")], writes=[yb])
                    kb.op(DVE, lambda c=c, zu=zu, y=y: V.scalar_tensor_tensor(
                        out=y[:, 0:256], in0=zu[:, 0:256], scalar=cw[:, c * 3:c * 3 + 1], in1=y[:, 0:256],
                        op0=ALU.mult, op1=ALU.add), reads=[zub, yb, bf("par")], writes=[yb])
                    kb.op(DVE, lambda c=c, zu=zu, y=y: V.scalar_tensor_tensor(
                        out=y[:, 0:256], in0=zu[:, 2:258], scalar=cw[:, c * 3 + 2:c * 3 + 3], in1=y[:, 0:256],
                        op0=ALU.mult, op1=ALU.add), reads=[zub, yb, bf("par")], writes=[yb])
                    kb.op(DVE, lambda c=c, s=s, y=y, pgb=pgb: V.tensor_tensor(
                        out=SC[c][:, s * 256:(s + 1) * 256], in0=pgb, in1=y[:, 0:256], op=ALU.mult),
                        reads=[pgbb, yb], writes=[TTb[c]])
                ws.done()
            pss = PS[3][:, 0:512]
            for c in range(4):
                sq, sqb = (TT[4], TTb[4]) if c % 2 == 0 else (TT[6], TTb[6])
                kb.op(ACT, lambda c=c, sq=sq: A_.activation(out=sq[:, :], in_=SC[c][:, :], func=AF.Square),
                      reads=[TTb[c]], writes=[sqb])
                kb.op(PE, lambda c=c, sq=sq: T_.matmul(pss, lhsT=o512, rhs=sq[:, :], start=(c == 0), stop=(c == 3)),
                      reads=[sqb, bf("cmat")], writes=[PSb[3][0]])
            rs = TT[5]
            kb.op(ACT, lambda: A_.activation(out=rs[:, :], in_=pss, func=AF.Ln, bias=epsT[:, 0:1]),
                  reads=[PSb[3][0], bf("epsT")], writes=[TTb[5]])
            kb.op(ACT, lambda: A_.activation(out=rs[:, :], in_=rs[:, :], func=AF.Exp, scale=-0.5),
                  reads=[TTb[5]], writes=[TTb[5]])
            for c in range(4):
                kb.op(DVE, lambda c=c: V.scalar_tensor_tensor(out=mixT[:, 4 + c, :], in0=SC[c][:, :],
                                                              scalar=gcv[:, c:c + 1], in1=rs[:, :],
                                                              op0=ALU.mult, op1=ALU.mult),
                      reads=[TTb[c], TTb[5], bf("par")], writes=[bf("mixT")])

        def mixout_stage(b, xi):
            sls = [ws.acquire(U_MO + o) for o in range(3)]
            k = 0
            for t in range(4):
                for h in range(2):
                    pi, ph = k % 4, 0
                    po = psh(pi, ph)

                    def mmo(t=t, h=h, po=po):
                        last = None
                        for kk in range(8):
                            sl = sls[kk // 3][0]
                            ki = kk % 3
                            last = T_.matmul(po, lhsT=mixT[:, kk, t * 128:(t + 1) * 128],
                                             rhs=sl[:, ki * 1024 + h * 512:ki * 1024 + (h + 1) * 512],
                                             start=(kk == 0), stop=(kk == 7))
                        return last
                    kb.op(PE, mmo, reads=[s_[1] for s_ in sls] + [bf("mixT")], writes=[PSb[pi][ph]])
                    tmp, tb = TT[k % 4], TTb[k % 4]
                    kb.op(DVE, lambda tmp=tmp, po=po, h=h: V.tensor_tensor(
                        out=tmp[:, :], in0=po, in1=Gt[2][:, h * 512:(h + 1) * 512], op=ALU.mult),
                        reads=[PSb[pi][ph], bf("G2")], writes=[tb])
                    kb.op(POOL, lambda tmp=tmp, t=t, h=h: G_.tensor_tensor(
                        out=XB[xi][:, t, h * 512:(h + 1) * 512], in0=XB[xi][:, t, h * 512:(h + 1) * 512],
                        in1=tmp[:, :], op=ALU.add),
                        reads=[tb, XBb[xi][t]], writes=[XBb[xi][t]])
                    k += 1
            ws.done(3)

        def final_gen(b, xi):
            for t in range(4):
                kb.op(ACT, lambda t=t: A_.activation(out=junk[:], in_=XB[xi][:, t, :], func=AF.Square,
                                                     scale=1.0 / 32.0, accum_out=stat[:, 8 + t:9 + t]),
                      reads=[XBb[xi][t]], writes=[bf("fss%d" % t)])
                yield
                kb.op(ACT, lambda t=t: A_.activation(out=stat[:, 8 + t:9 + t], in_=stat[:, 8 + t:9 + t], func=AF.Ln,
                                                     bias=epsT[:, 0:1]),
                      reads=[bf("fss%d" % t), bf("epsT")], writes=[bf("fss%d" % t)])
                yield
                kb.op(ACT, lambda t=t: A_.activation(out=stat[:, 12 + t:13 + t], in_=stat[:, 8 + t:9 + t], func=AF.Exp,
                                                     scale=-0.5),
                      reads=[bf("fss%d" % t)], writes=[bf("frs%d" % t)])
                yield
                kb.op(DVE, lambda t=t: V.scalar_tensor_tensor(out=XB[xi][:, t, :], in0=XB[xi][:, t, :],
                                                              scalar=stat[:, 12 + t:13 + t], in1=fnbc[:, :],
                                                              op0=ALU.mult, op1=ALU.mult),
                      reads=[XBb[xi][t], bf("frs%d" % t), bf("fnbc")], writes=[XBb[xi][t]])
                yield
            kb.dma(POOL, XB_st[xi], out_d[b * TB:(b + 1) * TB, :].rearrange("(t p) d -> p t d", p=128),
                   XB[xi][:, :, :], reads=XBb[xi])
            yield

        def load_blockB(b, xi):
            kb.dma(SP, XB_ld[xi], XB[xi][:, :, :], x1s_d[b * TB:(b + 1) * TB, :].rearrange("(t p) d -> p t d", p=128),
                   reads=[x1sb[b]], writes=XBb[xi])

        def load_h2T(b):
            lo = b * TB - 2
            hi = b * TB + TB + 2
            rd = [h2sb[bb] for bb in (b - 1, b, b + 1) if 0 <= bb < NB]
            if b == 0:
                kb.op(DVE, lambda: V.memset(h2T[:, :, 0:2], 0.0), writes=h2Tb)
                kb.dma(SP, h2_ld, h2T[:, :, 2:TB + 4], h2s_d[:, :, 0:hi], reads=rd, writes=h2Tb)
            elif b == NB - 1:
                kb.op(DVE, lambda: V.memset(h2T[:, :, TB + 2:TB + 4], 0.0), writes=h2Tb)
                kb.dma(SP, h2_ld, h2T[:, :, 0:TB + 2], h2s_d[:, :, lo:S], reads=rd, writes=h2Tb)
            else:
                kb.dma(SP, h2_ld, h2T[:, :, :], h2s_d[:, :, lo:hi], reads=rd, writes=h2Tb)

        def load_rope(b):
            kb.dma(SP, rope_ld, ropeT[:, :, :], rope_d[:, :, b * TB:(b + 1) * TB], writes=[bf("rope")])

        nbB = NB if stop_after is None else stop_after
        if limit < 4:
            nbB = 0
        else:
            load_blockB(0, 0)
            load_h2T(0)
            load_rope(0)
        pending_final = [None]
        for b in range(nbB):
            xi = b % 2
            if blimit >= 1:
                q_stage(b, extra=pending_final[0])
                pending_final[0] = None
            if b + 1 < nbB:
                load_blockB(b + 1, 1 - xi)
            if b + 1 < nbB:
                load_rope(b + 1)
            if blimit >= 2:
                attn_stage(b)
            if blimit >= 3:
                conv_stage(b)
            if b + 1 < nbB:
                load_h2T(b + 1)
            if blimit < 4:
                continue
            if debug and b == 0:
                for c in range(16):
                    tt = TT[6 + c % 2]
                    mf = mixT[:, :, :].rearrange("p a b -> p (a b)")
                    kb.op(DVE, lambda c=c, tt=tt, mf=mf: V.tensor_copy(out=tt[:, 0:256], in_=mf[:, c * 256:(c + 1) * 256]),
                          reads=[bf("mixT")], writes=[TTb[6 + c % 2]])
                    kb.dma(POOL, dbg_s2[c % 2], dbg_mix[:, c * 256:(c + 1) * 256], tt[:, 0:256], reads=[TTb[6 + c % 2]])
            mixout_stage(b, xi)
            if debug and b == 0:
                kb.dma(POOL, dbg_st, dbg_x2[:, :].rearrange("(t p) d -> p t d", p=128), XB[xi][:, :, :], reads=XBb[xi])
            if blimit >= 5:
                ffn(xi, 4, M_A3, M_B3, 3, U_F2)
            if blimit >= 6:
                pending_final[0] = final_gen(b, xi)
        if pending_final[0] is not None:
            for _ in pending_final[0]:
                pass

        for ds in XB_st + [dbg_st, misc_st, h2_st, wkv_ld] + dbg_s2 + ring_ss + ring_dc + stg_dc + stg_ss:
            if ds.count:
                POOL.wait((ds.sem, ds.count))
        for e in (PE, ACT, DVE, POOL):
            if e.n:
                SP.wait((e.sem, e.n))
        for ds in XB_ld + ring_ds + [h2_ld, rope_ld, misc_ld] + ada_ld:
            if ds.count:
                SP.wait((ds.sem, ds.count))
    return nc


def _consts():
    cm = np.zeros((128, 640), np.float32)
    cm[:, 0:128] = np.eye(128, dtype=np.float32)
    R = np.zeros((128, 128), np.float32)
    for i in range(64):
        R[2 * i + 1, 2 * i] = -1.0
        R[2 * i, 2 * i + 1] = 1.0
    cm[:, 128:256] = R
    bd = np.zeros((128, 128), np.float32)
    bd[0:64, 0:64] = 1.0 / 64.0
    bd[64:128, 64:128] = 1.0 / 64.0
    cm[:, 256:384] = bd
    cm[:, 384:512] = 1.0 / 512.0
    cm[:, 512:640] = 1.0
    GRID_W = 64
    rows = S // GRID_W
    row = np.repeat(np.arange(rows, dtype=np.float32), GRID_W)
    col = np.tile(np.arange(GRID_W, dtype=np.float32), rows)
    n_freq = 16
    inv = (np.float32(10000.0) ** (-np.arange(n_freq, dtype=np.float32) / np.float32(n_freq))).astype(np.float32)
    ang = np.concatenate([row[:, None] * inv, col[:, None] * inv], axis=-1).astype(np.float32)
    cos = np.cos(ang).astype(np.float32)
    sin = np.sin(ang).astype(np.float32)
    pi = (np.arange(128) % 64) // 2
    rope = np.stack([cos[:, pi].T, sin[:, pi].T], axis=1).astype(np.float32)
    return cm, np.ascontiguousarray(rope)


def _fm(v):
    return np.ascontiguousarray(np.asarray(v, np.float32).reshape(8, 128).T)


def make_in_maps(inputs, cores):
    cm, rope = _consts()
    par = np.zeros((128, NPAR), np.float32)
    par[:, 0:8] = _fm(inputs["norm_ffn1"][0])
    par[:, 8:16] = _fm(inputs["norm_mix"][0])
    par[:, 16:24] = _fm(inputs["norm_ffn2"][0])
    p = np.arange(128)
    par[:, 24] = np.asarray(inputs["q_norm"][0])[p % 64]
    par[:, 25] = np.asarray(inputs["k_norm"][0])[p % 64]
    aon = np.asarray(inputs["attn_out_norm"][0])
    for j in range(4):
        par[:, 26 + j] = aon[(j + 4 * (p // 64)) * 64 + (p % 64)]
    con = np.asarray(inputs["conv_out_norm"][0])
    cwv = np.asarray(inputs["conv_w"][0])
    for c in range(4):
        par[:, 30 + c] = con[c * 128 + p]
        for k in range(3):
            par[:, 34 + c * 3 + k] = cwv[k, c * 128 + p]
    par[:, 48:120] = np.asarray(inputs["b_ada"][0], np.float32).reshape(72, 128).T
    shared = {
        "w_ada": np.ascontiguousarray(inputs["w_ada"][0]),
        "b_ada": np.ascontiguousarray(np.asarray(inputs["b_ada"], np.float32).reshape(1, 9 * D)),
        "w1i": np.ascontiguousarray(inputs["w_ffn1_in"][0]),
        "w1o": np.ascontiguousarray(inputs["w_ffn1_out"][0]),
        "wmi": np.ascontiguousarray(inputs["w_mix_in"][0]),
        "wmo": np.ascontiguousarray(inputs["w_mix_out"][0]),
        "w2i": np.ascontiguousarray(inputs["w_ffn2_in"][0]),
        "w2o": np.ascontiguousarray(inputs["w_ffn2_out"][0]),
        "params": par,
        "fnorm": np.ascontiguousarray(np.asarray(inputs["final_norm"], np.float32).reshape(1, D)),
        "cmat": cm,
        "rope": rope,
    }
    maps = []
    cctx = np.asarray(inputs["c_ctx"], np.float32)
    for b in cores:
        cc = np.stack([_fm(inputs["c"][b]), _fm(cctx)], axis=1)
        m = dict(shared)
        m["x"] = np.ascontiguousarray(inputs["x"][b])
        m["ctx"] = np.ascontiguousarray(inputs["ctx"][b])
        m["cc"] = np.ascontiguousarray(cc)
        maps.append(m)
    return maps


_NC_CACHE = {}


def kernel(**inputs):
    inputs = {k: np.asarray(v) for k, v in inputs.items()}
    if "nc" not in _NC_CACHE:
        _NC_CACHE["nc"] = build_program()
    nc = _NC_CACHE["nc"]
    maps = make_in_maps(inputs, list(range(8)))
    res = run_bass_kernel_spmd(nc, maps, core_ids=list(range(8)))
    out = np.stack([np.asarray(r["out"], np.float32) for r in res.results], axis=0)
    return out
```

```python
import numpy as np
from contextlib import ExitStack
import concourse.bass as bass
import concourse.mybir as mybir
from concourse.bass_utils import run_bass_kernel_spmd

F32 = mybir.dt.float32
BF16 = mybir.dt.bfloat16
AF = mybir.ActivationFunctionType
ALU = mybir.AluOpType

D = 1024
S = 4096
CTX = 256
DFF = 2816
NF = 22
EPS = 1e-6
TB = 512
NB = S // TB
NKC = 34
NS = 5
SLOT = 3072
SEM_LIMIT = 30000

U_F1 = 0
U_F2 = 30
U_Q = 60
U_CV = 62
U_MO = 66
NU = 69

NPAR = 48 + 72


class Buf:
    __slots__ = ("name", "w", "rs", "psum")

    def __init__(self, name="", psum=False):
        self.name = name
        self.w = None
        self.rs = []
        self.psum = psum


class DSem:
    def __init__(self, sem):
        self.sem = sem
        self.count = 0


class Eng:
    def __init__(self, name, h, sems, is_pe=False):
        self.name = name
        self.h = h
        self.sems = list(sems)
        self.sem = self.sems.pop(0)
        self.n = 0
        self.known = {}
        self.is_pe = is_pe
        self.skip_waw = False
        self.mine = set([id(self.sem)])

    def wait(self, ev):
        if ev is None:
            return
        sem, val = ev
        k = id(sem)
        if self.known.get(k, 0) >= val:
            return
        self.h.wait_ge(sem, val)
        self.known[k] = val

    def tick(self, inst):
        if self.n >= SEM_LIMIT:
            self.sem = self.sems.pop(0)
            self.mine.add(id(self.sem))
            self.n = 0
        self.n += 1
        inst.then_inc(self.sem, 1)
        return (self.sem, self.n)


class KB:
    def __init__(self, nc, mk_sem):
        self.nc = nc
        self.pe = Eng("pe", nc.tensor, [mk_sem("pe%d" % i) for i in range(2)], is_pe=True)
        self.act = Eng("act", nc.scalar, [mk_sem("act%d" % i) for i in range(2)])
        self.dve = Eng("dve", nc.vector, [mk_sem("dve%d" % i) for i in range(3)])
        self.pool = Eng("pool", nc.gpsimd, [mk_sem("pool%d" % i) for i in range(2)])
        self.sp = Eng("sp", nc.sync, [mk_sem("sp%d" % i) for i in range(1)])
        self.act.skip_waw = True

    def _deps(self, eng, reads, writes):
        def need(ev):
            if ev is None:
                return
            if eng.is_pe and id(ev[0]) in eng.mine:
                return
            eng.wait(ev)
        for b in reads:
            need(b.w)
            if b.psum:
                for ev in b.rs:
                    if id(ev[0]) not in eng.mine:
                        need(ev)
        for b in writes:
            if not (eng.skip_waw and b.w is not None and id(b.w[0]) in eng.mine):
                need(b.w)
            for ev in b.rs:
                need(ev)

    def _commit(self, ev, reads, writes):
        for b in reads:
            b.rs.append(ev)
            if len(b.rs) > 48:
                best = {}
                for s, v in b.rs:
                    if id(s) not in best or best[id(s)][1] < v:
                        best[id(s)] = (s, v)
                b.rs = list(best.values())
        for b in writes:
            b.w = ev
            b.rs = []

    def op(self, eng, fn, reads=(), writes=()):
        self._deps(eng, reads, writes)
        inst = fn()
        ev = eng.tick(inst)
        self._commit(ev, reads, writes)
        return ev

    def dma(self, eng, dsem, out, in_, reads=(), writes=(), **kw):
        self._deps(eng, reads, writes)
        inst = eng.h.dma_start(out=out, in_=in_, **kw)
        dsem.count += 16
        inst.then_inc(dsem.sem, 16)
        ev = (dsem.sem, dsem.count)
        self._commit(ev, reads, writes)
        return ev


QDBG = [None]
NFILL = [0]
PFLIM = [10 ** 9]


def build_program(debug=False, stop_after=None, limit=4, blimit=9):
    nc = bass.Bass("TRN2", target_bir_lowering=False)

    def din(name, shape, dt=F32):
        return nc.dram_tensor(name, list(shape), dt, kind="ExternalInput").ap()

    x_d = din("x", [S, D])
    ctx_d = din("ctx", [CTX, D])
    cc_d = din("cc", [128, 2, 8])
    wada_d = din("w_ada", [D, 9 * D])
    bada_d = din("b_ada", [1, 9 * D])
    w1i_d = din("w1i", [D, 2 * DFF])
    w1o_d = din("w1o", [DFF, D])
    wmi_d = din("wmi", [D, 2304])
    wmo_d = din("wmo", [D, D])
    w2i_d = din("w2i", [D, 2 * DFF])
    w2o_d = din("w2o", [DFF, D])
    par_d = din("params", [128, NPAR])
    fn_d = din("fnorm", [1, D])
    cm_d = din("cmat", [128, 640])
    rope_d = din("rope", [128, 2, S])
    out_d = nc.dram_tensor("out", [S, D], F32, kind="ExternalOutput").ap()

    skind = "ExternalOutput" if debug else "Internal"
    wscr_d = nc.dram_tensor("wscr", [NU, 128, SLOT], BF16, kind="Internal").ap()
    x1s_d = nc.dram_tensor("x1s", [S, D], F32, kind=skind).ap()
    h2s_d = nc.dram_tensor("h2s", [128, 8, S], BF16, kind="Internal").ap()
    if debug:
        dbg_kt = nc.dram_tensor("dbg_kt", [128, NKC * 128], F32, kind="ExternalOutput").ap()
        dbg_va = nc.dram_tensor("dbg_va", [128, NKC * 192], F32, kind="ExternalOutput").ap()
        dbg_mt = nc.dram_tensor("dbg_mt", [128, 144], F32, kind="ExternalOutput").ap()
        dbg_g = nc.dram_tensor("dbg_g", [128, 4 * D], F32, kind="ExternalOutput").ap()
        dbg_mix = nc.dram_tensor("dbg_mix", [128, 8 * TB], F32, kind="ExternalOutput").ap()
        dbg_x2 = nc.dram_tensor("dbg_x2", [TB, D], F32, kind="ExternalOutput").ap()

    with ExitStack() as es:
        def sb(name, shape, dt):
            return es.enter_context(nc.sbuf_tensor("sb_" + name, list(shape), dt))

        def mk_sem(name):
            return es.enter_context(nc.semaphore("sem_" + name))

        kb = KB(nc, mk_sem)
        PE, ACT, DVE, POOL, SP = kb.pe, kb.act, kb.dve, kb.pool, kb.sp
        V = nc.vector
        A_ = nc.scalar
        T_ = nc.tensor
        G_ = nc.gpsimd

        KT = sb("KT", [128, NKC * 128], BF16)
        VA = sb("VA", [128, NKC, 192], BF16)
        Wkv = sb("Wkv", [128, 8, 256], BF16)
        cmat = sb("cmat", [128, 640], F32)
        identb = sb("identb", [128, 128], BF16)
        par = sb("par", [128, NPAR], F32)
        mods = sb("mods", [128, 10, 8], F32)
        mT = sb("mT", [128, 72, 2], F32)
        Gt = [sb("G%d" % i, [128, D], F32) for i in range(4)]
        fnbc = sb("fnbc", [128, D], F32)
        ring = [sb("ring%d" % i, [128, SLOT], BF16) for i in range(NS)]
        XB = [sb("XB%d" % i, [128, 4, D], F32) for i in range(2)]
        xn = sb("xn", [128, 4, D], BF16)
        hT = sb("hT", [128, 8, TB], BF16)
        h2T = sb("h2T", [128, 8, TB + 4], BF16)
        actT = sb("actT", [128, NF, TB], BF16)
        TT = [sb("T%d" % i, [128, 512], F32) for i in range(8)]
        ropeT = sb("ropeT", [128, 2, TB], F32)
        QT = sb("QT", [128, 4, TB], BF16)
        PTe = [sb("PTe%d" % i, [128, 1024], BF16) for i in range(3)]
        mixT = sb("mixT", [128, 8, TB], BF16)
        junk = sb("junk", [128, D], BF16)
        stg1 = sb("stg1", [128, SLOT], BF16)
        stat = sb("stat", [128, 16], F32)
        Sada = sb("Sada", [128, 8, 33], F32)
        cct = sb("cct", [128, 2, 8], F32)
        epsT = sb("epsT", [128, 1], F32)

        PS = [es.enter_context(nc.psum_tensor("PS%d" % i, [128, 1024], F32)) for i in range(4)]
        PSb = [[Buf("PS%d_%d" % (i, h), psum=True) for h in range(2)] for i in range(4)]

        def psh(i, h, n=512):
            return PS[i][:, h * 512:h * 512 + n]

        B = {}

        def bf(name):
            if name not in B:
                B[name] = Buf(name)
            return B[name]

        ringb = [Buf("ring%d" % i) for i in range(NS)]
        ring_ds = [DSem(mk_sem("ringl%d" % i)) for i in range(NS)]
        ring_dc = [DSem(mk_sem("ringc%d" % i)) for i in range(NS)]
        wkv_ld = DSem(mk_sem("wkvl"))
        stg_dc = [DSem(mk_sem("stgc%d" % i)) for i in range(2)]
        stg_ss = [DSem(mk_sem("stgs%d" % i)) for i in range(2)]
        ring_ss = [DSem(mk_sem("rings%d" % i)) for i in range(NS)]
        scrb = [Buf("scr%d" % u) for u in range(NU)]
        XBb = [[Buf("XB%d_%d" % (i, t)) for t in range(4)] for i in range(2)]
        XB_ld = [DSem(mk_sem("xbl%d" % i)) for i in range(2)]
        XB_st = [DSem(mk_sem("xbs%d" % i)) for i in range(2)]
        TTb = [Buf("T%d" % i) for i in range(8)]
        misc_ld = DSem(mk_sem("miscl"))
        misc_st = DSem(mk_sem("miscs"))
        ada_ld = [DSem(mk_sem("adal%d" % i)) for i in range(2)]
        h2_ld = DSem(mk_sem("h2l"))
        h2_st = DSem(mk_sem("h2s"))
        rope_ld = DSem(mk_sem("ropel"))
        dbg_st = DSem(mk_sem("dbgs"))
        dbg_s2 = [DSem(mk_sem("dbgs2_%d" % i)) for i in range(2)]
        x1sb = [Buf("x1s%d" % b) for b in range(NB)]
        h2sb = [Buf("h2s%d" % b) for b in range(NB)]

        ident_f = cmat[:, 0:128]
        Rm = cmat[:, 128:256]
        bd64 = cmat[:, 256:384]
        o512 = cmat[:, 384:512]
        ones_m = cmat[:, 512:640]

        g1 = par[:, 0:8]
        gm = par[:, 8:16]
        g2 = par[:, 16:24]
        qg = par[:, 24:25]
        kg = par[:, 25:26]
        ga = par[:, 26:30]
        gcv = par[:, 30:34]
        cw = par[:, 34:46]
        bT = par[:, 48:120]

        M_A1, M_B1, M_A1c, M_B1c, M_A2, M_B2, M_A2c, M_B2c, M_A3, M_B3 = range(10)

        def mod(i):
            return mods[:, i, :]

        kb.dma(SP, misc_ld, par[:], par_d[:, :], writes=[bf("par")])
        kb.dma(SP, misc_ld, cmat[:], cm_d[:, :], writes=[bf("cmat")])
        kb.dma(SP, misc_ld, cct[:], cc_d[:, :, :], writes=[bf("cct")])
        kb.dma(SP, misc_ld, fnbc[:], fn_d.partition_broadcast(128), writes=[bf("fnbc")])
        GSPEC = ((0, 2, 0.5, "G0"), (1, 2, 0.5, "G1c"), (2, 5, 1.0, "G2"), (3, 8, 0.5, "G3"))
        for gi, v, sc, nm in GSPEC:
            kb.dma(SP, misc_ld, Gt[gi][:], bada_d[0:1, v * D:(v + 1) * D].partition_broadcast(128), writes=[bf(nm)])
        for gi, v, sc, nm in GSPEC:
            bf(nm).w = (misc_ld.sem, misc_ld.count)
        for nm in ("par", "cmat", "cct", "fnbc"):
            bf(nm).w = (misc_ld.sem, misc_ld.count)

        kb.op(DVE, lambda: V.tensor_copy(out=identb[:], in_=ident_f), reads=[bf("cmat")], writes=[bf("identb")])
        kb.op(DVE, lambda: V.memset(Sada[:], 0.0), writes=[bf("Sada")])
        kb.op(DVE, lambda: V.memset(VA[:, :, 64:128], 1.0), writes=[bf("VAones")])
        kb.op(DVE, lambda: V.memset(stat[:], 0.0), writes=[bf("stat")])
        kb.op(DVE, lambda: V.memset(epsT[:], EPS), writes=[bf("epsT")])
        kb.op(ACT, lambda: A_.activation(out=Sada[:, :, 0], in_=cct[:, 0, :], func=AF.Silu),
              reads=[bf("cct")], writes=[bf("Sada")])
        kb.op(ACT, lambda: A_.activation(out=Sada[:, :, 32], in_=cct[:, 1, :], func=AF.Silu),
              reads=[bf("cct")], writes=[bf("Sada")])

        kb.dma(POOL, wkv_ld, Wkv[:, :, :],
               wmi_d[:, 512:768].rearrange("(kc p) c -> p kc c", p=128), writes=[bf("Wkv")])

        def cast_unit(u, sl, slb, dsem):
            w = [slb]

            def cd(out, in_):
                kb.dma(POOL, dsem, out, in_, writes=w)
            if u < U_Q:
                wi, wo = (w1i_d, w1o_d) if u < U_F2 else (w2i_d, w2o_d)
                r = u - (U_F1 if u < U_F2 else U_F2)
                if r < NF:
                    f = r
                    slv = sl[:, 0:2048].rearrange("p (k c) -> p k c", c=256)
                    cd(slv[:, :, 0:128], wi[:, f * 128:(f + 1) * 128].rearrange("(kc p) c -> p kc c", p=128))
                    cd(slv[:, :, 128:256],
                       wi[:, DFF + f * 128:DFF + (f + 1) * 128].rearrange("(kc p) c -> p kc c", p=128))
                else:
                    o = r - NF
                    for fi in range(3):
                        f = o * 3 + fi
                        if f < NF:
                            cd(sl[:, fi * 1024:(fi + 1) * 1024], wo[f * 128:(f + 1) * 128, :])
            elif u < U_CV:
                i = u - U_Q
                slv = sl[:, 0:2048].rearrange("p (k c) -> p k c", c=256)
                for jj in range(2):
                    j = i * 2 + jj
                    cd(slv[:, :, jj * 128:jj * 128 + 64],
                       wmi_d[:, j * 64:(j + 1) * 64].rearrange("(kc p) c -> p kc c", p=128))
                    cd(slv[:, :, jj * 128 + 64:jj * 128 + 128],
                       wmi_d[:, (j + 4) * 64:(j + 5) * 64].rearrange("(kc p) c -> p kc c", p=128))
            elif u < U_MO:
                c = u - U_CV
                slv = sl[:, 0:3072].rearrange("p (k c) -> p k c", c=384)
                for gi_, off in enumerate((768, 1280, 1792)):
                    cd(slv[:, :, gi_ * 128:(gi_ + 1) * 128],
                       wmi_d[:, off + c * 128:off + (c + 1) * 128].rearrange("(kc p) c -> p kc c", p=128))
            else:
                o = u - U_MO
                for ki in range(3):
                    k = o * 3 + ki
                    if k >= 8:
                        continue
                    dst = sl[:, ki * 1024:(ki + 1) * 1024]
                    if k < 4:
                        cd(dst[0:64, :], wmo_d[k * 64:(k + 1) * 64, :])
                        cd(dst[64:128, :], wmo_d[(k + 4) * 64:(k + 5) * 64, :])
                    else:
                        cd(dst, wmo_d[512 + (k - 4) * 128:512 + (k - 3) * 128, :])

        def usize(u):
            if u < U_Q:
                r = u - (U_F1 if u < U_F2 else U_F2)
                if r < NF:
                    return 2048
                return 1024 * min(3, NF - (r - NF) * 3)
            if u < U_CV:
                return 2048
            if u < U_MO:
                return 3072
            return 1024 * min(3, 8 - (u - U_MO) * 3)

        def store_unit(u, sl, slb, dsem):
            n = usize(u)
            kb.dma(POOL, dsem, wscr_d[u, :, 0:n], sl[:, 0:n], reads=[slb], writes=[scrb[u]])

        stg = [mixT[:, :, :].rearrange("p a b -> p (a b)"), stg1[:, :]]
        stgb = [bf("mixT"), Buf("stg1")]
        bg_state = [U_F2, 0]

        def bg_cast(n_units):
            for _ in range(n_units):
                u = bg_state[0]
                if u >= NU:
                    return
                k = bg_state[1] % 2
                cast_unit(u, stg[k], stgb[k], stg_dc[k])
                store_unit(u, stg[k], stgb[k], stg_ss[k])
                bg_state[0] += 1
                bg_state[1] += 1

        NPRE = U_F2
        pre_u = [0]
        st_u = [0]

        def precast_issue(n):
            for _ in range(n):
                if pre_u[0] < NPRE:
                    u = pre_u[0]
                    cast_unit(u, ring[u % NS], ringb[u % NS], ring_dc[u % NS])
                    pre_u[0] += 1

        def precast_store(n):
            for _ in range(n):
                if st_u[0] < pre_u[0]:
                    u = st_u[0]
                    n_ = usize(u)
                    kb.dma(SP, ring_ss[u % NS], wscr_d[u, :, 0:n_], ring[u % NS][:, 0:n_],
                           reads=[ringb[u % NS]], writes=[scrb[u]])
                    st_u[0] += 1

        precast_issue(NS)

        for gi, v, sc, nm in GSPEC:
            if sc != 1.0:
                kb.op(DVE, lambda gi=gi, sc=sc: V.tensor_scalar(out=Gt[gi][:], in0=Gt[gi][:], scalar1=sc, scalar2=None,
                                                                op0=ALU.mult), reads=[bf(nm)], writes=[bf(nm)])
        PMT = PS[1][:, 0:144]
        ada_first = True
        for q in range(18):
            xb = XB[q % 2]
            xbv = xb[:, :, :].rearrange("p a (b c) -> p (a b) c", c=512)
            kb.dma(SP, ada_ld[q % 2], xbv,
                   wada_d[:, q * 512:(q + 1) * 512].rearrange("(kc p) c -> p kc c", p=128),
                   writes=XBb[q % 2])
            precast_store(2)
            precast_issue(2)
            pm = PS[0][0:33, 0:512]

            def mm_ada(xbv=xbv, pm=pm):
                last = None
                for kc in range(8):
                    last = T_.matmul(pm, lhsT=Sada[:, kc, :], rhs=xbv[:, kc, :], start=(kc == 0), stop=(kc == 7))
                return last
            kb.op(PE, mm_ada, reads=XBb[q % 2] + [bf("Sada")], writes=[PSb[0][0]])
            mrow = TT[q % 2]
            kb.op(ACT, lambda mrow=mrow, pm=pm: A_.activation(out=mrow[0:33, :], in_=pm, func=AF.Copy),
                  reads=[PSb[0][0]], writes=[TTb[q % 2]])

            def mm_tr(mrow=mrow, q=q):
                last = None
                for jj in range(4):
                    j = q * 4 + jj
                    for w in range(2):
                        last = T_.matmul(PS[1][:, j * 2 + w:j * 2 + w + 1],
                                         lhsT=mrow[32 * w:32 * w + 1, jj * 128:(jj + 1) * 128],
                                         rhs=ones_m[32 * w:32 * w + 1, 0:1], start=True, stop=True)
                return last
            kb.op(PE, mm_tr, reads=[TTb[q % 2], bf("cmat")], writes=[PSb[1][0]])
            v = q // 2
            if v in (2, 5, 8):
                gi = {2: 0, 5: 2, 8: 3}[v]
                sc = 1.0 if v == 5 else 0.5
                half = q % 2
                pb = psh(2, 0)
                kb.op(PE, lambda mrow=mrow, pb=pb: T_.matmul(pb, lhsT=ones_m[0:1, :], rhs=mrow[0:1, :],
                                                          start=True, stop=True),
                      reads=[TTb[q % 2], bf("cmat")], writes=[PSb[2][0]])
                kb.op(DVE, lambda pb=pb, gi=gi, half=half, sc=sc: V.scalar_tensor_tensor(
                    out=Gt[gi][:, half * 512:(half + 1) * 512], in0=pb, scalar=sc,
                    in1=Gt[gi][:, half * 512:(half + 1) * 512], op0=ALU.mult, op1=ALU.add),
                    reads=[PSb[2][0], bf("G%d" % gi)], writes=[bf("G%d" % gi)])
                if v == 2:
                    pb2 = psh(2, 1)
                    kb.op(PE, lambda mrow=mrow, pb2=pb2: T_.matmul(pb2, lhsT=ones_m[32:33, :], rhs=mrow[32:33, :],
                                                                start=True, stop=True),
                          reads=[TTb[q % 2], bf("cmat")], writes=[PSb[2][1]])
                    kb.op(DVE, lambda pb2=pb2, half=half: V.scalar_tensor_tensor(
                        out=Gt[1][:, half * 512:(half + 1) * 512], in0=pb2, scalar=0.5,
                        in1=Gt[1][:, half * 512:(half + 1) * 512], op0=ALU.mult, op1=ALU.add),
                        reads=[PSb[2][1], bf("G1c")], writes=[bf("G1c")])
        while st_u[0] < NPRE:
            precast_store(1)
            precast_issue(1)
        kb.op(DVE, lambda: V.tensor_tensor(out=mT[:, :, 0], in0=PS[1][:, 0:144:2], in1=bT, op=ALU.add),
              reads=[PSb[1][0], bf("par")], writes=[bf("mT")])
        kb.op(DVE, lambda: V.tensor_tensor(out=mT[:, :, 1], in0=PS[1][:, 1:144:2], in1=bT, op=ALU.add),
              reads=[PSb[1][0], bf("par")], writes=[bf("mT")])

        def mk_mod(ai, bi, vs, vsh, g, w):
            kb.op(DVE, lambda: V.scalar_tensor_tensor(out=mod(ai), in0=mT[:, vs * 8:(vs + 1) * 8, w], scalar=1.0,
                                                      in1=g, op0=ALU.add, op1=ALU.mult),
                  reads=[bf("mT"), bf("par")], writes=[bf("mods")])
            kb.op(DVE, lambda: V.tensor_copy(out=mod(bi), in_=mT[:, vsh * 8:(vsh + 1) * 8, w]),
                  reads=[bf("mT")], writes=[bf("mods")])
        mk_mod(M_A1, M_B1, 1, 0, g1, 0)
        mk_mod(M_A1c, M_B1c, 1, 0, g1, 1)
        mk_mod(M_A2, M_B2, 4, 3, gm, 0)
        mk_mod(M_A2c, M_B2c, 4, 3, gm, 1)
        mk_mod(M_A3, M_B3, 7, 6, g2, 0)

        if debug:
            kb.dma(POOL, dbg_st, dbg_mt[:, :], mT[:, :, :].rearrange("p a b -> p (a b)"), reads=[bf("mT")])
            for i in range(4):
                kb.dma(POOL, dbg_st, dbg_g[:, i * D:(i + 1) * D], Gt[i][:], reads=[bf("G0"), bf("G1c"), bf("G2"), bf("G3")])

        class WStream:
            def __init__(self):
                self.sched = []
                self.nl = 0
                self.nu = 0
                self.rel = 0

            def done(self, n=1):
                self.rel += n

            def prefetch(self, upto):
                while self.nl < min(upto, len(self.sched)):
                    i = self.nl
                    u = self.sched[i]
                    s = i % NS
                    n = usize(u)
                    kb.dma(SP, ring_ds[s], ring[s][:, 0:n], wscr_d[u, :, 0:n], reads=[scrb[u]], writes=[ringb[s]])
                    self.nl += 1

            def acquire(self, u):
                i = self.nu
                assert self.sched[i] == u, (i, u, self.sched[i])
                assert i < self.rel + NS
                self.prefetch(min(self.rel + NS, PFLIM[0]))
                self.nu += 1
                return ring[i % NS], ringb[i % NS]

        ws = WStream()
        ffn1_units = list(range(U_F1, U_F1 + 30))
        ffn2_units = list(range(U_F2, U_F2 + 30))
        for b in range(NB + 1):
            ws.sched += ffn1_units
        for b in range(NB):
            ws.sched += [U_Q, U_Q + 1] + list(range(U_CV, U_CV + 4)) + list(range(U_MO, U_MO + 3)) + ffn2_units

        def rms_to_T(xi, NT, Ai, Bi, dstT, col0, dstb, part=0):
            TBk = NT * 128
            for t in range(NT if part != 2 else 0):
                kb.op(ACT, lambda t=t: A_.activation(out=junk[:], in_=XB[xi][:, t, :], func=AF.Square,
                                                     scale=1.0 / 32.0, accum_out=stat[:, t:t + 1]),
                      reads=[XBb[xi][t]], writes=[bf("ss%d" % t)])
                kb.op(ACT, lambda t=t: A_.activation(out=stat[:, t:t + 1], in_=stat[:, t:t + 1], func=AF.Ln,
                                                     bias=epsT[:, 0:1]),
                      reads=[bf("ss%d" % t), bf("epsT")], writes=[bf("ss%d" % t)])
                kb.op(ACT, lambda t=t: A_.activation(out=stat[:, 4 + t:5 + t], in_=stat[:, t:t + 1], func=AF.Exp,
                                                     scale=-0.5),
                      reads=[bf("ss%d" % t)], writes=[bf("rstd%d" % t)])
                if t % 2 == 0:
                    kb.op(ACT, lambda t=t: A_.activation(out=xn[:, t, :], in_=XB[xi][:, t, :], func=AF.Copy,
                                                         scale=stat[:, 4 + t:5 + t]),
                          reads=[XBb[xi][t], bf("rstd%d" % t)], writes=[bf("xn%d" % t)])
                else:
                    kb.op(DVE, lambda t=t: V.tensor_scalar(out=xn[:, t, :], in0=XB[xi][:, t, :],
                                                           scalar1=stat[:, 4 + t:5 + t], scalar2=None, op0=ALU.mult),
                          reads=[XBb[xi][t], bf("rstd%d" % t)], writes=[bf("xn%d" % t)])
            for kc in range(8 if part != 1 else 0):
                pi, ph = 2 + kc // 4, (kc % 4) // 2
                ptv = PS[pi].bitcast(BF16)
                base = (kc % 4) * 512

                def tr(kc=kc, ptv=ptv, base=base):
                    last = None
                    for t in range(NT):
                        last = T_.transpose(out=ptv[:, base + t * 128:base + (t + 1) * 128],
                                            in_=xn[:, t, kc * 128:(kc + 1) * 128], identity=identb[:])
                    return last
                kb.op(PE, tr, reads=[bf("xn%d" % t) for t in range(NT)] + [bf("identb")], writes=[PSb[pi][ph]])
                src = ptv[:, base:base + TBk]
                dst = dstT[:, kc, col0:col0 + TBk]
                if kc % 2 == 0:
                    kb.op(ACT, lambda src=src, dst=dst, kc=kc: A_.activation(
                        out=dst, in_=src, func=AF.Identity, scale=mod(Ai)[:, kc:kc + 1], bias=mod(Bi)[:, kc:kc + 1]),
                        reads=[PSb[pi][ph], bf("mods")], writes=[dstb[kc]])
                else:
                    kb.op(DVE, lambda src=src, dst=dst, kc=kc: V.tensor_scalar(
                        out=dst, in0=src, scalar1=mod(Ai)[:, kc:kc + 1], scalar2=mod(Bi)[:, kc:kc + 1],
                        op0=ALU.mult, op1=ALU.add),
                        reads=[PSb[pi][ph], bf("mods")], writes=[dstb[kc]])

        hTb = [bf("hT%d" % kc) for kc in range(8)]
        h2Tb = [bf("h2T%d" % kc) for kc in range(8)]

        def ffn(xi, NT, Ai, Bi, gi, ubase, do_norm=True, on_pool=True, between=None):
            TBk = NT * 128
            if do_norm:
                rms_to_T(xi, NT, Ai, Bi, hT, 0, hTb)
            for f in range(NF):
                sl, slb = ws.acquire(ubase + f)
                pg = PS[f % 2][:, 0:TBk]
                pu = PS[f % 2][:, 512:512 + TBk]

                def mm1(sl=sl, pg=pg, pu=pu):
                    last = None
                    for kc in range(8):
                        T_.matmul(pg, lhsT=sl[:, kc * 256:kc * 256 + 128], rhs=hT[:, kc, 0:TBk],
                                  start=(kc == 0), stop=(kc == 7))
                    for kc in range(8):
                        last = T_.matmul(pu, lhsT=sl[:, kc * 256 + 128:kc * 256 + 256], rhs=hT[:, kc, 0:TBk],
                                         start=(kc == 0), stop=(kc == 7))
                    return last
                kb.op(PE, mm1, reads=[slb] + hTb, writes=[PSb[f % 2][0], PSb[f % 2][1]])
                sg = TT[f % 2]
                kb.op(ACT, lambda sg=sg, pg=pg: A_.activation(out=sg[:, 0:TBk], in_=pg, func=AF.Silu),
                      reads=[PSb[f % 2][0]], writes=[TTb[f % 2]])
                kb.op(DVE, lambda sg=sg, pu=pu, f=f: V.tensor_tensor(out=actT[:, f, 0:TBk], in0=sg[:, 0:TBk], in1=pu,
                                                                     op=ALU.mult),
                      reads=[TTb[f % 2], PSb[f % 2][1]], writes=[bf("act%d" % f)])
                ws.done()
            if between is not None:
                between()
            accb = [PSb[t][h] for t in range(NT) for h in range(2)]
            for o in range(8):
                sl, slb = ws.acquire(ubase + NF + o)
                fs = [o * 3 + fi for fi in range(3) if o * 3 + fi < NF]

                def mm2(sl=sl, fs=fs, o=o):
                    last = None
                    for fi, f in enumerate(fs):
                        for t in range(NT):
                            for h in range(2):
                                last = T_.matmul(psh(t, h), lhsT=actT[:, f, t * 128:(t + 1) * 128],
                                                 rhs=sl[:, fi * 1024 + h * 512:fi * 1024 + (h + 1) * 512],
                                                 start=(f == 0), stop=(f == NF - 1))
                    return last
                kb.op(PE, mm2, reads=[slb] + [bf("act%d" % f) for f in fs], writes=accb)
                ws.done()
            k = 0
            for t in range(NT):
                for h in range(2):
                    tmp = TT[2 + k % 4]
                    tb = TTb[2 + k % 4]
                    kb.op(DVE, lambda tmp=tmp, t=t, h=h: V.tensor_tensor(
                        out=tmp[:, :], in0=psh(t, h), in1=Gt[gi][:, h * 512:(h + 1) * 512], op=ALU.mult),
                        reads=[PSb[t][h], bf("G%s" % ("1c" if gi == 1 else str(gi)))], writes=[tb])
                    if on_pool:
                        kb.op(POOL, lambda tmp=tmp, t=t, h=h: G_.tensor_tensor(
                            out=XB[xi][:, t, h * 512:(h + 1) * 512], in0=XB[xi][:, t, h * 512:(h + 1) * 512],
                            in1=tmp[:, :], op=ALU.add),
                            reads=[tb, XBb[xi][t]], writes=[XBb[xi][t]])
                    else:
                        kb.op(DVE, lambda tmp=tmp, t=t, h=h: V.tensor_tensor(
                            out=XB[xi][:, t, h * 512:(h + 1) * 512], in0=XB[xi][:, t, h * 512:(h + 1) * 512],
                            in1=tmp[:, :], op=ALU.add),
                            reads=[tb, XBb[xi][t]], writes=[XBb[xi][t]])
                    k += 1

        HN_SETS = [
            dict(raw=2, sq=3, kn=4, t1=5, ps=3),
            dict(raw=6, sq=7, kn=0, t1=1, ps=1),
        ]

        def headnorm_gen(psrc, psb, n, gsc, dst, dstb, with_rope, on_pool=True, tset=0):
            ts = HN_SETS[tset]
            raw, rawb = TT[ts["raw"]], TTb[ts["raw"]]
            sq, sqb = TT[ts["sq"]], TTb[ts["sq"]]
            kn, knb = TT[ts["kn"]], TTb[ts["kn"]]
            t1, t1b = TT[ts["t1"]], TTb[ts["t1"]]
            pi = ts["ps"]
            pss, pssb = PS[pi][:, 0:n], PSb[pi][0]
            prot, protb = PS[pi][:, 512:512 + n], PSb[pi][1]
            kb.op(DVE, lambda: V.tensor_copy(out=raw[:, 0:n], in_=psrc), reads=[psb], writes=[rawb])
            yield
            kb.op(ACT, lambda: A_.activation(out=sq[:, 0:n], in_=psrc, func=AF.Square), reads=[psb], writes=[sqb])
            yield
            kb.op(PE, lambda: T_.matmul(pss, lhsT=bd64, rhs=sq[:, 0:n], start=True, stop=True),
                  reads=[sqb, bf("cmat")], writes=[pssb])
            yield
            kb.op(ACT, lambda: A_.activation(out=sq[:, 0:n], in_=pss, func=AF.Ln, bias=epsT[:, 0:1]),
                  reads=[pssb, bf("epsT")], writes=[sqb])
            yield
            kb.op(ACT, lambda: A_.activation(out=sq[:, 0:n], in_=sq[:, 0:n], func=AF.Exp, scale=-0.5),
                  reads=[sqb], writes=[sqb])
            yield
            if not with_rope:
                kb.op(DVE, lambda: V.scalar_tensor_tensor(out=dst, in0=raw[:, 0:n], scalar=gsc, in1=sq[:, 0:n],
                                                          op0=ALU.mult, op1=ALU.mult),
                      reads=[rawb, sqb, bf("par")], writes=[dstb])
                yield
                return
            kb.op(DVE, lambda: V.scalar_tensor_tensor(out=kn[:, 0:n], in0=raw[:, 0:n], scalar=gsc, in1=sq[:, 0:n],
                                                      op0=ALU.mult, op1=ALU.mult),
                  reads=[rawb, sqb, bf("par")], writes=[knb])
            yield
            kb.op(PE, lambda: T_.matmul(prot, lhsT=Rm, rhs=kn[:, 0:n], start=True, stop=True),
                  reads=[knb, bf("cmat")], writes=[protb])
            yield
            if on_pool:
                kb.op(POOL, lambda: G_.tensor_tensor(out=t1[:, 0:n], in0=kn[:, 0:n], in1=ropeT[:, 0, 0:n], op=ALU.mult),
                      reads=[knb, bf("rope")], writes=[t1b])
            else:
                kb.op(DVE, lambda: V.tensor_tensor(out=t1[:, 0:n], in0=kn[:, 0:n], in1=ropeT[:, 0, 0:n], op=ALU.mult),
                      reads=[knb, bf("rope")], writes=[t1b])
            yield
            kb.op(DVE, lambda: V.tensor_tensor(out=raw[:, 0:n], in0=prot, in1=ropeT[:, 1, 0:n], op=ALU.mult),
                  reads=[protb, bf("rope")], writes=[rawb])
            yield
            kb.op(DVE, lambda: V.tensor_tensor(out=dst, in0=t1[:, 0:n], in1=raw[:, 0:n], op=ALU.add),
                  reads=[t1b, rawb], writes=[dstb])
            yield

        def headnorm_rope(psrc, psb, n, gsc, dst, dstb, with_rope, on_pool=True):
            for _ in headnorm_gen(psrc, psb, n, gsc, dst, dstb, with_rope, on_pool=on_pool, tset=0):
                pass

        def kv_stage(NT, kc0, with_rope):
            TBk = NT * 128
            pk = PS[0][:, 0:TBk]

            def mmk():
                last = None
                for kc in range(8):
                    last = T_.matmul(pk, lhsT=Wkv[:, kc, 0:128], rhs=h2T[:, kc, 2:2 + TBk],
                                     start=(kc == 0), stop=(kc == 7))
                return last
            kb.op(PE, mmk, reads=[bf("Wkv")] + h2Tb, writes=[PSb[0][0]])
            pv = PS[0][:, 512:512 + TBk]

            def mmv():
                last = None
                for t in range(NT):
                    for kc in range(8):
                        last = T_.matmul(pv[:, t * 128:(t + 1) * 128], lhsT=h2T[:, kc, 2 + t * 128:2 + (t + 1) * 128],
                                         rhs=Wkv[:, kc, 128:256], start=(kc == 0), stop=(kc == 7))
                return last
            kb.op(PE, mmv, reads=[bf("Wkv")] + h2Tb, writes=[PSb[0][1]])
            pv3 = pv.rearrange("p (t c) -> p t c", c=128)
            kb.op(ACT, lambda: A_.activation(out=VA[:, kc0:kc0 + NT, 0:64], in_=pv3[:, :, 0:64], func=AF.Copy),
                  reads=[PSb[0][1]], writes=[bf("VA")])
            kb.op(DVE, lambda: V.tensor_copy(out=VA[:, kc0:kc0 + NT, 128:192], in_=pv3[:, :, 64:128]),
                  reads=[PSb[0][1]], writes=[bf("VA")])
            headnorm_rope(pk, PSb[0][0], TBk, kg, KT[:, kc0 * 128:kc0 * 128 + TBk], bf("KT"), with_rope, on_pool=False)

        def load_x(src_ap, xi, NT, b_reads=()):
            kb.dma(SP, XB_ld[xi], XB[xi][:, 0:NT, :], src_ap.rearrange("(t p) d -> p t d", p=128),
                   reads=list(b_reads), writes=XBb[xi][0:NT])

        if limit >= 2:
            load_x(ctx_d[:, :], 0, 2)
            load_x(x_d[0:TB, :], 1, 4)
            rms_to_T(0, 2, M_A1c, M_B1c, hT, 0, hTb)
            ffn(0, 2, M_A1c, M_B1c, 1, U_F1, do_norm=False, on_pool=False,
                between=(lambda: rms_to_T(1, 4, M_A1, M_B1, hT, 0, hTb, part=1)) if limit >= 3 else None)
            if limit >= 3:
                rms_to_T(1, 4, M_A1, M_B1, hT, 0, hTb, part=2)
            rms_to_T(0, 2, M_A2c, M_B2c, h2T, 2, h2Tb)
            kv_stage(2, 0, False)
        for b in range(NB if limit >= 3 else 0):
            xi = (b + 1) % 2
            if b + 1 < NB:
                load_x(x_d[(b + 1) * TB:(b + 2) * TB, :], 1 - xi, 4)
            kb.dma(SP, rope_ld, ropeT[:, :, :], rope_d[:, :, b * TB:(b + 1) * TB], writes=[bf("rope")])
            if b < 4:
                bg_cast(10)
            nxt = (lambda xi=xi: rms_to_T(1 - xi, 4, M_A1, M_B1, hT, 0, hTb, part=1)) if b + 1 < NB else None
            ffn(xi, 4, M_A1, M_B1, 0, U_F1, do_norm=False, on_pool=False, between=nxt)
            kb.dma(POOL, XB_st[xi], x1s_d[b * TB:(b + 1) * TB, :].rearrange("(t p) d -> p t d", p=128),
                   XB[xi][:, :, :], reads=XBb[xi], writes=[x1sb[b]])
            if b + 1 < NB:
                rms_to_T(1 - xi, 4, M_A1, M_B1, hT, 0, hTb, part=2)
            rms_to_T(xi, 4, M_A2, M_B2, h2T, 2, h2Tb)
            kb.dma(POOL, h2_st, h2s_d[:, :, b * TB:(b + 1) * TB], h2T[:, :, 2:2 + TB], reads=h2Tb, writes=[h2sb[b]])
            kv_stage(4, 2 + b * 4, True)

        for b in range(NB):
            xi = (b + 1) % 2
            x1sb[b].w = (XB_st[xi].sem, XB_st[xi].count)
            h2sb[b].w = (h2_st.sem, h2_st.count)
        if debug:
            for c in range(NKC * 128 // 256):
                tt = TT[c % 2]
                kb.op(DVE, lambda c=c, tt=tt: V.tensor_copy(out=tt[:, 0:256], in_=KT[:, c * 256:(c + 1) * 256]),
                      reads=[bf("KT")], writes=[TTb[c % 2]])
                kb.dma(POOL, dbg_s2[c % 2], dbg_kt[:, c * 256:(c + 1) * 256], tt[:, 0:256], reads=[TTb[c % 2]])
            vaf = VA[:, :, :].rearrange("p a b -> p (a b)")
            for c in range(NKC * 192 // 384):
                tt = TT[c % 2]
                kb.op(DVE, lambda c=c, tt=tt: V.tensor_copy(out=tt[:, 0:384], in_=vaf[:, c * 384:(c + 1) * 384]),
                      reads=[bf("VA"), bf("VAones")], writes=[TTb[c % 2]])
                kb.dma(POOL, dbg_s2[c % 2], dbg_va[:, c * 384:(c + 1) * 384], tt[:, 0:384], reads=[TTb[c % 2]])

        def q_stage(b, extra=None):
            for i in range(2):
                sl, slb = ws.acquire(U_Q + i)
                gens = []
                for jj in range(2):
                    j = i * 2 + jj
                    pq = PS[0][:, jj * 512:jj * 512 + TB]

                    def mmq(sl=sl, jj=jj, pq=pq):
                        last = None
                        for kc in range(8):
                            last = T_.matmul(pq, lhsT=sl[:, kc * 256 + jj * 128:kc * 256 + (jj + 1) * 128],
                                             rhs=h2T[:, kc, 2:2 + TB], start=(kc == 0), stop=(kc == 7))
                        return last
                    kb.op(PE, mmq, reads=[slb] + h2Tb, writes=[PSb[0][jj]])
                    gens.append(headnorm_gen(pq, PSb[0][jj], TB, qg, QT[:, j, :], bf("QT%d" % j), True, tset=jj))
                if extra is not None and i == 0:
                    gens.append(extra)
                live = list(gens)
                while live:
                    for g in list(live):
                        try:
                            next(g)
                        except StopIteration:
                            live.remove(g)
                ws.done()

        def attn_stage(b):
            pending = {}
            for t in range(4):
                po = [psh(2, 0), psh(2, 1)]
                pob = [PSb[2][0], PSb[2][1]]

                def qk(kc, t=t):
                    pi = kc % 2
                    def f():
                        T_.matmul(psh(pi, 0), lhsT=KT[0:64, kc * 128:(kc + 1) * 128],
                                  rhs=QT[0:64, :, t * 128:(t + 1) * 128], start=True, stop=True)
                        return T_.matmul(psh(pi, 1), lhsT=KT[64:128, kc * 128:(kc + 1) * 128],
                                         rhs=QT[64:128, :, t * 128:(t + 1) * 128], start=True, stop=True)
                    kb.op(PE, f, reads=[bf("KT")] + [bf("QT%d" % j_) for j_ in range(4)], writes=[PSb[pi][0], PSb[pi][1]])

                def ex(kc):
                    pi = kc % 2
                    pe_ = kc % 3
                    kb.op(ACT, lambda: A_.activation(out=PTe[pe_][:, :], in_=PS[pi][:, :], func=AF.Exp, scale=0.125),
                          reads=[PSb[pi][0], PSb[pi][1]], writes=[bf("PTe%d" % pe_)])

                def pv(kc):
                    pe_ = kc % 3
                    def f():
                        T_.matmul(po[0], lhsT=VA[:, kc, 0:128], rhs=PTe[pe_][:, 0:512],
                                  start=(kc == 0), stop=(kc == NKC - 1))
                        return T_.matmul(po[1], lhsT=VA[:, kc, 64:192], rhs=PTe[pe_][:, 512:1024],
                                         start=(kc == 0), stop=(kc == NKC - 1))
                    kb.op(PE, f, reads=[bf("VA"), bf("VAones"), bf("PTe%d" % pe_)], writes=pob)
                def epi_parts(t=t):
                    o0, o1, rc, at, sq, rs = TT[0], TT[1], TT[2], TT[3], TT[4], TT[5]
                    pss = PS[3][:, 0:128]

                    def p1():
                        kb.op(DVE, lambda: V.reciprocal(out=rc[0:64, :], in_=o0[64:128, :]), reads=[TTb[0]], writes=[TTb[2]])
                        kb.op(DVE, lambda: V.reciprocal(out=rc[64:128, :], in_=o1[0:64, :]), reads=[TTb[1]], writes=[TTb[2]])
                        kb.op(DVE, lambda: V.tensor_tensor(out=at[0:64, :], in0=o0[0:64, :], in1=rc[0:64, :], op=ALU.mult),
                              reads=[TTb[0], TTb[2]], writes=[TTb[3]])
                        kb.op(DVE, lambda: V.tensor_tensor(out=at[64:128, :], in0=o1[64:128, :], in1=rc[64:128, :],
                                                           op=ALU.mult), reads=[TTb[1], TTb[2]], writes=[TTb[3]])

                    def p2():
                        kb.op(DVE, lambda: V.tensor_tensor(out=sq[:, :], in0=at[:, :], in1=at[:, :], op=ALU.mult),
                              reads=[TTb[3]], writes=[TTb[4]])

                    def p3():
                        def mss():
                            last = None
                            for j in range(4):
                                last = T_.matmul(pss, lhsT=o512, rhs=sq[:, j * 128:(j + 1) * 128],
                                                 start=(j == 0), stop=(j == 3))
                            return last
                        kb.op(PE, mss, reads=[TTb[4], bf("cmat")], writes=[PSb[3][0]])

                    def p4():
                        kb.op(ACT, lambda: A_.activation(out=rs[:, 0:128], in_=pss, func=AF.Ln, bias=epsT[:, 0:1]),
                              reads=[PSb[3][0], bf("epsT")], writes=[TTb[5]])
                        kb.op(ACT, lambda: A_.activation(out=rs[:, 0:128], in_=rs[:, 0:128], func=AF.Exp, scale=-0.5),
                              reads=[TTb[5]], writes=[TTb[5]])

                    def p5():
                        for j in range(4):
                            kb.op(DVE, lambda j=j: V.scalar_tensor_tensor(
                                out=mixT[:, j, t * 128:(t + 1) * 128], in0=at[:, j * 128:(j + 1) * 128],
                                scalar=ga[:, j:j + 1], in1=rs[:, 0:128], op0=ALU.mult, op1=ALU.mult),
                                reads=[TTb[3], TTb[5], bf("par")], writes=[bf("mixT")])
                    return {2: p1, 4: p2, 14: p3, 20: p4, 26: p5}

                qk(0)
                qk(1)
                for kc in range(NKC):
                    ex(kc)
                    if kc + 2 < NKC:
                        qk(kc + 2)
                    pv(kc)
                    if kc in pending:
                        pending[kc]()
                pending.clear()
                kb.op(DVE, lambda: V.tensor_copy(out=TT[0][:, :], in_=po[0]), reads=[pob[0]], writes=[TTb[0]])
                kb.op(DVE, lambda: V.tensor_copy(out=TT[1][:, :], in_=po[1]), reads=[pob[1]], writes=[TTb[1]])
                pending.update(epi_parts())
                if t == 3:
                    for k_ in sorted(pending):
                        pending[k_]()
                    pending.clear()

        def conv_stage(b):
            SC = [TT[0], TT[1], TT[2], TT[3]]
            it = 0
            for c in range(4):
                sl, slb = ws.acquire(U_CV + c)
                for s in range(2):
                    par_ = it % 2
                    it += 1
                    if par_ == 0:
                        pgb, pgbb = PS[0][:, 0:256], PSb[0][0]
                        pgc, pgcb = PS[0][:, 512:512 + 258], PSb[0][1]
                        puu, puub = PS[1][:, 0:258], PSb[1][0]
                        zu, zub, y, yb = TT[4], TTb[4], TT[5], TTb[5]
                    else:
                        pgb, pgbb = PS[2][:, 0:256], PSb[2][0]
                        pgc, pgcb = PS[2][:, 512:512 + 258], PSb[2][1]
                        puu, puub = PS[1][:, 512:512 + 258], PSb[1][1]
                        zu, zub, y, yb = TT[6], TTb[6], TT[7], TTb[7]

                    def mmc(sl=sl, s=s, pgb=pgb, pgc=pgc, puu=puu):
                        last = None
                        for kc in range(8):
                            T_.matmul(pgb, lhsT=sl[:, kc * 384:kc * 384 + 128],
                                      rhs=h2T[:, kc, 2 + s * 256:2 + s * 256 + 256], start=(kc == 0), stop=(kc == 7))
                        for kc in range(8):
                            T_.matmul(pgc, lhsT=sl[:, kc * 384 + 128:kc * 384 + 256],
                                      rhs=h2T[:, kc, 1 + s * 256:1 + s * 256 + 258], start=(kc == 0), stop=(kc == 7))
                        for kc in range(8):
                            last = T_.matmul(puu, lhsT=sl[:, kc * 384 + 256:kc * 384 + 384],
                                             rhs=h2T[:, kc, 1 + s * 256:1 + s * 256 + 258], start=(kc == 0), stop=(kc == 7))
                        return last
                    kb.op(PE, mmc, reads=[slb] + h2Tb, writes=[pgbb, pgcb, puub])
                    kb.op(ACT, lambda zu=zu, puu=puu: A_.activation(out=zu[:, 0:258], in_=puu, func=AF.Copy),
                          reads=[puub], writes=[zub])
                    kb.op(DVE, lambda zu=zu, pgc=pgc: V.tensor_tensor(out=zu[:, 0:258], in0=pgc, in1=zu[:, 0:258],
                                                                      op=ALU.mult),
                          reads=[pgcb, zub], writes=[zub])
                    kb.op(ACT, lambda c=c, zu=zu, y=y: A_.activation(out=y[:, 0:256], in_=zu[:, 1:257], func=AF.Copy,
                                                                     scale=cw[:, c * 3 + 1:c * 3 + 2]),
                          reads=[zub, bf("par")], writes=[yb])
                    kb.op(DVE, lambda c=c, zu=zu, y=y: V.scalar_tensor_tensor(
                        out=y[:, 0:256], in0=zu[:, 0:256], scalar=cw[:, c * 3:c * 3 + 1], in1=y[:, 0:256],
                        op0=ALU.mult, op1=ALU.add), reads=[zub, yb, bf("par")], writes=[yb])
                    kb.op(DVE, lambda c=c, zu=zu, y=y: V.scalar_tensor_tensor(
                        out=y[:, 0:256], in0=zu[:, 2:258], scalar=cw[:, c * 3 + 2:c * 3 + 3], in1=y[:, 0:256],
                        op0=ALU.mult, op1=ALU.add), reads=[zub, yb, bf("par")], writes=[yb])
                    kb.op(DVE, lambda c=c, s=s, y=y, pgb=pgb: V.tensor_tensor(
                        out=SC[c][:, s * 256:(s + 1) * 256], in0=pgb, in1=y[:, 0:256], op=ALU.mult),
                        reads=[pgbb, yb], writes=[TTb[c]])
                ws.done()
            pss = PS[3][:, 0:512]
            for c in range(4):
                sq, sqb = (TT[4], TTb[4]) if c % 2 == 0 else (TT[6], TTb[6])
                kb.op(ACT, lambda c=c, sq=sq: A_.activation(out=sq[:, :], in_=SC[c][:, :], func=AF.Square),
                      reads=[TTb[c]], writes=[sqb])
                kb.op(PE, lambda c=c, sq=sq: T_.matmul(pss, lhsT=o512, rhs=sq[:, :], start=(c == 0), stop=(c == 3)),
                      reads=[sqb, bf("cmat")], writes=[PSb[3][0]])
            rs = TT[5]
            kb.op(ACT, lambda: A_.activation(out=rs[:, :], in_=pss, func=AF.Ln, bias=epsT[:, 0:1]),
                  reads=[PSb[3][0], bf("epsT")], writes=[TTb[5]])
            kb.op(ACT, lambda: A_.activation(out=rs[:, :], in_=rs[:, :], func=AF.Exp, scale=-0.5),
                  reads=[TTb[5]], writes=[TTb[5]])
            for c in range(4):
                kb.op(DVE, lambda c=c: V.scalar_tensor_tensor(out=mixT[:, 4 + c, :], in0=SC[c][:, :],
                                                              scalar=gcv[:, c:c + 1], in1=rs[:, :],
                                                              op0=ALU.mult, op1=ALU.mult),
                      reads=[TTb[c], TTb[5], bf("par")], writes=[bf("mixT")])

        def mixout_stage(b, xi):
            sls = [ws.acquire(U_MO + o) for o in range(3)]
            k = 0
            for t in range(4):
                for h in range(2):
                    pi, ph = k % 4, 0
                    po = psh(pi, ph)

                    def mmo(t=t, h=h, po=po):
                        last = None
                        for kk in range(8):
                            sl = sls[kk // 3][0]
                            ki = kk % 3
                            last = T_.matmul(po, lhsT=mixT[:, kk, t * 128:(t + 1) * 128],
                                             rhs=sl[:, ki * 1024 + h * 512:ki * 1024 + (h + 1) * 512],
                                             start=(kk == 0), stop=(kk == 7))
                        return last
                    kb.op(PE, mmo, reads=[s_[1] for s_ in sls] + [bf("mixT")], writes=[PSb[pi][ph]])
                    tmp, tb = TT[k % 4], TTb[k % 4]
                    kb.op(DVE, lambda tmp=tmp, po=po, h=h: V.tensor_tensor(
                        out=tmp[:, :], in0=po, in1=Gt[2][:, h * 512:(h + 1) * 512], op=ALU.mult),
                        reads=[PSb[pi][ph], bf("G2")], writes=[tb])
                    kb.op(POOL, lambda tmp=tmp, t=t, h=h: G_.tensor_tensor(
                        out=XB[xi][:, t, h * 512:(h + 1) * 512], in0=XB[xi][:, t, h * 512:(h + 1) * 512],
                        in1=tmp[:, :], op=ALU.add),
                        reads=[tb, XBb[xi][t]], writes=[XBb[xi][t]])
                    k += 1
            ws.done(3)

        def final_gen(b, xi):
            for t in range(4):
                kb.op(ACT, lambda t=t: A_.activation(out=junk[:], in_=XB[xi][:, t, :], func=AF.Square,
                                                     scale=1.0 / 32.0, accum_out=stat[:, 8 + t:9 + t]),
                      reads=[XBb[xi][t]], writes=[bf("fss%d" % t)])
                yield
                kb.op(ACT, lambda t=t: A_.activation(out=stat[:, 8 + t:9 + t], in_=stat[:, 8 + t:9 + t], func=AF.Ln,
                                                     bias=epsT[:, 0:1]),
                      reads=[bf("fss%d" % t), bf("epsT")], writes=[bf("fss%d" % t)])
                yield
                kb.op(ACT, lambda t=t: A_.activation(out=stat[:, 12 + t:13 + t], in_=stat[:, 8 + t:9 + t], func=AF.Exp,
                                                     scale=-0.5),
                      reads=[bf("fss%d" % t)], writes=[bf("frs%d" % t)])
                yield
                kb.op(DVE, lambda t=t: V.scalar_tensor_tensor(out=XB[xi][:, t, :], in0=XB[xi][:, t, :],
                                                              scalar=stat[:, 12 + t:13 + t], in1=fnbc[:, :],
                                                              op0=ALU.mult, op1=ALU.mult),
                      reads=[XBb[xi][t], bf("frs%d" % t), bf("fnbc")], writes=[XBb[xi][t]])
                yield
            kb.dma(POOL, XB_st[xi], out_d[b * TB:(b + 1) * TB, :].rearrange("(t p) d -> p t d", p=128),
                   XB[xi][:, :, :], reads=XBb[xi])
            yield

        def load_blockB(b, xi):
            kb.dma(SP, XB_ld[xi], XB[xi][:, :, :], x1s_d[b * TB:(b + 1) * TB, :].rearrange("(t p) d -> p t d", p=128),
                   reads=[x1sb[b]], writes=XBb[xi])

        def load_h2T(b):
            lo = b * TB - 2
            hi = b * TB + TB + 2
            rd = [h2sb[bb] for bb in (b - 1, b, b + 1) if 0 <= bb < NB]
            if b == 0:
                kb.op(DVE, lambda: V.memset(h2T[:, :, 0:2], 0.0), writes=h2Tb)
                kb.dma(SP, h2_ld, h2T[:, :, 2:TB + 4], h2s_d[:, :, 0:hi], reads=rd, writes=h2Tb)
            elif b == NB - 1:
                kb.op(DVE, lambda: V.memset(h2T[:, :, TB + 2:TB + 4], 0.0), writes=h2Tb)
                kb.dma(SP, h2_ld, h2T[:, :, 0:TB + 2], h2s_d[:, :, lo:S], reads=rd, writes=h2Tb)
            else:
                kb.dma(SP, h2_ld, h2T[:, :, :], h2s_d[:, :, lo:hi], reads=rd, writes=h2Tb)

        def load_rope(b):
            kb.dma(SP, rope_ld, ropeT[:, :, :], rope_d[:, :, b * TB:(b + 1) * TB], writes=[bf("rope")])

        nbB = NB if stop_after is None else stop_after
        if limit < 4:
            nbB = 0
        else:
            load_blockB(0, 0)
            load_h2T(0)
            load_rope(0)
        pending_final = [None]
        for b in range(nbB):
            xi = b % 2
            if blimit >= 1:
                q_stage(b, extra=pending_final[0])
                pending_final[0] = None
            if b + 1 < nbB:
                load_blockB(b + 1, 1 - xi)
            if b + 1 < nbB:
                load_rope(b + 1)
            if blimit >= 2:
                attn_stage(b)
            if blimit >= 3:
                conv_stage(b)
            if b + 1 < nbB:
                load_h2T(b + 1)
            if blimit < 4:
                continue
            if debug and b == 0:
                for c in range(16):
                    tt = TT[6 + c % 2]
                    mf = mixT[:, :, :].rearrange("p a b -> p (a b)")
                    kb.op(DVE, lambda c=c, tt=tt, mf=mf: V.tensor_copy(out=tt[:, 0:256], in_=mf[:, c * 256:(c + 1) * 256]),
                          reads=[bf("mixT")], writes=[TTb[6 + c % 2]])
                    kb.dma(POOL, dbg_s2[c % 2], dbg_mix[:, c * 256:(c + 1) * 256], tt[:, 0:256], reads=[TTb[6 + c % 2]])
            mixout_stage(b, xi)
            if debug and b == 0:
                kb.dma(POOL, dbg_st, dbg_x2[:, :].rearrange("(t p) d -> p t d", p=128), XB[xi][:, :, :], reads=XBb[xi])
            if blimit >= 5:
                ffn(xi, 4, M_A3, M_B3, 3, U_F2)
            if blimit >= 6:
                pending_final[0] = final_gen(b, xi)
        if pending_final[0] is not None:
            for _ in pending_final[0]:
                pass

        for ds in XB_st + [dbg_st, misc_st, h2_st, wkv_ld] + dbg_s2 + ring_ss + ring_dc + stg_dc + stg_ss:
            if ds.count:
                POOL.wait((ds.sem, ds.count))
        for e in (PE, ACT, DVE, POOL):
            if e.n:
                SP.wait((e.sem, e.n))
        for ds in XB_ld + ring_ds + [h2_ld, rope_ld, misc_ld] + ada_ld:
            if ds.count:
                SP.wait((ds.sem, ds.count))
    return nc


def _consts():
    cm = np.zeros((128, 640), np.float32)
    cm[:, 0:128] = np.eye(128, dtype=np.float32)
    R = np.zeros((128, 128), np.float32)
    for i in range(64):
        R[2 * i + 1, 2 * i] = -1.0
        R[2 * i, 2 * i + 1] = 1.0
    cm[:, 128:256] = R
    bd = np.zeros((128, 128), np.float32)
    bd[0:64, 0:64] = 1.0 / 64.0
    bd[64:128, 64:128] = 1.0 / 64.0
    cm[:, 256:384] = bd
    cm[:, 384:512] = 1.0 / 512.0
    cm[:, 512:640] = 1.0
    GRID_W = 64
    rows = S // GRID_W
    row = np.repeat(np.arange(rows, dtype=np.float32), GRID_W)
    col = np.tile(np.arange(GRID_W, dtype=np.float32), rows)
    n_freq = 16
    inv = (np.float32(10000.0) ** (-np.arange(n_freq, dtype=np.float32) / np.float32(n_freq))).astype(np.float32)
    ang = np.concatenate([row[:, None] * inv, col[:, None] * inv], axis=-1).astype(np.float32)
    cos = np.cos(ang).astype(np.float32)
    sin = np.sin(ang).astype(np.float32)
    pi = (np.arange(128) % 64) // 2
    rope = np.stack([cos[:, pi].T, sin[:, pi].T], axis=1).astype(np.float32)
    return cm, np.ascontiguousarray(rope)


def _fm(v):
    return np.ascontiguousarray(np.asarray(v, np.float32).reshape(8, 128).T)


def make_in_maps(inputs, cores):
    cm, rope = _consts()
    par = np.zeros((128, NPAR), np.float32)
    par[:, 0:8] = _fm(inputs["norm_ffn1"][0])
    par[:, 8:16] = _fm(inputs["norm_mix"][0])
    par[:, 16:24] = _fm(inputs["norm_ffn2"][0])
    p = np.arange(128)
    par[:, 24] = np.asarray(inputs["q_norm"][0])[p % 64]
    par[:, 25] = np.asarray(inputs["k_norm"][0])[p % 64]
    aon = np.asarray(inputs["attn_out_norm"][0])
    for j in range(4):
        par[:, 26 + j] = aon[(j + 4 * (p // 64)) * 64 + (p % 64)]
    con = np.asarray(inputs["conv_out_norm"][0])
    cwv = np.asarray(inputs["conv_w"][0])
    for c in range(4):
        par[:, 30 + c] = con[c * 128 + p]
        for k in range(3):
            par[:, 34 + c * 3 + k] = cwv[k, c * 128 + p]
    par[:, 48:120] = np.asarray(inputs["b_ada"][0], np.float32).reshape(72, 128).T
    shared = {
        "w_ada": np.ascontiguousarray(inputs["w_ada"][0]),
        "b_ada": np.ascontiguousarray(np.asarray(inputs["b_ada"], np.float32).reshape(1, 9 * D)),
        "w1i": np.ascontiguousarray(inputs["w_ffn1_in"][0]),
        "w1o": np.ascontiguousarray(inputs["w_ffn1_out"][0]),
        "wmi": np.ascontiguousarray(inputs["w_mix_in"][0]),
        "wmo": np.ascontiguousarray(inputs["w_mix_out"][0]),
        "w2i": np.ascontiguousarray(inputs["w_ffn2_in"][0]),
        "w2o": np.ascontiguousarray(inputs["w_ffn2_out"][0]),
        "params": par,
        "fnorm": np.ascontiguousarray(np.asarray(inputs["final_norm"], np.float32).reshape(1, D)),
        "cmat": cm,
        "rope": rope,
    }
    maps = []
    cctx = np.asarray(inputs["c_ctx"], np.float32)
    for b in cores:
        cc = np.stack([_fm(inputs["c"][b]), _fm(cctx)], axis=1)
        m = dict(shared)
        m["x"] = np.ascontiguousarray(inputs["x"][b])
        m["ctx"] = np.ascontiguousarray(inputs["ctx"][b])
        m["cc"] = np.ascontiguousarray(cc)
        maps.append(m)
    return maps


_NC_CACHE = {}


def kernel(**inputs):
    inputs = {k: np.asarray(v) for k, v in inputs.items()}
    if "nc" not in _NC_CACHE:
        _NC_CACHE["nc"] = build_program()
    nc = _NC_CACHE["nc"]
    maps = make_in_maps(inputs, list(range(8)))
    res = run_bass_kernel_spmd(nc, maps, core_ids=list(range(8)))
    out = np.stack([np.asarray(r["out"], np.float32) for r in res.results], axis=0)
    return out
```

```python
import numpy as np
from contextlib import ExitStack
import concourse.bass as bass
import concourse.mybir as mybir
from concourse.bass_utils import run_bass_kernel_spmd

F32 = mybir.dt.float32
BF16 = mybir.dt.bfloat16
AF = mybir.ActivationFunctionType
ALU = mybir.AluOpType

D = 1024
S = 4096
CTX = 256
DFF = 2816
NF = 22
EPS = 1e-6
TB = 512
NB = S // TB
NKC = 34
NS = 5
SLOT = 3072
SEM_LIMIT = 30000

U_F1 = 0
U_F2 = 30
U_Q = 60
U_CV = 62
U_MO = 66
NU = 69

NPAR = 48 + 72


class Buf:
    __slots__ = ("name", "w", "rs", "psum")

    def __init__(self, name="", psum=False):
        self.name = name
        self.w = None
        self.rs = []
        self.psum = psum


class DSem:
    def __init__(self, sem):
        self.sem = sem
        self.count = 0


class Eng:
    def __init__(self, name, h, sems, is_pe=False):
        self.name = name
        self.h = h
        self.sems = list(sems)
        self.sem = self.sems.pop(0)
        self.n = 0
        self.known = {}
        self.is_pe = is_pe
        self.skip_waw = False
        self.mine = set([id(self.sem)])

    def wait(self, ev):
        if ev is None:
            return
        sem, val = ev
        k = id(sem)
        if self.known.get(k, 0) >= val:
            return
        self.h.wait_ge(sem, val)
        self.known[k] = val

    def tick(self, inst):
        if self.n >= SEM_LIMIT:
            self.sem = self.sems.pop(0)
            self.mine.add(id(self.sem))
            self.n = 0
        self.n += 1
        inst.then_inc(self.sem, 1)
        return (self.sem, self.n)


class KB:
    def __init__(self, nc, mk_sem):
        self.nc = nc
        self.pe = Eng("pe", nc.tensor, [mk_sem("pe%d" % i) for i in range(2)], is_pe=True)
        self.act = Eng("act", nc.scalar, [mk_sem("act%d" % i) for i in range(2)])
        self.dve = Eng("dve", nc.vector, [mk_sem("dve%d" % i) for i in range(3)])
        self.pool = Eng("pool", nc.gpsimd, [mk_sem("pool%d" % i) for i in range(2)])
        self.sp = Eng("sp", nc.sync, [mk_sem("sp%d" % i) for i in range(1)])
        self.act.skip_waw = True

    def _deps(self, eng, reads, writes):
        def need(ev):
            if ev is None:
                return
            if eng.is_pe and id(ev[0]) in eng.mine:
                return
            eng.wait(ev)
        for b in reads:
            need(b.w)
            if b.psum:
                for ev in b.rs:
                    if id(ev[0]) not in eng.mine:
                        need(ev)
        for b in writes:
            if not (eng.skip_waw and b.w is not None and id(b.w[0]) in eng.mine):
                need(b.w)
            for ev in b.rs:
                need(ev)

    def _commit(self, ev, reads, writes):
        for b in reads:
            b.rs.append(ev)
            if len(b.rs) > 48:
                best = {}
                for s, v in b.rs:
                    if id(s) not in best or best[id(s)][1] < v:
                        best[id(s)] = (s, v)
                b.rs = list(best.values())
        for b in writes:
            b.w = ev
            b.rs = []

    def op(self, eng, fn, reads=(), writes=()):
        self._deps(eng, reads, writes)
        inst = fn()
        ev = eng.tick(inst)
        self._commit(ev, reads, writes)
        return ev

    def dma(self, eng, dsem, out, in_, reads=(), writes=(), **kw):
        self._deps(eng, reads, writes)
        inst = eng.h.dma_start(out=out, in_=in_, **kw)
        dsem.count += 16
        inst.then_inc(dsem.sem, 16)
        ev = (dsem.sem, dsem.count)
        self._commit(ev, reads, writes)
        return ev


QDBG = [None]
NFILL = [0]
PFLIM = [10 ** 9]


def build_program(debug=False, stop_after=None, limit=4, blimit=9):
    nc = bass.Bass("TRN2", target_bir_lowering=False)

    def din(name, shape, dt=F32):
        return nc.dram_tensor(name, list(shape), dt, kind="ExternalInput").ap()

    x_d = din("x", [S, D])
    ctx_d = din("ctx", [CTX, D])
    cc_d = din("cc", [128, 2, 8])
    wada_d = din("w_ada", [D, 9 * D])
    bada_d = din("b_ada", [1, 9 * D])
    w1i_d = din("w1i", [D, 2 * DFF])
    w1o_d = din("w1o", [DFF, D])
    wmi_d = din("wmi", [D, 2304])
    wmo_d = din("wmo", [D, D])
    w2i_d = din("w2i", [D, 2 * DFF])
    w2o_d = din("w2o", [DFF, D])
    par_d = din("params", [128, NPAR])
    fn_d = din("fnorm", [1, D])
    cm_d = din("cmat", [128, 640])
    rope_d = din("rope", [128, 2, S])
    out_d = nc.dram_tensor("out", [S, D], F32, kind="ExternalOutput").ap()

    skind = "ExternalOutput" if debug else "Internal"
    wscr_d = nc.dram_tensor("wscr", [NU, 128, SLOT], BF16, kind="Internal").ap()
    x1s_d = nc.dram_tensor("x1s", [S, D], F32, kind=skind).ap()
    h2s_d = nc.dram_tensor("h2s", [128, 8, S], BF16, kind="Internal").ap()
    if debug:
        dbg_kt = nc.dram_tensor("dbg_kt", [128, NKC * 128], F32, kind="ExternalOutput").ap()
        dbg_va = nc.dram_tensor("dbg_va", [128, NKC * 192], F32, kind="ExternalOutput").ap()
        dbg_mt = nc.dram_tensor("dbg_mt", [128, 144], F32, kind="ExternalOutput").ap()
        dbg_g = nc.dram_tensor("dbg_g", [128, 4 * D], F32, kind="ExternalOutput").ap()
        dbg_mix = nc.dram_tensor("dbg_mix", [128, 8 * TB], F32, kind="ExternalOutput").ap()
        dbg_x2 = nc.dram_tensor("dbg_x2", [TB, D], F32, kind="ExternalOutput").ap()

    with ExitStack() as es:
        def sb(name, shape, dt):
            return es.enter_context(nc.sbuf_tensor("sb_" + name, list(shape), dt))

        def mk_sem(name):
            return es.enter_context(nc.semaphore("sem_" + name))

        kb = KB(nc, mk_sem)
        PE, ACT, DVE, POOL, SP = kb.pe, kb.act, kb.dve, kb.pool, kb.sp
        V = nc.vector
        A_ = nc.scalar
        T_ = nc.tensor
        G_ = nc.gpsimd

        KT = sb("KT", [128, NKC * 128], BF16)
        VA = sb("VA", [128, NKC, 192], BF16)
        Wkv = sb("Wkv", [128, 8, 256], BF16)
        cmat = sb("cmat", [128, 640], F32)
        identb = sb("identb", [128, 128], BF16)
        par = sb("par", [128, NPAR], F32)
        mods = sb("mods", [128, 10, 8], F32)
        mT = sb("mT", [128, 72, 2], F32)
        Gt = [sb("G%d" % i, [128, D], F32) for i in range(4)]
        fnbc = sb("fnbc", [128, D], F32)
        ring = [sb("ring%d" % i, [128, SLOT], BF16) for i in range(NS)]
        XB = [sb("XB%d" % i, [128, 4, D], F32) for i in range(2)]
        xn = sb("xn", [128, 4, D], BF16)
        hT = sb("hT", [128, 8, TB], BF16)
        h2T = sb("h2T", [128, 8, TB + 4], BF16)
        actT = sb("actT", [128, NF, TB], BF16)
        TT = [sb("T%d" % i, [128, 512], F32) for i in range(8)]
        ropeT = sb("ropeT", [128, 2, TB], F32)
        QT = sb("QT", [128, 4, TB], BF16)
        PTe = [sb("PTe%d" % i, [128, 1024], BF16) for i in range(3)]
        mixT = sb("mixT", [128, 8, TB], BF16)
        junk = sb("junk", [128, D], BF16)
        stg1 = sb("stg1", [128, SLOT], BF16)
        stat = sb("stat", [128, 16], F32)
        Sada = sb("Sada", [128, 8, 33], F32)
        cct = sb("cct", [128, 2, 8], F32)
        epsT = sb("epsT", [128, 1], F32)

        PS = [es.enter_context(nc.psum_tensor("PS%d" % i, [128, 1024], F32)) for i in range(4)]
        PSb = [[Buf("PS%d_%d" % (i, h), psum=True) for h in range(2)] for i in range(4)]

        def psh(i, h, n=512):
            return PS[i][:, h * 512:h * 512 + n]

        B = {}

        def bf(name):
            if name not in B:
                B[name] = Buf(name)
            return B[name]

        ringb = [Buf("ring%d" % i) for i in range(NS)]
        ring_ds = [DSem(mk_sem("ringl%d" % i)) for i in range(NS)]
        ring_dc = [DSem(mk_sem("ringc%d" % i)) for i in range(NS)]
        wkv_ld = DSem(mk_sem("wkvl"))
        stg_dc = [DSem(mk_sem("stgc%d" % i)) for i in range(2)]
        stg_ss = [DSem(mk_sem("stgs%d" % i)) for i in range(2)]
        ring_ss = [DSem(mk_sem("rings%d" % i)) for i in range(NS)]
        scrb = [Buf("scr%d" % u) for u in range(NU)]
        XBb = [[Buf("XB%d_%d" % (i, t)) for t in range(4)] for i in range(2)]
        XB_ld = [DSem(mk_sem("xbl%d" % i)) for i in range(2)]
        XB_st = [DSem(mk_sem("xbs%d" % i)) for i in range(2)]
        TTb = [Buf("T%d" % i) for i in range(8)]
        misc_ld = DSem(mk_sem("miscl"))
        misc_st = DSem(mk_sem("miscs"))
        ada_ld = [DSem(mk_sem("adal%d" % i)) for i in range(2)]
        h2_ld = DSem(mk_sem("h2l"))
        h2_st = DSem(mk_sem("h2s"))
        rope_ld = DSem(mk_sem("ropel"))
        dbg_st = DSem(mk_sem("dbgs"))
        dbg_s2 = [DSem(mk_sem("dbgs2_%d" % i)) for i in range(2)]
        x1sb = [Buf("x1s%d" % b) for b in range(NB)]
        h2sb = [Buf("h2s%d" % b) for b in range(NB)]

        ident_f = cmat[:, 0:128]
        Rm = cmat[:, 128:256]
        bd64 = cmat[:, 256:384]
        o512 = cmat[:, 384:512]
        ones_m = cmat[:, 512:640]

        g1 = par[:, 0:8]
        gm = par[:, 8:16]
        g2 = par[:, 16:24]
        qg = par[:, 24:25]
        kg = par[:, 25:26]
        ga = par[:, 26:30]
        gcv = par[:, 30:34]
        cw = par[:, 34:46]
        bT = par[:, 48:120]

        M_A1, M_B1, M_A1c, M_B1c, M_A2, M_B2, M_A2c, M_B2c, M_A3, M_B3 = range(10)

        def mod(i):
            return mods[:, i, :]

        kb.dma(SP, misc_ld, par[:], par_d[:, :], writes=[bf("par")])
        kb.dma(SP, misc_ld, cmat[:], cm_d[:, :], writes=[bf("cmat")])
        kb.dma(SP, misc_ld, cct[:], cc_d[:, :, :], writes=[bf("cct")])
        kb.dma(SP, misc_ld, fnbc[:], fn_d.partition_broadcast(128), writes=[bf("fnbc")])
        GSPEC = ((0, 2, 0.5, "G0"), (1, 2, 0.5, "G1c"), (2, 5, 1.0, "G2"), (3, 8, 0.5, "G3"))
        for gi, v, sc, nm in GSPEC:
            kb.dma(SP, misc_ld, Gt[gi][:], bada_d[0:1, v * D:(v + 1) * D].partition_broadcast(128), writes=[bf(nm)])
        for gi, v, sc, nm in GSPEC:
            bf(nm).w = (misc_ld.sem, misc_ld.count)
        for nm in ("par", "cmat", "cct", "fnbc"):
            bf(nm).w = (misc_ld.sem, misc_ld.count)

        kb.op(DVE, lambda: V.tensor_copy(out=identb[:], in_=ident_f), reads=[bf("cmat")], writes=[bf("identb")])
        kb.op(DVE, lambda: V.memset(Sada[:], 0.0), writes=[bf("Sada")])
        kb.op(DVE, lambda: V.memset(VA[:, :, 64:128], 1.0), writes=[bf("VAones")])
        kb.op(DVE, lambda: V.memset(stat[:], 0.0), writes=[bf("stat")])
        kb.op(DVE, lambda: V.memset(epsT[:], EPS), writes=[bf("epsT")])
        kb.op(ACT, lambda: A_.activation(out=Sada[:, :, 0], in_=cct[:, 0, :], func=AF.Silu),
              reads=[bf("cct")], writes=[bf("Sada")])
        kb.op(ACT, lambda: A_.activation(out=Sada[:, :, 32], in_=cct[:, 1, :], func=AF.Silu),
              reads=[bf("cct")], writes=[bf("Sada")])

        kb.dma(POOL, wkv_ld, Wkv[:, :, :],
               wmi_d[:, 512:768].rearrange("(kc p) c -> p kc c", p=128), writes=[bf("Wkv")])

        def cast_unit(u, sl, slb, dsem):
            w = [slb]

            def cd(out, in_):
                kb.dma(POOL, dsem, out, in_, writes=w)
            if u < U_Q:
                wi, wo = (w1i_d, w1o_d) if u < U_F2 else (w2i_d, w2o_d)
                r = u - (U_F1 if u < U_F2 else U_F2)
                if r < NF:
                    f = r
                    slv = sl[:, 0:2048].rearrange("p (k c) -> p k c", c=256)
                    cd(slv[:, :, 0:128], wi[:, f * 128:(f + 1) * 128].rearrange("(kc p) c -> p kc c", p=128))
                    cd(slv[:, :, 128:256],
                       wi[:, DFF + f * 128:DFF + (f + 1) * 128].rearrange("(kc p) c -> p kc c", p=128))
                else:
                    o = r - NF
                    for fi in range(3):
                        f = o * 3 + fi
                        if f < NF:
                            cd(sl[:, fi * 1024:(fi + 1) * 1024], wo[f * 128:(f + 1) * 128, :])
            elif u < U_CV:
                i = u - U_Q
                slv = sl[:, 0:2048].rearrange("p (k c) -> p k c", c=256)
                for jj in range(2):
                    j = i * 2 + jj
                    cd(slv[:, :, jj * 128:jj * 128 + 64],
                       wmi_d[:, j * 64:(j + 1) * 64].rearrange("(kc p) c -> p kc c", p=128))
                    cd(slv[:, :, jj * 128 + 64:jj * 128 + 128],
                       wmi_d[:, (j + 4) * 64:(j + 5) * 64].rearrange("(kc p) c -> p kc c", p=128))
            elif u < U_MO:
                c = u - U_CV
                slv = sl[:, 0:3072].rearrange("p (k c) -> p k c", c=384)
                for gi_, off in enumerate((768, 1280, 1792)):
                    cd(slv[:, :, gi_ * 128:(gi_ + 1) * 128],
                       wmi_d[:, off + c * 128:off + (c + 1) * 128].rearrange("(kc p) c -> p kc c", p=128))
            else:
                o = u - U_MO
                for ki in range(3):
                    k = o * 3 + ki
                    if k >= 8:
                        continue
                    dst = sl[:, ki * 1024:(ki + 1) * 1024]
                    if k < 4:
                        cd(dst[0:64, :], wmo_d[k * 64:(k + 1) * 64, :])
                        cd(dst[64:128, :], wmo_d[(k + 4) * 64:(k + 5) * 64, :])
                    else:
                        cd(dst, wmo_d[512 + (k - 4) * 128:512 + (k - 3) * 128, :])

        def usize(u):
            if u < U_Q:
                r = u - (U_F1 if u < U_F2 else U_F2)
                if r < NF:
                    return 2048
                return 1024 * min(3, NF - (r - NF) * 3)
            if u < U_CV:
                return 2048
            if u < U_MO:
                return 3072
            return 1024 * min(3, 8 - (u - U_MO) * 3)

        def store_unit(u, sl, slb, dsem):
            n = usize(u)
            kb.dma(POOL, dsem, wscr_d[u, :, 0:n], sl[:, 0:n], reads=[slb], writes=[scrb[u]])

        stg = [mixT[:, :, :].rearrange("p a b -> p (a b)"), stg1[:, :]]
        stgb = [bf("mixT"), Buf("stg1")]
        bg_state = [U_F2, 0]

        def bg_cast(n_units):
            for _ in range(n_units):
                u = bg_state[0]
                if u >= NU:
                    return
                k = bg_state[1] % 2
                cast_unit(u, stg[k], stgb[k], stg_dc[k])
                store_unit(u, stg[k], stgb[k], stg_ss[k])
                bg_state[0] += 1
                bg_state[1] += 1

        NPRE = U_F2
        pre_u = [0]
        st_u = [0]

        def precast_issue(n):
            for _ in range(n):
                if pre_u[0] < NPRE:
                    u = pre_u[0]
                    cast_unit(u, ring[u % NS], ringb[u % NS], ring_dc[u % NS])
                    pre_u[0] += 1

        def precast_store(n):
            for _ in range(n):
                if st_u[0] < pre_u[0]:
                    u = st_u[0]
                    n_ = usize(u)
                    kb.dma(SP, ring_ss[u % NS], wscr_d[u, :, 0:n_], ring[u % NS][:, 0:n_],
                           reads=[ringb[u % NS]], writes=[scrb[u]])
                    st_u[0] += 1

        precast_issue(NS)

        for gi, v, sc, nm in GSPEC:
            if sc != 1.0:
                kb.op(DVE, lambda gi=gi, sc=sc: V.tensor_scalar(out=Gt[gi][:], in0=Gt[gi][:], scalar1=sc, scalar2=None,
                                                                op0=ALU.mult), reads=[bf(nm)], writes=[bf(nm)])
        PMT = PS[1][:, 0:144]
        ada_first = True
        for q in range(18):
            xb = XB[q % 2]
            xbv = xb[:, :, :].rearrange("p a (b c) -> p (a b) c", c=512)
            kb.dma(SP, ada_ld[q % 2], xbv,
                   wada_d[:, q * 512:(q + 1) * 512].rearrange("(kc p) c -> p kc c", p=128),
                   writes=XBb[q % 2])
            precast_store(2)
            precast_issue(2)
            pm = PS[0][0:33, 0:512]

            def mm_ada(xbv=xbv, pm=pm):
                last = None
                for kc in range(8):
                    last = T_.matmul(pm, lhsT=Sada[:, kc, :], rhs=xbv[:, kc, :], start=(kc == 0), stop=(kc == 7))
                return last
            kb.op(PE, mm_ada, reads=XBb[q % 2] + [bf("Sada")], writes=[PSb[0][0]])
            mrow = TT[q % 2]
            kb.op(ACT, lambda mrow=mrow, pm=pm: A_.activation(out=mrow[0:33, :], in_=pm, func=AF.Copy),
                  reads=[PSb[0][0]], writes=[TTb[q % 2]])

            def mm_tr(mrow=mrow, q=q):
                last = None
                for jj in range(4):
                    j = q * 4 + jj
                    for w in range(2):
                        last = T_.matmul(PS[1][:, j * 2 + w:j * 2 + w + 1],
                                         lhsT=mrow[32 * w:32 * w + 1, jj * 128:(jj + 1) * 128],
                                         rhs=ones_m[32 * w:32 * w + 1, 0:1], start=True, stop=True)
                return last
            kb.op(PE, mm_tr, reads=[TTb[q % 2], bf("cmat")], writes=[PSb[1][0]])
            v = q // 2
            if v in (2, 5, 8):
                gi = {2: 0, 5: 2, 8: 3}[v]
                sc = 1.0 if v == 5 else 0.5
                half = q % 2
                pb = psh(2, 0)
                kb.op(PE, lambda mrow=mrow, pb=pb: T_.matmul(pb, lhsT=ones_m[0:1, :], rhs=mrow[0:1, :],
                                                          start=True, stop=True),
                      reads=[TTb[q % 2], bf("cmat")], writes=[PSb[2][0]])
                kb.op(DVE, lambda pb=pb, gi=gi, half=half, sc=sc: V.scalar_tensor_tensor(
                    out=Gt[gi][:, half * 512:(half + 1) * 512], in0=pb, scalar=sc,
                    in1=Gt[gi][:, half * 512:(half + 1) * 512], op0=ALU.mult, op1=ALU.add),
                    reads=[PSb[2][0], bf("G%d" % gi)], writes=[bf("G%d" % gi)])
                if v == 2:
                    pb2 = psh(2, 1)
                    kb.op(PE, lambda mrow=mrow, pb2=pb2: T_.matmul(pb2, lhsT=ones_m[32:33, :], rhs=mrow[32:33, :],
                                                                start=True, stop=True),
                          reads=[TTb[q % 2], bf("cmat")], writes=[PSb[2][1]])
                    kb.op(DVE, lambda pb2=pb2, half=half: V.scalar_tensor_tensor(
                        out=Gt[1][:, half * 512:(half + 1) * 512], in0=pb2, scalar=0.5,
                        in1=Gt[1][:, half * 512:(half + 1) * 512], op0=ALU.mult, op1=ALU.add),
                        reads=[PSb[2][1], bf("G1c")], writes=[bf("G1c")])
        while st_u[0] < NPRE:
            precast_store(1)
            precast_issue(1)
        kb.op(DVE, lambda: V.tensor_tensor(out=mT[:, :, 0], in0=PS[1][:, 0:144:2], in1=bT, op=ALU.add),
              reads=[PSb[1][0], bf("par")], writes=[bf("mT")])
        kb.op(DVE, lambda: V.tensor_tensor(out=mT[:, :, 1], in0=PS[1][:, 1:144:2], in1=bT, op=ALU.add),
              reads=[PSb[1][0], bf("par")], writes=[bf("mT")])

        def mk_mod(ai, bi, vs, vsh, g, w):
            kb.op(DVE, lambda: V.scalar_tensor_tensor(out=mod(ai), in0=mT[:, vs * 8:(vs + 1) * 8, w], scalar=1.0,
                                                      in1=g, op0=ALU.add, op1=ALU.mult),
                  reads=[bf("mT"), bf("par")], writes=[bf("mods")])
            kb.op(DVE, lambda: V.tensor_copy(out=mod(bi), in_=mT[:, vsh * 8:(vsh + 1) * 8, w]),
                  reads=[bf("mT")], writes=[bf("mods")])
        mk_mod(M_A1, M_B1, 1, 0, g1, 0)
        mk_mod(M_A1c, M_B1c, 1, 0, g1, 1)
        mk_mod(M_A2, M_B2, 4, 3, gm, 0)
        mk_mod(M_A2c, M_B2c, 4, 3, gm, 1)
        mk_mod(M_A3, M_B3, 7, 6, g2, 0)

        if debug:
            kb.dma(POOL, dbg_st, dbg_mt[:, :], mT[:, :, :].rearrange("p a b -> p (a b)"), reads=[bf("mT")])
            for i in range(4):
                kb.dma(POOL, dbg_st, dbg_g[:, i * D:(i + 1) * D], Gt[i][:], reads=[bf("G0"), bf("G1c"), bf("G2"), bf("G3")])

        class WStream:
            def __init__(self):
                self.sched = []
                self.nl = 0
                self.nu = 0
                self.rel = 0

            def done(self, n=1):
                self.rel += n

            def prefetch(self, upto):
                while self.nl < min(upto, len(self.sched)):
                    i = self.nl
                    u = self.sched[i]
                    s = i % NS
                    n = usize(u)
                    kb.dma(SP, ring_ds[s], ring[s][:, 0:n], wscr_d[u, :, 0:n], reads=[scrb[u]], writes=[ringb[s]])
                    self.nl += 1

            def acquire(self, u):
                i = self.nu
                assert self.sched[i] == u, (i, u, self.sched[i])
                assert i < self.rel + NS
                self.prefetch(min(self.rel + NS, PFLIM[0]))
                self.nu += 1
                return ring[i % NS], ringb[i % NS]

        ws = WStream()
        ffn1_units = list(range(U_F1, U_F1 + 30))
        ffn2_units = list(range(U_F2, U_F2 + 30))
        ws.sched += ffn1_units
        for b in range(NB):
            ws.sched += ffn1_units + ffn1_units[NF:]
        for b in range(NB):
            ws.sched += ([U_Q, U_Q + 1] + list(range(U_CV, U_CV + 4)) + list(range(U_MO, U_MO + 3))
                         + ffn2_units + ffn2_units[NF:])

        def rms_to_T(xi, NT, Ai, Bi, dstT, col0, dstb, part=0):
            TBk = NT * 128
            for t in range(NT if part != 2 else 0):
                kb.op(ACT, lambda t=t: A_.activation(out=junk[:], in_=XB[xi][:, t, :], func=AF.Square,
                                                     scale=1.0 / 32.0, accum_out=stat[:, t:t + 1]),
                      reads=[XBb[xi][t]], writes=[bf("ss%d" % t)])
                kb.op(ACT, lambda t=t: A_.activation(out=stat[:, t:t + 1], in_=stat[:, t:t + 1], func=AF.Ln,
                                                     bias=epsT[:, 0:1]),
                      reads=[bf("ss%d" % t), bf("epsT")], writes=[bf("ss%d" % t)])
                kb.op(ACT, lambda t=t: A_.activation(out=stat[:, 4 + t:5 + t], in_=stat[:, t:t + 1], func=AF.Exp,
                                                     scale=-0.5),
                      reads=[bf("ss%d" % t)], writes=[bf("rstd%d" % t)])
                if t % 2 == 0:
                    kb.op(ACT, lambda t=t: A_.activation(out=xn[:, t, :], in_=XB[xi][:, t, :], func=AF.Copy,
                                                         scale=stat[:, 4 + t:5 + t]),
                          reads=[XBb[xi][t], bf("rstd%d" % t)], writes=[bf("xn%d" % t)])
                else:
                    kb.op(DVE, lambda t=t: V.tensor_scalar(out=xn[:, t, :], in0=XB[xi][:, t, :],
                                                           scalar1=stat[:, 4 + t:5 + t], scalar2=None, op0=ALU.mult),
                          reads=[XBb[xi][t], bf("rstd%d" % t)], writes=[bf("xn%d" % t)])
            for kc in range(8 if part != 1 else 0):
                pi, ph = 2 + kc // 4, (kc % 4) // 2
                ptv = PS[pi].bitcast(BF16)
                base = (kc % 4) * 512

                def tr(kc=kc, ptv=ptv, base=base):
                    last = None
                    for t in range(NT):
                        last = T_.transpose(out=ptv[:, base + t * 128:base + (t + 1) * 128],
                                            in_=xn[:, t, kc * 128:(kc + 1) * 128], identity=identb[:])
                    return last
                kb.op(PE, tr, reads=[bf("xn%d" % t) for t in range(NT)] + [bf("identb")], writes=[PSb[pi][ph]])
                src = ptv[:, base:base + TBk]
                dst = dstT[:, kc, col0:col0 + TBk]
                if kc % 2 == 0:
                    kb.op(ACT, lambda src=src, dst=dst, kc=kc: A_.activation(
                        out=dst, in_=src, func=AF.Identity, scale=mod(Ai)[:, kc:kc + 1], bias=mod(Bi)[:, kc:kc + 1]),
                        reads=[PSb[pi][ph], bf("mods")], writes=[dstb[kc]])
                else:
                    kb.op(DVE, lambda src=src, dst=dst, kc=kc: V.tensor_scalar(
                        out=dst, in0=src, scalar1=mod(Ai)[:, kc:kc + 1], scalar2=mod(Bi)[:, kc:kc + 1],
                        op0=ALU.mult, op1=ALU.add),
                        reads=[PSb[pi][ph], bf("mods")], writes=[dstb[kc]])

        hTb = [bf("hT%d" % kc) for kc in range(8)]
        h2Tb = [bf("h2T%d" % kc) for kc in range(8)]

        def ffn(xi, NT, Ai, Bi, gi, ubase, do_norm=True, on_pool=True, between=None):
            TBk = NT * 128
            if do_norm:
                rms_to_T(xi, NT, Ai, Bi, hT, 0, hTb)
            for f in range(NF):
                sl, slb = ws.acquire(ubase + f)
                pg = PS[f % 2][:, 0:TBk]
                pu = PS[f % 2][:, 512:512 + TBk]

                def mm1(sl=sl, pg=pg, pu=pu):
                    last = None
                    for kc in range(8):
                        T_.matmul(pg, lhsT=sl[:, kc * 256:kc * 256 + 128], rhs=hT[:, kc, 0:TBk],
                                  start=(kc == 0), stop=(kc == 7))
                    for kc in range(8):
                        last = T_.matmul(pu, lhsT=sl[:, kc * 256 + 128:kc * 256 + 256], rhs=hT[:, kc, 0:TBk],
                                         start=(kc == 0), stop=(kc == 7))
                    return last
                kb.op(PE, mm1, reads=[slb] + hTb, writes=[PSb[f % 2][0], PSb[f % 2][1]])
                sg = TT[f % 2]
                kb.op(ACT, lambda sg=sg, pg=pg: A_.activation(out=sg[:, 0:TBk], in_=pg, func=AF.Silu),
                      reads=[PSb[f % 2][0]], writes=[TTb[f % 2]])
                kb.op(DVE, lambda sg=sg, pu=pu, f=f: V.tensor_tensor(out=actT[:, f, 0:TBk], in0=sg[:, 0:TBk], in1=pu,
                                                                     op=ALU.mult),
                      reads=[TTb[f % 2], PSb[f % 2][1]], writes=[bf("act%d" % f)])
                ws.done()
            if between is not None:
                between()
            k = 0
            for tp in range(max(1, NT // 2)):
                tls = [t for t in (2 * tp, 2 * tp + 1) if t < NT]
                accb = [PSb[t][h] for t in tls for h in range(2)]
                for o in range(8):
                    sl, slb = ws.acquire(ubase + NF + o)
                    fs = [o * 3 + fi for fi in range(3) if o * 3 + fi < NF]

                    def mm2(sl=sl, fs=fs, tls=tls):
                        last = None
                        for fi, f in enumerate(fs):
                            for t in tls:
                                for h in range(2):
                                    last = T_.matmul(psh(t, h), lhsT=actT[:, f, t * 128:(t + 1) * 128],
                                                     rhs=sl[:, fi * 1024 + h * 512:fi * 1024 + (h + 1) * 512],
                                                     start=(f == 0), stop=(f == NF - 1))
                        return last
                    kb.op(PE, mm2, reads=[slb] + [bf("act%d" % f) for f in fs], writes=accb)
                    ws.done()
                for t in tls:
                    for h in range(2):
                        tmp = TT[2 + k % 4]
                        tb = TTb[2 + k % 4]
                        kb.op(DVE, lambda tmp=tmp, t=t, h=h: V.tensor_tensor(
                            out=tmp[:, :], in0=psh(t, h), in1=Gt[gi][:, h * 512:(h + 1) * 512], op=ALU.mult),
                            reads=[PSb[t][h], bf("G%s" % ("1c" if gi == 1 else str(gi)))], writes=[tb])
                        if on_pool:
                            kb.op(POOL, lambda tmp=tmp, t=t, h=h: G_.tensor_tensor(
                                out=XB[xi][:, t, h * 512:(h + 1) * 512], in0=XB[xi][:, t, h * 512:(h + 1) * 512],
                                in1=tmp[:, :], op=ALU.add),
                                reads=[tb, XBb[xi][t]], writes=[XBb[xi][t]])
                        else:
                            kb.op(DVE, lambda tmp=tmp, t=t, h=h: V.tensor_tensor(
                                out=XB[xi][:, t, h * 512:(h + 1) * 512], in0=XB[xi][:, t, h * 512:(h + 1) * 512],
                                in1=tmp[:, :], op=ALU.add),
                                reads=[tb, XBb[xi][t]], writes=[XBb[xi][t]])
                        k += 1

        HN_SETS = [
            dict(raw=2, sq=3, kn=4, t1=5, ps=3),
            dict(raw=6, sq=7, kn=0, t1=1, ps=1),
        ]

        def headnorm_gen(psrc, psb, n, gsc, dst, dstb, with_rope, on_pool=True, tset=0):
            ts = HN_SETS[tset]
            raw, rawb = TT[ts["raw"]], TTb[ts["raw"]]
            sq, sqb = TT[ts["sq"]], TTb[ts["sq"]]
            kn, knb = TT[ts["kn"]], TTb[ts["kn"]]
            t1, t1b = TT[ts["t1"]], TTb[ts["t1"]]
            pi = ts["ps"]
            pss, pssb = PS[pi][:, 0:n], PSb[pi][0]
            prot, protb = PS[pi][:, 512:512 + n], PSb[pi][1]
            kb.op(DVE, lambda: V.tensor_copy(out=raw[:, 0:n], in_=psrc), reads=[psb], writes=[rawb])
            yield
            kb.op(ACT, lambda: A_.activation(out=sq[:, 0:n], in_=psrc, func=AF.Square), reads=[psb], writes=[sqb])
            yield
            kb.op(PE, lambda: T_.matmul(pss, lhsT=bd64, rhs=sq[:, 0:n], start=True, stop=True),
                  reads=[sqb, bf("cmat")], writes=[pssb])
            yield
            kb.op(ACT, lambda: A_.activation(out=sq[:, 0:n], in_=pss, func=AF.Ln, bias=epsT[:, 0:1]),
                  reads=[pssb, bf("epsT")], writes=[sqb])
            yield
            kb.op(ACT, lambda: A_.activation(out=sq[:, 0:n], in_=sq[:, 0:n], func=AF.Exp, scale=-0.5),
                  reads=[sqb], writes=[sqb])
            yield
            if not with_rope:
                kb.op(DVE, lambda: V.scalar_tensor_tensor(out=dst, in0=raw[:, 0:n], scalar=gsc, in1=sq[:, 0:n],
                                                          op0=ALU.mult, op1=ALU.mult),
                      reads=[rawb, sqb, bf("par")], writes=[dstb])
                yield
                return
            kb.op(DVE, lambda: V.scalar_tensor_tensor(out=kn[:, 0:n], in0=raw[:, 0:n], scalar=gsc, in1=sq[:, 0:n],
                                                      op0=ALU.mult, op1=ALU.mult),
                  reads=[rawb, sqb, bf("par")], writes=[knb])
            yield
            kb.op(PE, lambda: T_.matmul(prot, lhsT=Rm, rhs=kn[:, 0:n], start=True, stop=True),
                  reads=[knb, bf("cmat")], writes=[protb])
            yield
            if on_pool:
                kb.op(POOL, lambda: G_.tensor_tensor(out=t1[:, 0:n], in0=kn[:, 0:n], in1=ropeT[:, 0, 0:n], op=ALU.mult),
                      reads=[knb, bf("rope")], writes=[t1b])
            else:
                kb.op(DVE, lambda: V.tensor_tensor(out=t1[:, 0:n], in0=kn[:, 0:n], in1=ropeT[:, 0, 0:n], op=ALU.mult),
                      reads=[knb, bf("rope")], writes=[t1b])
            yield
            kb.op(DVE, lambda: V.tensor_tensor(out=raw[:, 0:n], in0=prot, in1=ropeT[:, 1, 0:n], op=ALU.mult),
                  reads=[protb, bf("rope")], writes=[rawb])
            yield
            kb.op(DVE, lambda: V.tensor_tensor(out=dst, in0=t1[:, 0:n], in1=raw[:, 0:n], op=ALU.add),
                  reads=[t1b, rawb], writes=[dstb])
            yield

        def headnorm_rope(psrc, psb, n, gsc, dst, dstb, with_rope, on_pool=True):
            for _ in headnorm_gen(psrc, psb, n, gsc, dst, dstb, with_rope, on_pool=on_pool, tset=0):
                pass

        def kv_stage(NT, kc0, with_rope):
            TBk = NT * 128
            pk = PS[0][:, 0:TBk]

            def mmk():
                last = None
                for kc in range(8):
                    last = T_.matmul(pk, lhsT=Wkv[:, kc, 0:128], rhs=h2T[:, kc, 2:2 + TBk],
                                     start=(kc == 0), stop=(kc == 7))
                return last
            kb.op(PE, mmk, reads=[bf("Wkv")] + h2Tb, writes=[PSb[0][0]])
            pv = PS[0][:, 512:512 + TBk]

            def mmv():
                last = None
                for t in range(NT):
                    for kc in range(8):
                        last = T_.matmul(pv[:, t * 128:(t + 1) * 128], lhsT=h2T[:, kc, 2 + t * 128:2 + (t + 1) * 128],
                                         rhs=Wkv[:, kc, 128:256], start=(kc == 0), stop=(kc == 7))
                return last
            kb.op(PE, mmv, reads=[bf("Wkv")] + h2Tb, writes=[PSb[0][1]])
            pv3 = pv.rearrange("p (t c) -> p t c", c=128)
            kb.op(ACT, lambda: A_.activation(out=VA[:, kc0:kc0 + NT, 0:64], in_=pv3[:, :, 0:64], func=AF.Copy),
                  reads=[PSb[0][1]], writes=[bf("VA")])
            kb.op(DVE, lambda: V.tensor_copy(out=VA[:, kc0:kc0 + NT, 128:192], in_=pv3[:, :, 64:128]),
                  reads=[PSb[0][1]], writes=[bf("VA")])
            headnorm_rope(pk, PSb[0][0], TBk, kg, KT[:, kc0 * 128:kc0 * 128 + TBk], bf("KT"), with_rope, on_pool=False)

        def load_x(src_ap, xi, NT, b_reads=()):
            kb.dma(SP, XB_ld[xi], XB[xi][:, 0:NT, :], src_ap.rearrange("(t p) d -> p t d", p=128),
                   reads=list(b_reads), writes=XBb[xi][0:NT])

        if limit >= 2:
            load_x(ctx_d[:, :], 0, 2)
            load_x(x_d[0:TB, :], 1, 4)
            rms_to_T(0, 2, M_A1c, M_B1c, hT, 0, hTb)
            ffn(0, 2, M_A1c, M_B1c, 1, U_F1, do_norm=False, on_pool=False,
                between=(lambda: rms_to_T(1, 4, M_A1, M_B1, hT, 0, hTb, part=1)) if limit >= 3 else None)
            if limit >= 3:
                rms_to_T(1, 4, M_A1, M_B1, hT, 0, hTb, part=2)
            rms_to_T(0, 2, M_A2c, M_B2c, h2T, 2, h2Tb)
            kv_stage(2, 0, False)
        for b in range(NB if limit >= 3 else 0):
            xi = (b + 1) % 2
            if b + 1 < NB:
                load_x(x_d[(b + 1) * TB:(b + 2) * TB, :], 1 - xi, 4)
            kb.dma(SP, rope_ld, ropeT[:, :, :], rope_d[:, :, b * TB:(b + 1) * TB], writes=[bf("rope")])
            if b < 4:
                bg_cast(10)
            nxt = (lambda xi=xi: rms_to_T(1 - xi, 4, M_A1, M_B1, hT, 0, hTb, part=1)) if b + 1 < NB else None
            ffn(xi, 4, M_A1, M_B1, 0, U_F1, do_norm=False, on_pool=False, between=nxt)
            kb.dma(POOL, XB_st[xi], x1s_d[b * TB:(b + 1) * TB, :].rearrange("(t p) d -> p t d", p=128),
                   XB[xi][:, :, :], reads=XBb[xi], writes=[x1sb[b]])
            if b + 1 < NB:
                rms_to_T(1 - xi, 4, M_A1, M_B1, hT, 0, hTb, part=2)
            rms_to_T(xi, 4, M_A2, M_B2, h2T, 2, h2Tb)
            kb.dma(POOL, h2_st, h2s_d[:, :, b * TB:(b + 1) * TB], h2T[:, :, 2:2 + TB], reads=h2Tb, writes=[h2sb[b]])
            kv_stage(4, 2 + b * 4, True)

        for b in range(NB):
            xi = (b + 1) % 2
            x1sb[b].w = (XB_st[xi].sem, XB_st[xi].count)
            h2sb[b].w = (h2_st.sem, h2_st.count)
        if debug:
            for c in range(NKC * 128 // 256):
                tt = TT[c % 2]
                kb.op(DVE, lambda c=c, tt=tt: V.tensor_copy(out=tt[:, 0:256], in_=KT[:, c * 256:(c + 1) * 256]),
                      reads=[bf("KT")], writes=[TTb[c % 2]])
                kb.dma(POOL, dbg_s2[c % 2], dbg_kt[:, c * 256:(c + 1) * 256], tt[:, 0:256], reads=[TTb[c % 2]])
            vaf = VA[:, :, :].rearrange("p a b -> p (a b)")
            for c in range(NKC * 192 // 384):
                tt = TT[c % 2]
                kb.op(DVE, lambda c=c, tt=tt: V.tensor_copy(out=tt[:, 0:384], in_=vaf[:, c * 384:(c + 1) * 384]),
                      reads=[bf("VA"), bf("VAones")], writes=[TTb[c % 2]])
                kb.dma(POOL, dbg_s2[c % 2], dbg_va[:, c * 384:(c + 1) * 384], tt[:, 0:384], reads=[TTb[c % 2]])

        def q_stage(b, extra=None):
            for i in range(2):
                sl, slb = ws.acquire(U_Q + i)
                gens = []
                for jj in range(2):
                    j = i * 2 + jj
                    pq = PS[0][:, jj * 512:jj * 512 + TB]

                    def mmq(sl=sl, jj=jj, pq=pq):
                        last = None
                        for kc in range(8):
                            last = T_.matmul(pq, lhsT=sl[:, kc * 256 + jj * 128:kc * 256 + (jj + 1) * 128],
                                             rhs=h2T[:, kc, 2:2 + TB], start=(kc == 0), stop=(kc == 7))
                        return last
                    kb.op(PE, mmq, reads=[slb] + h2Tb, writes=[PSb[0][jj]])
                    gens.append(headnorm_gen(pq, PSb[0][jj], TB, qg, QT[:, j, :], bf("QT%d" % j), True, tset=jj))
                if extra is not None and i == 0:
                    gens.append(extra)
                live = list(gens)
                while live:
                    for g in list(live):
                        try:
                            next(g)
                        except StopIteration:
                            live.remove(g)
                ws.done()

        def attn_stage(b):
            pending = {}
            for t in range(4):
                po = [psh(2, 0), psh(2, 1)]
                pob = [PSb[2][0], PSb[2][1]]

                def qk(kc, t=t):
                    pi = kc % 2
                    def f():
                        T_.matmul(psh(pi, 0), lhsT=KT[0:64, kc * 128:(kc + 1) * 128],
                                  rhs=QT[0:64, :, t * 128:(t + 1) * 128], start=True, stop=True)
                        return T_.matmul(psh(pi, 1), lhsT=KT[64:128, kc * 128:(kc + 1) * 128],
                                         rhs=QT[64:128, :, t * 128:(t + 1) * 128], start=True, stop=True)
                    kb.op(PE, f, reads=[bf("KT")] + [bf("QT%d" % j_) for j_ in range(4)], writes=[PSb[pi][0], PSb[pi][1]])

                def ex(kc):
                    pi = kc % 2
                    pe_ = kc % 3
                    kb.op(ACT, lambda: A_.activation(out=PTe[pe_][:, :], in_=PS[pi][:, :], func=AF.Exp, scale=0.125),
                          reads=[PSb[pi][0], PSb[pi][1]], writes=[bf("PTe%d" % pe_)])

                def pv(kc):
                    pe_ = kc % 3
                    def f():
                        T_.matmul(po[0], lhsT=VA[:, kc, 0:128], rhs=PTe[pe_][:, 0:512],
                                  start=(kc == 0), stop=(kc == NKC - 1))
                        return T_.matmul(po[1], lhsT=VA[:, kc, 64:192], rhs=PTe[pe_][:, 512:1024],
                                         start=(kc == 0), stop=(kc == NKC - 1))
                    kb.op(PE, f, reads=[bf("VA"), bf("VAones"), bf("PTe%d" % pe_)], writes=pob)
                def epi_parts(t=t):
                    o0, o1, rc, at, sq, rs = TT[0], TT[1], TT[2], TT[3], TT[4], TT[5]
                    pss = PS[3][:, 0:128]

                    def p1():
                        kb.op(DVE, lambda: V.reciprocal(out=rc[0:64, :], in_=o0[64:128, :]), reads=[TTb[0]], writes=[TTb[2]])
                        kb.op(DVE, lambda: V.reciprocal(out=rc[64:128, :], in_=o1[0:64, :]), reads=[TTb[1]], writes=[TTb[2]])
                        kb.op(DVE, lambda: V.tensor_tensor(out=at[0:64, :], in0=o0[0:64, :], in1=rc[0:64, :], op=ALU.mult),
                              reads=[TTb[0], TTb[2]], writes=[TTb[3]])
                        kb.op(DVE, lambda: V.tensor_tensor(out=at[64:128, :], in0=o1[64:128, :], in1=rc[64:128, :],
                                                           op=ALU.mult), reads=[TTb[1], TTb[2]], writes=[TTb[3]])

                    def p2():
                        kb.op(DVE, lambda: V.tensor_tensor(out=sq[:, :], in0=at[:, :], in1=at[:, :], op=ALU.mult),
                              reads=[TTb[3]], writes=[TTb[4]])

                    def p3():
                        def mss():
                            last = None
                            for j in range(4):
                                last = T_.matmul(pss, lhsT=o512, rhs=sq[:, j * 128:(j + 1) * 128],
                                                 start=(j == 0), stop=(j == 3))
                            return last
                        kb.op(PE, mss, reads=[TTb[4], bf("cmat")], writes=[PSb[3][0]])

                    def p4():
                        kb.op(ACT, lambda: A_.activation(out=rs[:, 0:128], in_=pss, func=AF.Ln, bias=epsT[:, 0:1]),
                              reads=[PSb[3][0], bf("epsT")], writes=[TTb[5]])
                        kb.op(ACT, lambda: A_.activation(out=rs[:, 0:128], in_=rs[:, 0:128], func=AF.Exp, scale=-0.5),
                              reads=[TTb[5]], writes=[TTb[5]])

                    def p5():
                        for j in range(4):
                            kb.op(DVE, lambda j=j: V.scalar_tensor_tensor(
                                out=mixT[:, j, t * 128:(t + 1) * 128], in0=at[:, j * 128:(j + 1) * 128],
                                scalar=ga[:, j:j + 1], in1=rs[:, 0:128], op0=ALU.mult, op1=ALU.mult),
                                reads=[TTb[3], TTb[5], bf("par")], writes=[bf("mixT")])
                    return {2: p1, 4: p2, 14: p3, 20: p4, 26: p5}

                qk(0)
                qk(1)
                for kc in range(NKC):
                    ex(kc)
                    if kc + 2 < NKC:
                        qk(kc + 2)
                    pv(kc)
                    if kc in pending:
                        pending[kc]()
                pending.clear()
                kb.op(DVE, lambda: V.tensor_copy(out=TT[0][:, :], in_=po[0]), reads=[pob[0]], writes=[TTb[0]])
                kb.op(DVE, lambda: V.tensor_copy(out=TT[1][:, :], in_=po[1]), reads=[pob[1]], writes=[TTb[1]])
                pending.update(epi_parts())
                if t == 3:
                    for k_ in sorted(pending):
                        pending[k_]()
                    pending.clear()

        def conv_stage(b):
            SC = [TT[0], TT[1], TT[2], TT[3]]
            it = 0
            for c in range(4):
                sl, slb = ws.acquire(U_CV + c)
                for s in range(2):
                    par_ = it % 2
                    it += 1
                    if par_ == 0:
                        pgb, pgbb = PS[0][:, 0:256], PSb[0][0]
                        pgc, pgcb = PS[0][:, 512:512 + 258], PSb[0][1]
                        puu, puub = PS[1][:, 0:258], PSb[1][0]
                        zu, zub, y, yb = TT[4], TTb[4], TT[5], TTb[5]
                    else:
                        pgb, pgbb = PS[2][:, 0:256], PSb[2][0]
                        pgc, pgcb = PS[2][:, 512:512 + 258], PSb[2][1]
                        puu, puub = PS[1][:, 512:512 + 258], PSb[1][1]
                        zu, zub, y, yb = TT[6], TTb[6], TT[7], TTb[7]

                    def mmc(sl=sl, s=s, pgb=pgb, pgc=pgc, puu=puu):
                        last = None
                        for kc in range(8):
                            T_.matmul(pgb, lhsT=sl[:, kc * 384:kc * 384 + 128],
                                      rhs=h2T[:, kc, 2 + s * 256:2 + s * 256 + 256], start=(kc == 0), stop=(kc == 7))
                        for kc in range(8):
                            T_.matmul(pgc, lhsT=sl[:, kc * 384 + 128:kc * 384 + 256],
                                      rhs=h2T[:, kc, 1 + s * 256:1 + s * 256 + 258], start=(kc == 0), stop=(kc == 7))
                        for kc in range(8):
                            last = T_.matmul(puu, lhsT=sl[:, kc * 384 + 256:kc * 384 + 384],
                                             rhs=h2T[:, kc, 1 + s * 256:1 + s * 256 + 258], start=(kc == 0), stop=(kc == 7))
                        return last
                    kb.op(PE, mmc, reads=[slb] + h2Tb, writes=[pgbb, pgcb, puub])
                    kb.op(ACT, lambda zu=zu, puu=puu: A_.activation(out=zu[:, 0:258], in_=puu, func=AF.Copy),
                          reads=[puub], writes=[zub])
                    kb.op(DVE, lambda zu=zu, pgc=pgc: V.tensor_tensor(out=zu[:, 0:258], in0=pgc, in1=zu[:, 0:258],
                                                                      op=ALU.mult),
                          reads=[pgcb, zub], writes=[zub])
                    kb.op(ACT, lambda c=c, zu=zu, y=y: A_.activation(out=y[:, 0:256], in_=zu[:, 1:257], func=AF.Copy,
                                                                     scale=cw[:, c * 3 + 1:c * 3 + 2]),
                          reads=[zub, bf("par")], writes=[yb])
                    kb.op(DVE, lambda c=c, zu=zu, y=y: V.scalar_tensor_tensor(
                        out=y[:, 0:256], in0=zu[:, 0:256], scalar=cw[:, c * 3:c * 3 + 1], in1=y[:, 0:256],
                        op0=ALU.mult, op1=ALU.add), reads=[zub, yb, bf("par")], writes=[yb])
                    kb.op(DVE, lambda c=c, zu=zu, y=y: V.scalar_tensor_tensor(
                        out=y[:, 0:256], in0=zu[:, 2:258], scalar=cw[:, c * 3 + 2:c * 3 + 3], in1=y[:, 0:256],
                        op0=ALU.mult, op1=ALU.add), reads=[zub, yb, bf("par")], writes=[yb])
                    kb.op(DVE, lambda c=c, s=s, y=y, pgb=pgb: V.tensor_tensor(
                        out=SC[c][:, s * 256:(s + 1) * 256], in0=pgb, in1=y[:, 0:256], op=ALU.mult),
                        reads=[pgbb, yb], writes=[TTb[c]])
                ws.done()
            pss = PS[3][:, 0:512]
            for c in range(4):
                sq, sqb = (TT[4], TTb[4]) if c % 2 == 0 else (TT[6], TTb[6])
                kb.op(ACT, lambda c=c, sq=sq: A_.activation(out=sq[:, :], in_=SC[c][:, :], func=AF.Square),
                      reads=[TTb[c]], writes=[sqb])
                kb.op(PE, lambda c=c, sq=sq: T_.matmul(pss, lhsT=o512, rhs=sq[:, :], start=(c == 0), stop=(c == 3)),
                      reads=[sqb, bf("cmat")], writes=[PSb[3][0]])
            rs = TT[5]
            kb.op(ACT, lambda: A_.activation(out=rs[:, :], in_=pss, func=AF.Ln, bias=epsT[:, 0:1]),
                  reads=[PSb[3][0], bf("epsT")], writes=[TTb[5]])
            kb.op(ACT, lambda: A_.activation(out=rs[:, :], in_=rs[:, :], func=AF.Exp, scale=-0.5),
                  reads=[TTb[5]], writes=[TTb[5]])
            for c in range(4):
                kb.op(DVE, lambda c=c: V.scalar_tensor_tensor(out=mixT[:, 4 + c, :], in0=SC[c][:, :],
                                                              scalar=gcv[:, c:c + 1], in1=rs[:, :],
                                                              op0=ALU.mult, op1=ALU.mult),
                      reads=[TTb[c], TTb[5], bf("par")], writes=[bf("mixT")])

        def mixout_stage(b, xi):
            sls = [ws.acquire(U_MO + o) for o in range(3)]
            k = 0
            for t in range(4):
                for h in range(2):
                    pi, ph = k % 4, 0
                    po = psh(pi, ph)

                    def mmo(t=t, h=h, po=po):
                        last = None
                        for kk in range(8):
                            sl = sls[kk // 3][0]
                            ki = kk % 3
                            last = T_.matmul(po, lhsT=mixT[:, kk, t * 128:(t + 1) * 128],
                                             rhs=sl[:, ki * 1024 + h * 512:ki * 1024 + (h + 1) * 512],
                                             start=(kk == 0), stop=(kk == 7))
                        return last
                    kb.op(PE, mmo, reads=[s_[1] for s_ in sls] + [bf("mixT")], writes=[PSb[pi][ph]])
                    tmp, tb = TT[k % 4], TTb[k % 4]
                    kb.op(DVE, lambda tmp=tmp, po=po, h=h: V.tensor_tensor(
                        out=tmp[:, :], in0=po, in1=Gt[2][:, h * 512:(h + 1) * 512], op=ALU.mult),
                        reads=[PSb[pi][ph], bf("G2")], writes=[tb])
                    kb.op(POOL, lambda tmp=tmp, t=t, h=h: G_.tensor_tensor(
                        out=XB[xi][:, t, h * 512:(h + 1) * 512], in0=XB[xi][:, t, h * 512:(h + 1) * 512],
                        in1=tmp[:, :], op=ALU.add),
                        reads=[tb, XBb[xi][t]], writes=[XBb[xi][t]])
                    k += 1
            ws.done(3)

        def final_gen(b, xi):
            for t in range(4):
                kb.op(ACT, lambda t=t: A_.activation(out=junk[:], in_=XB[xi][:, t, :], func=AF.Square,
                                                     scale=1.0 / 32.0, accum_out=stat[:, 8 + t:9 + t]),
                      reads=[XBb[xi][t]], writes=[bf("fss%d" % t)])
                yield
                kb.op(ACT, lambda t=t: A_.activation(out=stat[:, 8 + t:9 + t], in_=stat[:, 8 + t:9 + t], func=AF.Ln,
                                                     bias=epsT[:, 0:1]),
                      reads=[bf("fss%d" % t), bf("epsT")], writes=[bf("fss%d" % t)])
                yield
                kb.op(ACT, lambda t=t: A_.activation(out=stat[:, 12 + t:13 + t], in_=stat[:, 8 + t:9 + t], func=AF.Exp,
                                                     scale=-0.5),
                      reads=[bf("fss%d" % t)], writes=[bf("frs%d" % t)])
                yield
                kb.op(DVE, lambda t=t: V.scalar_tensor_tensor(out=XB[xi][:, t, :], in0=XB[xi][:, t, :],
                                                              scalar=stat[:, 12 + t:13 + t], in1=fnbc[:, :],
                                                              op0=ALU.mult, op1=ALU.mult),
                      reads=[XBb[xi][t], bf("frs%d" % t), bf("fnbc")], writes=[XBb[xi][t]])
                yield
            kb.dma(POOL, XB_st[xi], out_d[b * TB:(b + 1) * TB, :].rearrange("(t p) d -> p t d", p=128),
                   XB[xi][:, :, :], reads=XBb[xi])
            yield

        def load_blockB(b, xi):
            kb.dma(SP, XB_ld[xi], XB[xi][:, :, :], x1s_d[b * TB:(b + 1) * TB, :].rearrange("(t p) d -> p t d", p=128),
                   reads=[x1sb[b]], writes=XBb[xi])

        def load_h2T(b):
            lo = b * TB - 2
            hi = b * TB + TB + 2
            rd = [h2sb[bb] for bb in (b - 1, b, b + 1) if 0 <= bb < NB]
            if b == 0:
                kb.op(DVE, lambda: V.memset(h2T[:, :, 0:2], 0.0), writes=h2Tb)
                kb.dma(SP, h2_ld, h2T[:, :, 2:TB + 4], h2s_d[:, :, 0:hi], reads=rd, writes=h2Tb)
            elif b == NB - 1:
                kb.op(DVE, lambda: V.memset(h2T[:, :, TB + 2:TB + 4], 0.0), writes=h2Tb)
                kb.dma(SP, h2_ld, h2T[:, :, 0:TB + 2], h2s_d[:, :, lo:S], reads=rd, writes=h2Tb)
            else:
                kb.dma(SP, h2_ld, h2T[:, :, :], h2s_d[:, :, lo:hi], reads=rd, writes=h2Tb)

        def load_rope(b):
            kb.dma(SP, rope_ld, ropeT[:, :, :], rope_d[:, :, b * TB:(b + 1) * TB], writes=[bf("rope")])

        nbB = NB if stop_after is None else stop_after
        if limit < 4:
            nbB = 0
        else:
            load_blockB(0, 0)
            load_h2T(0)
            load_rope(0)
        pending_final = [None]
        for b in range(nbB):
            xi = b % 2
            if blimit >= 1:
                q_stage(b, extra=pending_final[0])
                pending_final[0] = None
            if b + 1 < nbB:
                load_blockB(b + 1, 1 - xi)
            if b + 1 < nbB:
                load_rope(b + 1)
            if blimit >= 2:
                attn_stage(b)
            if blimit >= 3:
                conv_stage(b)
            if b + 1 < nbB:
                load_h2T(b + 1)
            if blimit < 4:
                continue
            if debug and b == 0:
                for c in range(16):
                    tt = TT[6 + c % 2]
                    mf = mixT[:, :, :].rearrange("p a b -> p (a b)")
                    kb.op(DVE, lambda c=c, tt=tt, mf=mf: V.tensor_copy(out=tt[:, 0:256], in_=mf[:, c * 256:(c + 1) * 256]),
                          reads=[bf("mixT")], writes=[TTb[6 + c % 2]])
                    kb.dma(POOL, dbg_s2[c % 2], dbg_mix[:, c * 256:(c + 1) * 256], tt[:, 0:256], reads=[TTb[6 + c % 2]])
            mixout_stage(b, xi)
            if debug and b == 0:
                kb.dma(POOL, dbg_st, dbg_x2[:, :].rearrange("(t p) d -> p t d", p=128), XB[xi][:, :, :], reads=XBb[xi])
            if blimit >= 5:
                ffn(xi, 4, M_A3, M_B3, 3, U_F2)
            if blimit >= 6:
                pending_final[0] = final_gen(b, xi)
        if pending_final[0] is not None:
            for _ in pending_final[0]:
                pass

        for ds in XB_st + [dbg_st, misc_st, h2_st, wkv_ld] + dbg_s2 + ring_ss + ring_dc + stg_dc + stg_ss:
            if ds.count:
                POOL.wait((ds.sem, ds.count))
        for e in (PE, ACT, DVE, POOL):
            if e.n:
                SP.wait((e.sem, e.n))
        for ds in XB_ld + ring_ds + [h2_ld, rope_ld, misc_ld] + ada_ld:
            if ds.count:
                SP.wait((ds.sem, ds.count))
    return nc


def _consts():
    cm = np.zeros((128, 640), np.float32)
    cm[:, 0:128] = np.eye(128, dtype=np.float32)
    R = np.zeros((128, 128), np.float32)
    for i in range(64):
        R[2 * i + 1, 2 * i] = -1.0
        R[2 * i, 2 * i + 1] = 1.0
    cm[:, 128:256] = R
    bd = np.zeros((128, 128), np.float32)
    bd[0:64, 0:64] = 1.0 / 64.0
    bd[64:128, 64:128] = 1.0 / 64.0
    cm[:, 256:384] = bd
    cm[:, 384:512] = 1.0 / 512.0
    cm[:, 512:640] = 1.0
    GRID_W = 64
    rows = S // GRID_W
    row = np.repeat(np.arange(rows, dtype=np.float32), GRID_W)
    col = np.tile(np.arange(GRID_W, dtype=np.float32), rows)
    n_freq = 16
    inv = (np.float32(10000.0) ** (-np.arange(n_freq, dtype=np.float32) / np.float32(n_freq))).astype(np.float32)
    ang = np.concatenate([row[:, None] * inv, col[:, None] * inv], axis=-1).astype(np.float32)
    cos = np.cos(ang).astype(np.float32)
    sin = np.sin(ang).astype(np.float32)
    pi = (np.arange(128) % 64) // 2
    rope = np.stack([cos[:, pi].T, sin[:, pi].T], axis=1).astype(np.float32)
    return cm, np.ascontiguousarray(rope)


def _fm(v):
    return np.ascontiguousarray(np.asarray(v, np.float32).reshape(8, 128).T)


def make_in_maps(inputs, cores):
    cm, rope = _consts()
    par = np.zeros((128, NPAR), np.float32)
    par[:, 0:8] = _fm(inputs["norm_ffn1"][0])
    par[:, 8:16] = _fm(inputs["norm_mix"][0])
    par[:, 16:24] = _fm(inputs["norm_ffn2"][0])
    p = np.arange(128)
    par[:, 24] = np.asarray(inputs["q_norm"][0])[p % 64]
    par[:, 25] = np.asarray(inputs["k_norm"][0])[p % 64]
    aon = np.asarray(inputs["attn_out_norm"][0])
    for j in range(4):
        par[:, 26 + j] = aon[(j + 4 * (p // 64)) * 64 + (p % 64)]
    con = np.asarray(inputs["conv_out_norm"][0])
    cwv = np.asarray(inputs["conv_w"][0])
    for c in range(4):
        par[:, 30 + c] = con[c * 128 + p]
        for k in range(3):
            par[:, 34 + c * 3 + k] = cwv[k, c * 128 + p]
    par[:, 48:120] = np.asarray(inputs["b_ada"][0], np.float32).reshape(72, 128).T
    shared = {
        "w_ada": np.ascontiguousarray(inputs["w_ada"][0]),
        "b_ada": np.ascontiguousarray(np.asarray(inputs["b_ada"], np.float32).reshape(1, 9 * D)),
        "w1i": np.ascontiguousarray(inputs["w_ffn1_in"][0]),
        "w1o": np.ascontiguousarray(inputs["w_ffn1_out"][0]),
        "wmi": np.ascontiguousarray(inputs["w_mix_in"][0]),
        "wmo": np.ascontiguousarray(inputs["w_mix_out"][0]),
        "w2i": np.ascontiguousarray(inputs["w_ffn2_in"][0]),
        "w2o": np.ascontiguousarray(inputs["w_ffn2_out"][0]),
        "params": par,
        "fnorm": np.ascontiguousarray(np.asarray(inputs["final_norm"], np.float32).reshape(1, D)),
        "cmat": cm,
        "rope": rope,
    }
    maps = []
    cctx = np.asarray(inputs["c_ctx"], np.float32)
    for b in cores:
        cc = np.stack([_fm(inputs["c"][b]), _fm(cctx)], axis=1)
        m = dict(shared)
        m["x"] = np.ascontiguousarray(inputs["x"][b])
        m["ctx"] = np.ascontiguousarray(inputs["ctx"][b])
        m["cc"] = np.ascontiguousarray(cc)
        maps.append(m)
    return maps


_NC_CACHE = {}


def kernel(**inputs):
    inputs = {k: np.asarray(v) for k, v in inputs.items()}
    if "nc" not in _NC_CACHE:
        _NC_CACHE["nc"] = build_program()
    nc = _NC_CACHE["nc"]
    maps = make_in_maps(inputs, list(range(8)))
    res = run_bass_kernel_spmd(nc, maps, core_ids=list(range(8)))
    out = np.stack([np.asarray(r["out"], np.float32) for r in res.results], axis=0)
    return out
```

```python
import numpy as np
from contextlib import ExitStack
import concourse.bass as bass
import concourse.mybir as mybir
from concourse.bass_utils import run_bass_kernel_spmd

F32 = mybir.dt.float32
BF16 = mybir.dt.bfloat16
AF = mybir.ActivationFunctionType
ALU = mybir.AluOpType

D = 1024
S = 4096
CTX = 256
DFF = 2816
NF = 22
EPS = 1e-6
TB = 512
NB = S // TB
NKC = 34
NS = 5
SLOT = 3072
SEM_LIMIT = 30000

U_F1 = 0
U_F2 = 30
U_Q = 60
U_CV = 62
U_MO = 66
NU = 69

NPAR = 48 + 72


class Buf:
    __slots__ = ("name", "w", "rs", "psum")

    def __init__(self, name="", psum=False):
        self.name = name
        self.w = None
        self.rs = []
        self.psum = psum


class DSem:
    def __init__(self, sem):
        self.sem = sem
        self.count = 0


class Eng:
    def __init__(self, name, h, sems, is_pe=False):
        self.name = name
        self.h = h
        self.sems = list(sems)
        self.sem = self.sems.pop(0)
        self.n = 0
        self.known = {}
        self.is_pe = is_pe
        self.skip_waw = False
        self.mine = set([id(self.sem)])

    def wait(self, ev):
        if ev is None:
            return
        sem, val = ev
        k = id(sem)
        if self.known.get(k, 0) >= val:
            return
        self.h.wait_ge(sem, val)
        self.known[k] = val

    def tick(self, inst):
        if self.n >= SEM_LIMIT:
            self.sem = self.sems.pop(0)
            self.mine.add(id(self.sem))
            self.n = 0
        self.n += 1
        inst.then_inc(self.sem, 1)
        return (self.sem, self.n)


class KB:
    def __init__(self, nc, mk_sem):
        self.nc = nc
        self.pe = Eng("pe", nc.tensor, [mk_sem("pe%d" % i) for i in range(2)], is_pe=True)
        self.act = Eng("act", nc.scalar, [mk_sem("act%d" % i) for i in range(2)])
        self.dve = Eng("dve", nc.vector, [mk_sem("dve%d" % i) for i in range(3)])
        self.pool = Eng("pool", nc.gpsimd, [mk_sem("pool%d" % i) for i in range(2)])
        self.sp = Eng("sp", nc.sync, [mk_sem("sp%d" % i) for i in range(1)])
        self.act.skip_waw = True

    def _deps(self, eng, reads, writes):
        def need(ev):
            if ev is None:
                return
            if eng.is_pe and id(ev[0]) in eng.mine:
                return
            eng.wait(ev)
        for b in reads:
            need(b.w)
            if b.psum:
                for ev in b.rs:
                    if id(ev[0]) not in eng.mine:
                        need(ev)
        for b in writes:
            if not (eng.skip_waw and b.w is not None and id(b.w[0]) in eng.mine):
                need(b.w)
            for ev in b.rs:
                need(ev)

    def _commit(self, ev, reads, writes):
        for b in reads:
            b.rs.append(ev)
            if len(b.rs) > 48:
                best = {}
                for s, v in b.rs:
                    if id(s) not in best or best[id(s)][1] < v:
                        best[id(s)] = (s, v)
                b.rs = list(best.values())
        for b in writes:
            b.w = ev
            b.rs = []

    def op(self, eng, fn, reads=(), writes=()):
        self._deps(eng, reads, writes)
        inst = fn()
        ev = eng.tick(inst)
        self._commit(ev, reads, writes)
        return ev

    def dma(self, eng, dsem, out, in_, reads=(), writes=(), **kw):
        self._deps(eng, reads, writes)
        inst = eng.h.dma_start(out=out, in_=in_, **kw)
        dsem.count += 16
        inst.then_inc(dsem.sem, 16)
        ev = (dsem.sem, dsem.count)
        self._commit(ev, reads, writes)
        return ev


QDBG = [None]
NFILL = [0]
PFLIM = [10 ** 9]


def build_program(debug=False, stop_after=None, limit=4, blimit=9):
    nc = bass.Bass("TRN2", target_bir_lowering=False)

    def din(name, shape, dt=F32):
        return nc.dram_tensor(name, list(shape), dt, kind="ExternalInput").ap()

    x_d = din("x", [S, D])
    ctx_d = din("ctx", [CTX, D])
    cc_d = din("cc", [128, 2, 8])
    wada_d = din("w_ada", [D, 9 * D])
    bada_d = din("b_ada", [1, 9 * D])
    w1i_d = din("w1i", [D, 2 * DFF])
    w1o_d = din("w1o", [DFF, D])
    wmi_d = din("wmi", [D, 2304])
    wmo_d = din("wmo", [D, D])
    w2i_d = din("w2i", [D, 2 * DFF])
    w2o_d = din("w2o", [DFF, D])
    par_d = din("params", [128, NPAR])
    fn_d = din("fnorm", [1, D])
    cm_d = din("cmat", [128, 640])
    rope_d = din("rope", [128, 2, S])
    out_d = nc.dram_tensor("out", [S, D], F32, kind="ExternalOutput").ap()

    skind = "ExternalOutput" if debug else "Internal"
    wscr_d = nc.dram_tensor("wscr", [NU, 128, SLOT], BF16, kind="Internal").ap()
    x1s_d = nc.dram_tensor("x1s", [S, D], F32, kind=skind).ap()
    h2s_d = nc.dram_tensor("h2s", [128, 8, S], BF16, kind="Internal").ap()
    if debug:
        dbg_kt = nc.dram_tensor("dbg_kt", [128, NKC * 128], F32, kind="ExternalOutput").ap()
        dbg_va = nc.dram_tensor("dbg_va", [128, NKC * 192], F32, kind="ExternalOutput").ap()
        dbg_mt = nc.dram_tensor("dbg_mt", [128, 144], F32, kind="ExternalOutput").ap()
        dbg_g = nc.dram_tensor("dbg_g", [128, 4 * D], F32, kind="ExternalOutput").ap()
        dbg_mix = nc.dram_tensor("dbg_mix", [128, 8 * TB], F32, kind="ExternalOutput").ap()
        dbg_x2 = nc.dram_tensor("dbg_x2", [TB, D], F32, kind="ExternalOutput").ap()

    with ExitStack() as es:
        def sb(name, shape, dt):
            return es.enter_context(nc.sbuf_tensor("sb_" + name, list(shape), dt))

        def mk_sem(name):
            return es.enter_context(nc.semaphore("sem_" + name))

        kb = KB(nc, mk_sem)
        PE, ACT, DVE, POOL, SP = kb.pe, kb.act, kb.dve, kb.pool, kb.sp
        V = nc.vector
        A_ = nc.scalar
        T_ = nc.tensor
        G_ = nc.gpsimd

        KT = sb("KT", [128, NKC * 128], BF16)
        VA = sb("VA", [128, NKC, 192], BF16)
        Wkv = sb("Wkv", [128, 8, 256], BF16)
        cmat = sb("cmat", [128, 640], F32)
        identb = sb("identb", [128, 128], BF16)
        par = sb("par", [128, NPAR], F32)
        mods = sb("mods", [128, 10, 8], F32)
        mT = sb("mT", [128, 72, 2], F32)
        Gt = [sb("G%d" % i, [128, D], F32) for i in range(4)]
        fnbc = sb("fnbc", [128, D], F32)
        ring = [sb("ring%d" % i, [128, SLOT], BF16) for i in range(NS)]
        XB = [sb("XB%d" % i, [128, 4, D], F32) for i in range(2)]
        xn = sb("xn", [128, 4, D], BF16)
        hT = sb("hT", [128, 8, TB], BF16)
        h2T = sb("h2T", [128, 8, TB + 4], BF16)
        actT = sb("actT", [128, NF, TB], BF16)
        TT = [sb("T%d" % i, [128, 512], F32) for i in range(8)]
        ropeT = sb("ropeT", [128, 2, TB], F32)
        QT = sb("QT", [128, 4, TB], BF16)
        PTe = [sb("PTe%d" % i, [128, 1024], BF16) for i in range(3)]
        mixT = sb("mixT", [128, 8, TB], BF16)
        junk = sb("junk", [128, D], BF16)
        stg1 = sb("stg1", [128, SLOT], BF16)
        stat = sb("stat", [128, 16], F32)
        Sada = sb("Sada", [128, 8, 33], F32)
        cct = sb("cct", [128, 2, 8], F32)
        epsT = sb("epsT", [128, 1], F32)

        PS = [es.enter_context(nc.psum_tensor("PS%d" % i, [128, 1024], F32)) for i in range(4)]
        PSb = [[Buf("PS%d_%d" % (i, h), psum=True) for h in range(2)] for i in range(4)]

        def psh(i, h, n=512):
            return PS[i][:, h * 512:h * 512 + n]

        B = {}

        def bf(name):
            if name not in B:
                B[name] = Buf(name)
            return B[name]

        ringb = [Buf("ring%d" % i) for i in range(NS)]
        ring_ds = [DSem(mk_sem("ringl%d" % i)) for i in range(NS)]
        ring_dc = [DSem(mk_sem("ringc%d" % i)) for i in range(NS)]
        wkv_ld = DSem(mk_sem("wkvl"))
        stg_dc = [DSem(mk_sem("stgc%d" % i)) for i in range(2)]
        stg_ss = [DSem(mk_sem("stgs%d" % i)) for i in range(2)]
        ring_ss = [DSem(mk_sem("rings%d" % i)) for i in range(NS)]
        scrb = [Buf("scr%d" % u) for u in range(NU)]
        XBb = [[Buf("XB%d_%d" % (i, t)) for t in range(4)] for i in range(2)]
        XB_ld = [DSem(mk_sem("xbl%d" % i)) for i in range(2)]
        XB_st = [DSem(mk_sem("xbs%d" % i)) for i in range(2)]
        TTb = [Buf("T%d" % i) for i in range(8)]
        misc_ld = DSem(mk_sem("miscl"))
        misc_st = DSem(mk_sem("miscs"))
        ada_ld = [DSem(mk_sem("adal%d" % i)) for i in range(2)]
        h2_ld = DSem(mk_sem("h2l"))
        h2_st = DSem(mk_sem("h2s"))
        rope_ld = DSem(mk_sem("ropel"))
        dbg_st = DSem(mk_sem("dbgs"))
        dbg_s2 = [DSem(mk_sem("dbgs2_%d" % i)) for i in range(2)]
        x1sb = [Buf("x1s%d" % b) for b in range(NB)]
        h2sb = [Buf("h2s%d" % b) for b in range(NB)]

        ident_f = cmat[:, 0:128]
        Rm = cmat[:, 128:256]
        bd64 = cmat[:, 256:384]
        o512 = cmat[:, 384:512]
        ones_m = cmat[:, 512:640]

        g1 = par[:, 0:8]
        gm = par[:, 8:16]
        g2 = par[:, 16:24]
        qg = par[:, 24:25]
        kg = par[:, 25:26]
        ga = par[:, 26:30]
        gcv = par[:, 30:34]
        cw = par[:, 34:46]
        bT = par[:, 48:120]

        M_A1, M_B1, M_A1c, M_B1c, M_A2, M_B2, M_A2c, M_B2c, M_A3, M_B3 = range(10)

        def mod(i):
            return mods[:, i, :]

        kb.dma(SP, misc_ld, par[:], par_d[:, :], writes=[bf("par")])
        kb.dma(SP, misc_ld, cmat[:], cm_d[:, :], writes=[bf("cmat")])
        kb.dma(SP, misc_ld, cct[:], cc_d[:, :, :], writes=[bf("cct")])
        kb.dma(SP, misc_ld, fnbc[:], fn_d.partition_broadcast(128), writes=[bf("fnbc")])
        GSPEC = ((0, 2, 0.5, "G0"), (1, 2, 0.5, "G1c"), (2, 5, 1.0, "G2"), (3, 8, 0.5, "G3"))
        for gi, v, sc, nm in GSPEC:
            kb.dma(SP, misc_ld, Gt[gi][:], bada_d[0:1, v * D:(v + 1) * D].partition_broadcast(128), writes=[bf(nm)])
        for gi, v, sc, nm in GSPEC:
            bf(nm).w = (misc_ld.sem, misc_ld.count)
        for nm in ("par", "cmat", "cct", "fnbc"):
            bf(nm).w = (misc_ld.sem, misc_ld.count)

        kb.op(DVE, lambda: V.tensor_copy(out=identb[:], in_=ident_f), reads=[bf("cmat")], writes=[bf("identb")])
        kb.op(DVE, lambda: V.memset(Sada[:], 0.0), writes=[bf("Sada")])
        kb.op(DVE, lambda: V.memset(VA[:, :, 64:128], 1.0), writes=[bf("VAones")])
        kb.op(DVE, lambda: V.memset(stat[:], 0.0), writes=[bf("stat")])
        kb.op(DVE, lambda: V.memset(epsT[:], EPS), writes=[bf("epsT")])
        kb.op(ACT, lambda: A_.activation(out=Sada[:, :, 0], in_=cct[:, 0, :], func=AF.Silu),
              reads=[bf("cct")], writes=[bf("Sada")])
        kb.op(ACT, lambda: A_.activation(out=Sada[:, :, 32], in_=cct[:, 1, :], func=AF.Silu),
              reads=[bf("cct")], writes=[bf("Sada")])

        kb.dma(POOL, wkv_ld, Wkv[:, :, :],
               wmi_d[:, 512:768].rearrange("(kc p) c -> p kc c", p=128), writes=[bf("Wkv")])

        def cast_unit(u, sl, slb, dsem):
            w = [slb]

            def cd(out, in_):
                kb.dma(POOL, dsem, out, in_, writes=w)
            if u < U_Q:
                wi, wo = (w1i_d, w1o_d) if u < U_F2 else (w2i_d, w2o_d)
                r = u - (U_F1 if u < U_F2 else U_F2)
                if r < NF:
                    f = r
                    slv = sl[:, 0:2048].rearrange("p (k c) -> p k c", c=256)
                    cd(slv[:, :, 0:128], wi[:, f * 128:(f + 1) * 128].rearrange("(kc p) c -> p kc c", p=128))
                    cd(slv[:, :, 128:256],
                       wi[:, DFF + f * 128:DFF + (f + 1) * 128].rearrange("(kc p) c -> p kc c", p=128))
                else:
                    o = r - NF
                    for fi in range(3):
                        f = o * 3 + fi
                        if f < NF:
                            cd(sl[:, fi * 1024:(fi + 1) * 1024], wo[f * 128:(f + 1) * 128, :])
            elif u < U_CV:
                i = u - U_Q
                slv = sl[:, 0:2048].rearrange("p (k c) -> p k c", c=256)
                for jj in range(2):
                    j = i * 2 + jj
                    cd(slv[:, :, jj * 128:jj * 128 + 64],
                       wmi_d[:, j * 64:(j + 1) * 64].rearrange("(kc p) c -> p kc c", p=128))
                    cd(slv[:, :, jj * 128 + 64:jj * 128 + 128],
                       wmi_d[:, (j + 4) * 64:(j + 5) * 64].rearrange("(kc p) c -> p kc c", p=128))
            elif u < U_MO:
                c = u - U_CV
                slv = sl[:, 0:3072].rearrange("p (k c) -> p k c", c=384)
                for gi_, off in enumerate((768, 1280, 1792)):
                    cd(slv[:, :, gi_ * 128:(gi_ + 1) * 128],
                       wmi_d[:, off + c * 128:off + (c + 1) * 128].rearrange("(kc p) c -> p kc c", p=128))
            else:
                o = u - U_MO
                for ki in range(3):
                    k = o * 3 + ki
                    if k >= 8:
                        continue
                    dst = sl[:, ki * 1024:(ki + 1) * 1024]
                    if k < 4:
                        cd(dst[0:64, :], wmo_d[k * 64:(k + 1) * 64, :])
                        cd(dst[64:128, :], wmo_d[(k + 4) * 64:(k + 5) * 64, :])
                    else:
                        cd(dst, wmo_d[512 + (k - 4) * 128:512 + (k - 3) * 128, :])

        def usize(u):
            if u < U_Q:
                r = u - (U_F1 if u < U_F2 else U_F2)
                if r < NF:
                    return 2048
                return 1024 * min(3, NF - (r - NF) * 3)
            if u < U_CV:
                return 2048
            if u < U_MO:
                return 3072
            return 1024 * min(3, 8 - (u - U_MO) * 3)

        def store_unit(u, sl, slb, dsem):
            n = usize(u)
            kb.dma(POOL, dsem, wscr_d[u, :, 0:n], sl[:, 0:n], reads=[slb], writes=[scrb[u]])

        stg = [mixT[:, :, :].rearrange("p a b -> p (a b)"), stg1[:, :]]
        stgb = [bf("mixT"), Buf("stg1")]
        bg_state = [U_F2, 0]

        def bg_cast(n_units):
            for _ in range(n_units):
                u = bg_state[0]
                if u >= NU:
                    return
                k = bg_state[1] % 2
                cast_unit(u, stg[k], stgb[k], stg_dc[k])
                store_unit(u, stg[k], stgb[k], stg_ss[k])
                bg_state[0] += 1
                bg_state[1] += 1

        NPRE = U_F2
        pre_u = [0]
        st_u = [0]

        def precast_issue(n):
            for _ in range(n):
                if pre_u[0] < NPRE:
                    u = pre_u[0]
                    cast_unit(u, ring[u % NS], ringb[u % NS], ring_dc[u % NS])
                    pre_u[0] += 1

        def precast_store(n):
            for _ in range(n):
                if st_u[0] < pre_u[0]:
                    u = st_u[0]
                    n_ = usize(u)
                    kb.dma(SP, ring_ss[u % NS], wscr_d[u, :, 0:n_], ring[u % NS][:, 0:n_],
                           reads=[ringb[u % NS]], writes=[scrb[u]])
                    st_u[0] += 1

        precast_issue(NS)

        for gi, v, sc, nm in GSPEC:
            if sc != 1.0:
                kb.op(DVE, lambda gi=gi, sc=sc: V.tensor_scalar(out=Gt[gi][:], in0=Gt[gi][:], scalar1=sc, scalar2=None,
                                                                op0=ALU.mult), reads=[bf(nm)], writes=[bf(nm)])
        PMT = PS[1][:, 0:144]
        ada_first = True
        for q in range(18):
            xb = XB[q % 2]
            xbv = xb[:, :, :].rearrange("p a (b c) -> p (a b) c", c=512)
            kb.dma(SP, ada_ld[q % 2], xbv,
                   wada_d[:, q * 512:(q + 1) * 512].rearrange("(kc p) c -> p kc c", p=128),
                   writes=XBb[q % 2])
            precast_store(2)
            precast_issue(2)
            pm = PS[0][0:33, 0:512]

            def mm_ada(xbv=xbv, pm=pm):
                last = None
                for kc in range(8):
                    last = T_.matmul(pm, lhsT=Sada[:, kc, :], rhs=xbv[:, kc, :], start=(kc == 0), stop=(kc == 7))
                return last
            kb.op(PE, mm_ada, reads=XBb[q % 2] + [bf("Sada")], writes=[PSb[0][0]])
            mrow = TT[q % 2]
            kb.op(ACT, lambda mrow=mrow, pm=pm: A_.activation(out=mrow[0:33, :], in_=pm, func=AF.Copy),
                  reads=[PSb[0][0]], writes=[TTb[q % 2]])

            def mm_tr(mrow=mrow, q=q):
                last = None
                for jj in range(4):
                    j = q * 4 + jj
                    for w in range(2):
                        last = T_.matmul(PS[1][:, j * 2 + w:j * 2 + w + 1],
                                         lhsT=mrow[32 * w:32 * w + 1, jj * 128:(jj + 1) * 128],
                                         rhs=ones_m[32 * w:32 * w + 1, 0:1], start=True, stop=True)
                return last
            kb.op(PE, mm_tr, reads=[TTb[q % 2], bf("cmat")], writes=[PSb[1][0]])
            v = q // 2
            if v in (2, 5, 8):
                gi = {2: 0, 5: 2, 8: 3}[v]
                sc = 1.0 if v == 5 else 0.5
                half = q % 2
                pb = psh(2, 0)
                kb.op(PE, lambda mrow=mrow, pb=pb: T_.matmul(pb, lhsT=ones_m[0:1, :], rhs=mrow[0:1, :],
                                                          start=True, stop=True),
                      reads=[TTb[q % 2], bf("cmat")], writes=[PSb[2][0]])
                kb.op(DVE, lambda pb=pb, gi=gi, half=half, sc=sc: V.scalar_tensor_tensor(
                    out=Gt[gi][:, half * 512:(half + 1) * 512], in0=pb, scalar=sc,
                    in1=Gt[gi][:, half * 512:(half + 1) * 512], op0=ALU.mult, op1=ALU.add),
                    reads=[PSb[2][0], bf("G%d" % gi)], writes=[bf("G%d" % gi)])
                if v == 2:
                    pb2 = psh(2, 1)
                    kb.op(PE, lambda mrow=mrow, pb2=pb2: T_.matmul(pb2, lhsT=ones_m[32:33, :], rhs=mrow[32:33, :],
                                                                start=True, stop=True),
                          reads=[TTb[q % 2], bf("cmat")], writes=[PSb[2][1]])
                    kb.op(DVE, lambda pb2=pb2, half=half: V.scalar_tensor_tensor(
                        out=Gt[1][:, half * 512:(half + 1) * 512], in0=pb2, scalar=0.5,
                        in1=Gt[1][:, half * 512:(half + 1) * 512], op0=ALU.mult, op1=ALU.add),
                        reads=[PSb[2][1], bf("G1c")], writes=[bf("G1c")])
        while st_u[0] < NPRE:
            precast_store(1)
            precast_issue(1)
        kb.op(DVE, lambda: V.tensor_tensor(out=mT[:, :, 0], in0=PS[1][:, 0:144:2], in1=bT, op=ALU.add),
              reads=[PSb[1][0], bf("par")], writes=[bf("mT")])
        kb.op(DVE, lambda: V.tensor_tensor(out=mT[:, :, 1], in0=PS[1][:, 1:144:2], in1=bT, op=ALU.add),
              reads=[PSb[1][0], bf("par")], writes=[bf("mT")])

        def mk_mod(ai, bi, vs, vsh, g, w):
            kb.op(DVE, lambda: V.scalar_tensor_tensor(out=mod(ai), in0=mT[:, vs * 8:(vs + 1) * 8, w], scalar=1.0,
                                                      in1=g, op0=ALU.add, op1=ALU.mult),
                  reads=[bf("mT"), bf("par")], writes=[bf("mods")])
            kb.op(DVE, lambda: V.tensor_copy(out=mod(bi), in_=mT[:, vsh * 8:(vsh + 1) * 8, w]),
                  reads=[bf("mT")], writes=[bf("mods")])
        mk_mod(M_A1, M_B1, 1, 0, g1, 0)
        mk_mod(M_A1c, M_B1c, 1, 0, g1, 1)
        mk_mod(M_A2, M_B2, 4, 3, gm, 0)
        mk_mod(M_A2c, M_B2c, 4, 3, gm, 1)
        mk_mod(M_A3, M_B3, 7, 6, g2, 0)

        if debug:
            kb.dma(POOL, dbg_st, dbg_mt[:, :], mT[:, :, :].rearrange("p a b -> p (a b)"), reads=[bf("mT")])
            for i in range(4):
                kb.dma(POOL, dbg_st, dbg_g[:, i * D:(i + 1) * D], Gt[i][:], reads=[bf("G0"), bf("G1c"), bf("G2"), bf("G3")])

        class WStream:
            def __init__(self):
                self.sched = []
                self.nl = 0
                self.nu = 0
                self.rel = 0

            def done(self, n=1):
                self.rel += n

            def prefetch(self, upto):
                while self.nl < min(upto, len(self.sched)):
                    i = self.nl
                    u = self.sched[i]
                    s = i % NS
                    n = usize(u)
                    kb.dma(SP, ring_ds[s], ring[s][:, 0:n], wscr_d[u, :, 0:n], reads=[scrb[u]], writes=[ringb[s]])
                    self.nl += 1

            def acquire(self, u):
                i = self.nu
                assert self.sched[i] == u, (i, u, self.sched[i])
                assert i < self.rel + NS
                self.prefetch(min(self.rel + NS, PFLIM[0]))
                self.nu += 1
                return ring[i % NS], ringb[i % NS]

        ws = WStream()
        ffn1_units = list(range(U_F1, U_F1 + 30))
        ffn2_units = list(range(U_F2, U_F2 + 30))
        for b in range(NB + 1):
            ws.sched += ffn1_units
        for b in range(NB):
            ws.sched += [U_Q, U_Q + 1] + list(range(U_CV, U_CV + 4)) + list(range(U_MO, U_MO + 3)) + ffn2_units

        def rms_to_T(xi, NT, Ai, Bi, dstT, col0, dstb, part=0):
            TBk = NT * 128
            for t in range(NT if part != 2 else 0):
                kb.op(ACT, lambda t=t: A_.activation(out=junk[:], in_=XB[xi][:, t, :], func=AF.Square,
                                                     scale=1.0 / 32.0, accum_out=stat[:, t:t + 1]),
                      reads=[XBb[xi][t]], writes=[bf("ss%d" % t)])
                kb.op(ACT, lambda t=t: A_.activation(out=stat[:, t:t + 1], in_=stat[:, t:t + 1], func=AF.Ln,
                                                     bias=epsT[:, 0:1]),
                      reads=[bf("ss%d" % t), bf("epsT")], writes=[bf("ss%d" % t)])
                kb.op(ACT, lambda t=t: A_.activation(out=stat[:, 4 + t:5 + t], in_=stat[:, t:t + 1], func=AF.Exp,
                                                     scale=-0.5),
                      reads=[bf("ss%d" % t)], writes=[bf("rstd%d" % t)])
                if t % 2 == 0:
                    kb.op(ACT, lambda t=t: A_.activation(out=xn[:, t, :], in_=XB[xi][:, t, :], func=AF.Copy,
                                                         scale=stat[:, 4 + t:5 + t]),
                          reads=[XBb[xi][t], bf("rstd%d" % t)], writes=[bf("xn%d" % t)])
                else:
                    kb.op(DVE, lambda t=t: V.tensor_scalar(out=xn[:, t, :], in0=XB[xi][:, t, :],
                                                           scalar1=stat[:, 4 + t:5 + t], scalar2=None, op0=ALU.mult),
                          reads=[XBb[xi][t], bf("rstd%d" % t)], writes=[bf("xn%d" % t)])
            for kc in range(8 if part != 1 else 0):
                pi, ph = 2 + kc // 4, (kc % 4) // 2
                ptv = PS[pi].bitcast(BF16)
                base = (kc % 4) * 512

                def tr(kc=kc, ptv=ptv, base=base):
                    last = None
                    for t in range(NT):
                        last = T_.transpose(out=ptv[:, base + t * 128:base + (t + 1) * 128],
                                            in_=xn[:, t, kc * 128:(kc + 1) * 128], identity=identb[:])
                    return last
                kb.op(PE, tr, reads=[bf("xn%d" % t) for t in range(NT)] + [bf("identb")], writes=[PSb[pi][ph]])
                src = ptv[:, base:base + TBk]
                dst = dstT[:, kc, col0:col0 + TBk]
                if kc % 2 == 0:
                    kb.op(ACT, lambda src=src, dst=dst, kc=kc: A_.activation(
                        out=dst, in_=src, func=AF.Identity, scale=mod(Ai)[:, kc:kc + 1], bias=mod(Bi)[:, kc:kc + 1]),
                        reads=[PSb[pi][ph], bf("mods")], writes=[dstb[kc]])
                else:
                    kb.op(DVE, lambda src=src, dst=dst, kc=kc: V.tensor_scalar(
                        out=dst, in0=src, scalar1=mod(Ai)[:, kc:kc + 1], scalar2=mod(Bi)[:, kc:kc + 1],
                        op0=ALU.mult, op1=ALU.add),
                        reads=[PSb[pi][ph], bf("mods")], writes=[dstb[kc]])

        hTb = [bf("hT%d" % kc) for kc in range(8)]
        h2Tb = [bf("h2T%d" % kc) for kc in range(8)]

        def ffn(xi, NT, Ai, Bi, gi, ubase, do_norm=True, on_pool=True, between=None):
            TBk = NT * 128
            if do_norm:
                rms_to_T(xi, NT, Ai, Bi, hT, 0, hTb)
            for f in range(NF):
                sl, slb = ws.acquire(ubase + f)
                pg = PS[f % 2][:, 0:TBk]
                pu = PS[f % 2][:, 512:512 + TBk]

                def mm1(sl=sl, pg=pg, pu=pu):
                    last = None
                    for kc in range(8):
                        T_.matmul(pg, lhsT=sl[:, kc * 256:kc * 256 + 128], rhs=hT[:, kc, 0:TBk],
                                  start=(kc == 0), stop=(kc == 7))
                    for kc in range(8):
                        last = T_.matmul(pu, lhsT=sl[:, kc * 256 + 128:kc * 256 + 256], rhs=hT[:, kc, 0:TBk],
                                         start=(kc == 0), stop=(kc == 7))
                    return last
                kb.op(PE, mm1, reads=[slb] + hTb, writes=[PSb[f % 2][0], PSb[f % 2][1]])
                sg = TT[f % 2]
                kb.op(ACT, lambda sg=sg, pg=pg: A_.activation(out=sg[:, 0:TBk], in_=pg, func=AF.Silu),
                      reads=[PSb[f % 2][0]], writes=[TTb[f % 2]])
                kb.op(DVE, lambda sg=sg, pu=pu, f=f: V.tensor_tensor(out=actT[:, f, 0:TBk], in0=sg[:, 0:TBk], in1=pu,
                                                                     op=ALU.mult),
                      reads=[TTb[f % 2], PSb[f % 2][1]], writes=[bf("act%d" % f)])
                ws.done()
            if between is not None:
                between()
            accb = [PSb[t][h] for t in range(NT) for h in range(2)]
            for o in range(8):
                sl, slb = ws.acquire(ubase + NF + o)
                fs = [o * 3 + fi for fi in range(3) if o * 3 + fi < NF]

                def mm2(sl=sl, fs=fs, o=o):
                    last = None
                    for fi, f in enumerate(fs):
                        for t in range(NT):
                            for h in range(2):
                                last = T_.matmul(psh(t, h), lhsT=actT[:, f, t * 128:(t + 1) * 128],
                                                 rhs=sl[:, fi * 1024 + h * 512:fi * 1024 + (h + 1) * 512],
                                                 start=(f == 0), stop=(f == NF - 1))
                    return last
                kb.op(PE, mm2, reads=[slb] + [bf("act%d" % f) for f in fs], writes=accb)
                ws.done()
            k = 0
            for t in range(NT):
                for h in range(2):
                    tmp = TT[2 + k % 4]
                    tb = TTb[2 + k % 4]
                    kb.op(DVE, lambda tmp=tmp, t=t, h=h: V.tensor_tensor(
                        out=tmp[:, :], in0=psh(t, h), in1=Gt[gi][:, h * 512:(h + 1) * 512], op=ALU.mult),
                        reads=[PSb[t][h], bf("G%s" % ("1c" if gi == 1 else str(gi)))], writes=[tb])
                    if on_pool:
                        kb.op(POOL, lambda tmp=tmp, t=t, h=h: G_.tensor_tensor(
                            out=XB[xi][:, t, h * 512:(h + 1) * 512], in0=XB[xi][:, t, h * 512:(h + 1) * 512],
                            in1=tmp[:, :], op=ALU.add),
                            reads=[tb, XBb[xi][t]], writes=[XBb[xi][t]])
                    else:
                        kb.op(DVE, lambda tmp=tmp, t=t, h=h: V.tensor_tensor(
                            out=XB[xi][:, t, h * 512:(h + 1) * 512], in0=XB[xi][:, t, h * 512:(h + 1) * 512],
                            in1=tmp[:, :], op=ALU.add),
                            reads=[tb, XBb[xi][t]], writes=[XBb[xi][t]])
                    k += 1

        HN_SETS = [
            dict(raw=2, sq=3, kn=4, t1=5, ps=3),
            dict(raw=6, sq=7, kn=0, t1=1, ps=1),
        ]

        def headnorm_gen(psrc, psb, n, gsc, dst, dstb, with_rope, on_pool=True, tset=0):
            ts = HN_SETS[tset]
            raw, rawb = TT[ts["raw"]], TTb[ts["raw"]]
            sq, sqb = TT[ts["sq"]], TTb[ts["sq"]]
            kn, knb = TT[ts["kn"]], TTb[ts["kn"]]
            t1, t1b = TT[ts["t1"]], TTb[ts["t1"]]
            pi = ts["ps"]
            pss, pssb = PS[pi][:, 0:n], PSb[pi][0]
            prot, protb = PS[pi][:, 512:512 + n], PSb[pi][1]
            kb.op(DVE, lambda: V.tensor_copy(out=raw[:, 0:n], in_=psrc), reads=[psb], writes=[rawb])
            yield
            kb.op(ACT, lambda: A_.activation(out=sq[:, 0:n], in_=psrc, func=AF.Square), reads=[psb], writes=[sqb])
            yield
            kb.op(PE, lambda: T_.matmul(pss, lhsT=bd64, rhs=sq[:, 0:n], start=True, stop=True),
                  reads=[sqb, bf("cmat")], writes=[pssb])
            yield
            kb.op(ACT, lambda: A_.activation(out=sq[:, 0:n], in_=pss, func=AF.Ln, bias=epsT[:, 0:1]),
                  reads=[pssb, bf("epsT")], writes=[sqb])
            yield
            kb.op(ACT, lambda: A_.activation(out=sq[:, 0:n], in_=sq[:, 0:n], func=AF.Exp, scale=-0.5),
                  reads=[sqb], writes=[sqb])
            yield
            if not with_rope:
                kb.op(DVE, lambda: V.scalar_tensor_tensor(out=dst, in0=raw[:, 0:n], scalar=gsc, in1=sq[:, 0:n],
                                                          op0=ALU.mult, op1=ALU.mult),
                      reads=[rawb, sqb, bf("par")], writes=[dstb])
                yield
                return
            kb.op(DVE, lambda: V.scalar_tensor_tensor(out=kn[:, 0:n], in0=raw[:, 0:n], scalar=gsc, in1=sq[:, 0:n],
                                                      op0=ALU.mult, op1=ALU.mult),
                  reads=[rawb, sqb, bf("par")], writes=[knb])
            yield
            kb.op(PE, lambda: T_.matmul(prot, lhsT=Rm, rhs=kn[:, 0:n], start=True, stop=True),
                  reads=[knb, bf("cmat")], writes=[protb])
            yield
            if on_pool:
                kb.op(POOL, lambda: G_.tensor_tensor(out=t1[:, 0:n], in0=kn[:, 0:n], in1=ropeT[:, 0, 0:n], op=ALU.mult),
                      reads=[knb, bf("rope")], writes=[t1b])
            else:
                kb.op(DVE, lambda: V.tensor_tensor(out=t1[:, 0:n], in0=kn[:, 0:n], in1=ropeT[:, 0, 0:n], op=ALU.mult),
                      reads=[knb, bf("rope")], writes=[t1b])
            yield
            kb.op(DVE, lambda: V.tensor_tensor(out=raw[:, 0:n], in0=prot, in1=ropeT[:, 1, 0:n], op=ALU.mult),
                  reads=[protb, bf("rope")], writes=[rawb])
            yield
            kb.op(DVE, lambda: V.tensor_tensor(out=dst, in0=t1[:, 0:n], in1=raw[:, 0:n], op=ALU.add),
                  reads=[t1b, rawb], writes=[dstb])
            yield

        def headnorm_rope(psrc, psb, n, gsc, dst, dstb, with_rope, on_pool=True):
            for _ in headnorm_gen(psrc, psb, n, gsc, dst, dstb, with_rope, on_pool=on_pool, tset=0):
                pass

        def kv_stage(NT, kc0, with_rope):
            TBk = NT * 128
            pk = PS[0][:, 0:TBk]

            def mmk():
                last = None
                for kc in range(8):
                    last = T_.matmul(pk, lhsT=Wkv[:, kc, 0:128], rhs=h2T[:, kc, 2:2 + TBk],
                                     start=(kc == 0), stop=(kc == 7))
                return last
            kb.op(PE, mmk, reads=[bf("Wkv")] + h2Tb, writes=[PSb[0][0]])
            pv = PS[0][:, 512:512 + TBk]

            def mmv():
                last = None
                for t in range(NT):
                    for kc in range(8):
                        last = T_.matmul(pv[:, t * 128:(t + 1) * 128], lhsT=h2T[:, kc, 2 + t * 128:2 + (t + 1) * 128],
                                         rhs=Wkv[:, kc, 128:256], start=(kc == 0), stop=(kc == 7))
                return last
            kb.op(PE, mmv, reads=[bf("Wkv")] + h2Tb, writes=[PSb[0][1]])
            pv3 = pv.rearrange("p (t c) -> p t c", c=128)
            kb.op(ACT, lambda: A_.activation(out=VA[:, kc0:kc0 + NT, 0:64], in_=pv3[:, :, 0:64], func=AF.Copy),
                  reads=[PSb[0][1]], writes=[bf("VA")])
            kb.op(DVE, lambda: V.tensor_copy(out=VA[:, kc0:kc0 + NT, 128:192], in_=pv3[:, :, 64:128]),
                  reads=[PSb[0][1]], writes=[bf("VA")])
            headnorm_rope(pk, PSb[0][0], TBk, kg, KT[:, kc0 * 128:kc0 * 128 + TBk], bf("KT"), with_rope, on_pool=False)

        def load_x(src_ap, xi, NT, b_reads=()):
            kb.dma(SP, XB_ld[xi], XB[xi][:, 0:NT, :], src_ap.rearrange("(t p) d -> p t d", p=128),
                   reads=list(b_reads), writes=XBb[xi][0:NT])

        if limit >= 2:
            load_x(ctx_d[:, :], 0, 2)
            load_x(x_d[0:TB, :], 1, 4)
            rms_to_T(0, 2, M_A1c, M_B1c, hT, 0, hTb)
            ffn(0, 2, M_A1c, M_B1c, 1, U_F1, do_norm=False, on_pool=False,
                between=(lambda: rms_to_T(1, 4, M_A1, M_B1, hT, 0, hTb, part=1)) if limit >= 3 else None)
            if limit >= 3:
                rms_to_T(1, 4, M_A1, M_B1, hT, 0, hTb, part=2)
            rms_to_T(0, 2, M_A2c, M_B2c, h2T, 2, h2Tb)
            kv_stage(2, 0, False)
        for b in range(NB if limit >= 3 else 0):
            xi = (b + 1) % 2
            if b + 1 < NB:
                load_x(x_d[(b + 1) * TB:(b + 2) * TB, :], 1 - xi, 4)
            kb.dma(SP, rope_ld, ropeT[:, :, :], rope_d[:, :, b * TB:(b + 1) * TB], writes=[bf("rope")])
            bg_cast((10, 10, 10, 3, 2, 2, 2, 0)[b])
            nxt = (lambda xi=xi: rms_to_T(1 - xi, 4, M_A1, M_B1, hT, 0, hTb, part=1)) if b + 1 < NB else None
            ffn(xi, 4, M_A1, M_B1, 0, U_F1, do_norm=False, on_pool=False, between=nxt)
            kb.dma(POOL, XB_st[xi], x1s_d[b * TB:(b + 1) * TB, :].rearrange("(t p) d -> p t d", p=128),
                   XB[xi][:, :, :], reads=XBb[xi], writes=[x1sb[b]])
            if b + 1 < NB:
                rms_to_T(1 - xi, 4, M_A1, M_B1, hT, 0, hTb, part=2)
            rms_to_T(xi, 4, M_A2, M_B2, h2T, 2, h2Tb)
            kb.dma(POOL, h2_st, h2s_d[:, :, b * TB:(b + 1) * TB], h2T[:, :, 2:2 + TB], reads=h2Tb, writes=[h2sb[b]])
            kv_stage(4, 2 + b * 4, True)

        for b in range(NB):
            xi = (b + 1) % 2
            x1sb[b].w = (XB_st[xi].sem, XB_st[xi].count)
            h2sb[b].w = (h2_st.sem, h2_st.count)
        if debug:
            for c in range(NKC * 128 // 256):
                tt = TT[c % 2]
                kb.op(DVE, lambda c=c, tt=tt: V.tensor_copy(out=tt[:, 0:256], in_=KT[:, c * 256:(c + 1) * 256]),
                      reads=[bf("KT")], writes=[TTb[c % 2]])
                kb.dma(POOL, dbg_s2[c % 2], dbg_kt[:, c * 256:(c + 1) * 256], tt[:, 0:256], reads=[TTb[c % 2]])
            vaf = VA[:, :, :].rearrange("p a b -> p (a b)")
            for c in range(NKC * 192 // 384):
                tt = TT[c % 2]
                kb.op(DVE, lambda c=c, tt=tt: V.tensor_copy(out=tt[:, 0:384], in_=vaf[:, c * 384:(c + 1) * 384]),
                      reads=[bf("VA"), bf("VAones")], writes=[TTb[c % 2]])
                kb.dma(POOL, dbg_s2[c % 2], dbg_va[:, c * 384:(c + 1) * 384], tt[:, 0:384], reads=[TTb[c % 2]])

        def q_stage(b, extra=None):
            for i in range(2):
                sl, slb = ws.acquire(U_Q + i)
                gens = []
                for jj in range(2):
                    j = i * 2 + jj
                    pq = PS[0][:, jj * 512:jj * 512 + TB]

                    def mmq(sl=sl, jj=jj, pq=pq):
                        last = None
                        for kc in range(8):
                            last = T_.matmul(pq, lhsT=sl[:, kc * 256 + jj * 128:kc * 256 + (jj + 1) * 128],
                                             rhs=h2T[:, kc, 2:2 + TB], start=(kc == 0), stop=(kc == 7))
                        return last
                    kb.op(PE, mmq, reads=[slb] + h2Tb, writes=[PSb[0][jj]])
                    gens.append(headnorm_gen(pq, PSb[0][jj], TB, qg, QT[:, j, :], bf("QT%d" % j), True, tset=jj))
                if extra is not None and i == 0:
                    gens.append(extra)
                live = list(gens)
                while live:
                    for g in list(live):
                        try:
                            next(g)
                        except StopIteration:
                            live.remove(g)
                ws.done()

        def attn_stage(b):
            pending = {}
            for t in range(4):
                po = [psh(2, 0), psh(2, 1)]
                pob = [PSb[2][0], PSb[2][1]]

                def qk(kc, t=t):
                    pi = kc % 2
                    def f():
                        T_.matmul(psh(pi, 0), lhsT=KT[0:64, kc * 128:(kc + 1) * 128],
                                  rhs=QT[0:64, :, t * 128:(t + 1) * 128], start=True, stop=True)
                        return T_.matmul(psh(pi, 1), lhsT=KT[64:128, kc * 128:(kc + 1) * 128],
                                         rhs=QT[64:128, :, t * 128:(t + 1) * 128], start=True, stop=True)
                    kb.op(PE, f, reads=[bf("KT")] + [bf("QT%d" % j_) for j_ in range(4)], writes=[PSb[pi][0], PSb[pi][1]])

                def ex(kc):
                    pi = kc % 2
                    pe_ = kc % 3
                    kb.op(ACT, lambda: A_.activation(out=PTe[pe_][:, :], in_=PS[pi][:, :], func=AF.Exp, scale=0.125),
                          reads=[PSb[pi][0], PSb[pi][1]], writes=[bf("PTe%d" % pe_)])

                def pv(kc):
                    pe_ = kc % 3
                    def f():
                        T_.matmul(po[0], lhsT=VA[:, kc, 0:128], rhs=PTe[pe_][:, 0:512],
                                  start=(kc == 0), stop=(kc == NKC - 1))
                        return T_.matmul(po[1], lhsT=VA[:, kc, 64:192], rhs=PTe[pe_][:, 512:1024],
                                         start=(kc == 0), stop=(kc == NKC - 1))
                    kb.op(PE, f, reads=[bf("VA"), bf("VAones"), bf("PTe%d" % pe_)], writes=pob)
                def epi_parts(t=t):
                    o0, o1, rc, at, sq, rs = TT[0], TT[1], TT[2], TT[3], TT[4], TT[5]
                    pss = PS[3][:, 0:128]

                    def p1():
                        kb.op(DVE, lambda: V.reciprocal(out=rc[0:64, :], in_=o0[64:128, :]), reads=[TTb[0]], writes=[TTb[2]])
                        kb.op(DVE, lambda: V.reciprocal(out=rc[64:128, :], in_=o1[0:64, :]), reads=[TTb[1]], writes=[TTb[2]])
                        kb.op(DVE, lambda: V.tensor_tensor(out=at[0:64, :], in0=o0[0:64, :], in1=rc[0:64, :], op=ALU.mult),
                              reads=[TTb[0], TTb[2]], writes=[TTb[3]])
                        kb.op(DVE, lambda: V.tensor_tensor(out=at[64:128, :], in0=o1[64:128, :], in1=rc[64:128, :],
                                                           op=ALU.mult), reads=[TTb[1], TTb[2]], writes=[TTb[3]])

                    def p2():
                        kb.op(DVE, lambda: V.tensor_tensor(out=sq[:, :], in0=at[:, :], in1=at[:, :], op=ALU.mult),
                              reads=[TTb[3]], writes=[TTb[4]])

                    def p3():
                        def mss():
                            last = None
                            for j in range(4):
                                last = T_.matmul(pss, lhsT=o512, rhs=sq[:, j * 128:(j + 1) * 128],
                                                 start=(j == 0), stop=(j == 3))
                            return last
                        kb.op(PE, mss, reads=[TTb[4], bf("cmat")], writes=[PSb[3][0]])

                    def p4():
                        kb.op(ACT, lambda: A_.activation(out=rs[:, 0:128], in_=pss, func=AF.Ln, bias=epsT[:, 0:1]),
                              reads=[PSb[3][0], bf("epsT")], writes=[TTb[5]])
                        kb.op(ACT, lambda: A_.activation(out=rs[:, 0:128], in_=rs[:, 0:128], func=AF.Exp, scale=-0.5),
                              reads=[TTb[5]], writes=[TTb[5]])

                    def p5():
                        for j in range(4):
                            kb.op(DVE, lambda j=j: V.scalar_tensor_tensor(
                                out=mixT[:, j, t * 128:(t + 1) * 128], in0=at[:, j * 128:(j + 1) * 128],
                                scalar=ga[:, j:j + 1], in1=rs[:, 0:128], op0=ALU.mult, op1=ALU.mult),
                                reads=[TTb[3], TTb[5], bf("par")], writes=[bf("mixT")])
                    return {2: p1, 4: p2, 14: p3, 20: p4, 26: p5}

                qk(0)
                qk(1)
                for kc in range(NKC):
                    ex(kc)
                    if kc + 2 < NKC:
                        qk(kc + 2)
                    pv(kc)
                    if kc in pending:
                        pending[kc]()
                pending.clear()
                kb.op(DVE, lambda: V.tensor_copy(out=TT[0][:, :], in_=po[0]), reads=[pob[0]], writes=[TTb[0]])
                kb.op(DVE, lambda: V.tensor_copy(out=TT[1][:, :], in_=po[1]), reads=[pob[1]], writes=[TTb[1]])
                pending.update(epi_parts())
                if t == 3:
                    for k_ in sorted(pending):
                        pending[k_]()
                    pending.clear()

        def conv_stage(b):
            SC = [TT[0], TT[1], TT[2], TT[3]]
            it = 0
            for c in range(4):
                sl, slb = ws.acquire(U_CV + c)
                for s in range(2):
                    par_ = it % 2
                    it += 1
                    if par_ == 0:
                        pgb, pgbb = PS[0][:, 0:256], PSb[0][0]
                        pgc, pgcb = PS[0][:, 512:512 + 258], PSb[0][1]
                        puu, puub = PS[1][:, 0:258], PSb[1][0]
                        zu, zub, y, yb = TT[4], TTb[4], TT[5], TTb[5]
                    else:
                        pgb, pgbb = PS[2][:, 0:256], PSb[2][0]
                        pgc, pgcb = PS[2][:, 512:512 + 258], PSb[2][1]
                        puu, puub = PS[1][:, 512:512 + 258], PSb[1][1]
                        zu, zub, y, yb = TT[6], TTb[6], TT[7], TTb[7]

                    def mmc(sl=sl, s=s, pgb=pgb, pgc=pgc, puu=puu):
                        last = None
                        for kc in range(8):
                            T_.matmul(pgb, lhsT=sl[:, kc * 384:kc * 384 + 128],
                                      rhs=h2T[:, kc, 2 + s * 256:2 + s * 256 + 256], start=(kc == 0), stop=(kc == 7))
                        for kc in range(8):
                            T_.matmul(pgc, lhsT=sl[:, kc * 384 + 128:kc * 384 + 256],
                                      rhs=h2T[:, kc, 1 + s * 256:1 + s * 256 + 258], start=(kc == 0), stop=(kc == 7))
                        for kc in range(8):
                            last = T_.matmul(puu, lhsT=sl[:, kc * 384 + 256:kc * 384 + 384],
                                             rhs=h2T[:, kc, 1 + s * 256:1 + s * 256 + 258], start=(kc == 0), stop=(kc == 7))
                        return last
                    kb.op(PE, mmc, reads=[slb] + h2Tb, writes=[pgbb, pgcb, puub])
                    kb.op(ACT, lambda zu=zu, puu=puu: A_.activation(out=zu[:, 0:258], in_=puu, func=AF.Copy),
                          reads=[puub], writes=[zub])
                    kb.op(DVE, lambda zu=zu, pgc=pgc: V.tensor_tensor(out=zu[:, 0:258], in0=pgc, in1=zu[:, 0:258],
                                                                      op=ALU.mult),
                          reads=[pgcb, zub], writes=[zub])
                    kb.op(ACT, lambda c=c, zu=zu, y=y: A_.activation(out=y[:, 0:256], in_=zu[:, 1:257], func=AF.Copy,
                                                                     scale=cw[:, c * 3 + 1:c * 3 + 2]),
                          reads=[zub, bf("par")], writes=[yb])
                    kb.op(DVE, lambda c=c, zu=zu, y=y: V.scalar_tensor_tensor(
                        out=y[:, 0:256], in0=zu[:, 0:256], scalar=cw[:, c * 3:c * 3 + 1], in1=y[:, 0:256],
                        op0=ALU.mult, op1=ALU.add), reads=[zub, yb, bf("par")], writes=[yb])
                    kb.op(DVE, lambda c=c, zu=zu, y=y: V.scalar_tensor_tensor(
                        out=y[:, 0:256], in0=zu[:, 2:258], scalar=cw[:, c * 3 + 2:c * 3 + 3], in1=y[:, 0:256],
                        op0=ALU.mult, op1=ALU.add), reads=[zub, yb, bf("par")], writes=[yb])
                    kb.op(DVE, lambda c=c, s=s, y=y, pgb=pgb: V.tensor_tensor(
                        out=SC[c][:, s * 256:(s + 1) * 256], in0=pgb, in1=y[:, 0:256], op=ALU.mult),
                        reads=[pgbb, yb], writes=[TTb[c]])
                ws.done()
            pss = PS[3][:, 0:512]
            for c in range(4):
                sq, sqb = (TT[4], TTb[4]) if c % 2 == 0 else (TT[6], TTb[6])
                kb.op(ACT, lambda c=c, sq=sq: A_.activation(out=sq[:, :], in_=SC[c][:, :], func=AF.Square),
                      reads=[TTb[c]], writes=[sqb])
                kb.op(PE, lambda c=c, sq=sq: T_.matmul(pss, lhsT=o512, rhs=sq[:, :], start=(c == 0), stop=(c == 3)),
                      reads=[sqb, bf("cmat")], writes=[PSb[3][0]])
            rs = TT[5]
            kb.op(ACT, lambda: A_.activation(out=rs[:, :], in_=pss, func=AF.Ln, bias=epsT[:, 0:1]),
                  reads=[PSb[3][0], bf("epsT")], writes=[TTb[5]])
            kb.op(ACT, lambda: A_.activation(out=rs[:, :], in_=rs[:, :], func=AF.Exp, scale=-0.5),
                  reads=[TTb[5]], writes=[TTb[5]])
            for c in range(4):
                kb.op(DVE, lambda c=c: V.scalar_tensor_tensor(out=mixT[:, 4 + c, :], in0=SC[c][:, :],
                                                              scalar=gcv[:, c:c + 1], in1=rs[:, :],
                                                              op0=ALU.mult, op1=ALU.mult),
                      reads=[TTb[c], TTb[5], bf("par")], writes=[bf("mixT")])

        def mixout_stage(b, xi):
            sls = [ws.acquire(U_MO + o) for o in range(3)]
            k = 0
            for t in range(4):
                for h in range(2):
                    pi, ph = k % 4, 0
                    po = psh(pi, ph)

                    def mmo(t=t, h=h, po=po):
                        last = None
                        for kk in range(8):
                            sl = sls[kk // 3][0]
                            ki = kk % 3
                            last = T_.matmul(po, lhsT=mixT[:, kk, t * 128:(t + 1) * 128],
                                             rhs=sl[:, ki * 1024 + h * 512:ki * 1024 + (h + 1) * 512],
                                             start=(kk == 0), stop=(kk == 7))
                        return last
                    kb.op(PE, mmo, reads=[s_[1] for s_ in sls] + [bf("mixT")], writes=[PSb[pi][ph]])
                    tmp, tb = TT[k % 4], TTb[k % 4]
                    kb.op(DVE, lambda tmp=tmp, po=po, h=h: V.tensor_tensor(
                        out=tmp[:, :], in0=po, in1=Gt[2][:, h * 512:(h + 1) * 512], op=ALU.mult),
                        reads=[PSb[pi][ph], bf("G2")], writes=[tb])
                    kb.op(POOL, lambda tmp=tmp, t=t, h=h: G_.tensor_tensor(
                        out=XB[xi][:, t, h * 512:(h + 1) * 512], in0=XB[xi][:, t, h * 512:(h + 1) * 512],
                        in1=tmp[:, :], op=ALU.add),
                        reads=[tb, XBb[xi][t]], writes=[XBb[xi][t]])
                    k += 1
            ws.done(3)

        def final_gen(b, xi):
            for t in range(4):
                kb.op(ACT, lambda t=t: A_.activation(out=junk[:], in_=XB[xi][:, t, :], func=AF.Square,
                                                     scale=1.0 / 32.0, accum_out=stat[:, 8 + t:9 + t]),
                      reads=[XBb[xi][t]], writes=[bf("fss%d" % t)])
                yield
                kb.op(ACT, lambda t=t: A_.activation(out=stat[:, 8 + t:9 + t], in_=stat[:, 8 + t:9 + t], func=AF.Ln,
                                                     bias=epsT[:, 0:1]),
                      reads=[bf("fss%d" % t), bf("epsT")], writes=[bf("fss%d" % t)])
                yield
                kb.op(ACT, lambda t=t: A_.activation(out=stat[:, 12 + t:13 + t], in_=stat[:, 8 + t:9 + t], func=AF.Exp,
                                                     scale=-0.5),
                      reads=[bf("fss%d" % t)], writes=[bf("frs%d" % t)])
                yield
                kb.op(DVE, lambda t=t: V.scalar_tensor_tensor(out=XB[xi][:, t, :], in0=XB[xi][:, t, :],
                                                              scalar=stat[:, 12 + t:13 + t], in1=fnbc[:, :],
                                                              op0=ALU.mult, op1=ALU.mult),
                      reads=[XBb[xi][t], bf("frs%d" % t), bf("fnbc")], writes=[XBb[xi][t]])
                yield
            kb.dma(POOL, XB_st[xi], out_d[b * TB:(b + 1) * TB, :].rearrange("(t p) d -> p t d", p=128),
                   XB[xi][:, :, :], reads=XBb[xi])
            yield

        def load_blockB(b, xi):
            kb.dma(SP, XB_ld[xi], XB[xi][:, :, :], x1s_d[b * TB:(b + 1) * TB, :].rearrange("(t p) d -> p t d", p=128),
                   reads=[x1sb[b]], writes=XBb[xi])

        def load_h2T(b):
            lo = b * TB - 2
            hi = b * TB + TB + 2
            rd = [h2sb[bb] for bb in (b - 1, b, b + 1) if 0 <= bb < NB]
            if b == 0:
                kb.op(DVE, lambda: V.memset(h2T[:, :, 0:2], 0.0), writes=h2Tb)
                kb.dma(SP, h2_ld, h2T[:, :, 2:TB + 4], h2s_d[:, :, 0:hi], reads=rd, writes=h2Tb)
            elif b == NB - 1:
                kb.op(DVE, lambda: V.memset(h2T[:, :, TB + 2:TB + 4], 0.0), writes=h2Tb)
                kb.dma(SP, h2_ld, h2T[:, :, 0:TB + 2], h2s_d[:, :, lo:S], reads=rd, writes=h2Tb)
            else:
                kb.dma(SP, h2_ld, h2T[:, :, :], h2s_d[:, :, lo:hi], reads=rd, writes=h2Tb)

        def load_rope(b):
            kb.dma(SP, rope_ld, ropeT[:, :, :], rope_d[:, :, b * TB:(b + 1) * TB], writes=[bf("rope")])

        nbB = NB if stop_after is None else stop_after
        if limit < 4:
            nbB = 0
        else:
            load_blockB(0, 0)
            load_h2T(0)
            load_rope(0)
        pending_final = [None]
        for b in range(nbB):
            xi = b % 2
            if blimit >= 1:
                q_stage(b, extra=pending_final[0])
                pending_final[0] = None
            if b + 1 < nbB:
                load_blockB(b + 1, 1 - xi)
            if b + 1 < nbB:
                load_rope(b + 1)
            if blimit >= 2:
                attn_stage(b)
            if blimit >= 3:
                conv_stage(b)
            if b + 1 < nbB:
                load_h2T(b + 1)
            if blimit < 4:
                continue
            if debug and b == 0:
                for c in range(16):
                    tt = TT[6 + c % 2]
                    mf = mixT[:, :, :].rearrange("p a b -> p (a b)")
                    kb.op(DVE, lambda c=c, tt=tt, mf=mf: V.tensor_copy(out=tt[:, 0:256], in_=mf[:, c * 256:(c + 1) * 256]),
                          reads=[bf("mixT")], writes=[TTb[6 + c % 2]])
                    kb.dma(POOL, dbg_s2[c % 2], dbg_mix[:, c * 256:(c + 1) * 256], tt[:, 0:256], reads=[TTb[6 + c % 2]])
            mixout_stage(b, xi)
            if debug and b == 0:
                kb.dma(POOL, dbg_st, dbg_x2[:, :].rearrange("(t p) d -> p t d", p=128), XB[xi][:, :, :], reads=XBb[xi])
            if blimit >= 5:
                ffn(xi, 4, M_A3, M_B3, 3, U_F2)
            if blimit >= 6:
                pending_final[0] = final_gen(b, xi)
        if pending_final[0] is not None:
            for _ in pending_final[0]:
                pass

        for ds in XB_st + [dbg_st, misc_st, h2_st, wkv_ld] + dbg_s2 + ring_ss + ring_dc + stg_dc + stg_ss:
            if ds.count:
                POOL.wait((ds.sem, ds.count))
        for e in (PE, ACT, DVE, POOL):
            if e.n:
                SP.wait((e.sem, e.n))
        for ds in XB_ld + ring_ds + [h2_ld, rope_ld, misc_ld] + ada_ld:
            if ds.count:
                SP.wait((ds.sem, ds.count))
    return nc


def _consts():
    cm = np.zeros((128, 640), np.float32)
    cm[:, 0:128] = np.eye(128, dtype=np.float32)
    R = np.zeros((128, 128), np.float32)
    for i in range(64):
        R[2 * i + 1, 2 * i] = -1.0
        R[2 * i, 2 * i + 1] = 1.0
    cm[:, 128:256] = R
    bd = np.zeros((128, 128), np.float32)
    bd[0:64, 0:64] = 1.0 / 64.0
    bd[64:128, 64:128] = 1.0 / 64.0
    cm[:, 256:384] = bd
    cm[:, 384:512] = 1.0 / 512.0
    cm[:, 512:640] = 1.0
    GRID_W = 64
    rows = S // GRID_W
    row = np.repeat(np.arange(rows, dtype=np.float32), GRID_W)
    col = np.tile(np.arange(GRID_W, dtype=np.float32), rows)
    n_freq = 16
    inv = (np.float32(10000.0) ** (-np.arange(n_freq, dtype=np.float32) / np.float32(n_freq))).astype(np.float32)
    ang = np.concatenate([row[:, None] * inv, col[:, None] * inv], axis=-1).astype(np.float32)
    cos = np.cos(ang).astype(np.float32)
    sin = np.sin(ang).astype(np.float32)
    pi = (np.arange(128) % 64) // 2
    rope = np.stack([cos[:, pi].T, sin[:, pi].T], axis=1).astype(np.float32)
    return cm, np.ascontiguousarray(rope)


def _fm(v):
    return np.ascontiguousarray(np.asarray(v, np.float32).reshape(8, 128).T)


def make_in_maps(inputs, cores):
    cm, rope = _consts()
    par = np.zeros((128, NPAR), np.float32)
    par[:, 0:8] = _fm(inputs["norm_ffn1"][0])
    par[:, 8:16] = _fm(inputs["norm_mix"][0])
    par[:, 16:24] = _fm(inputs["norm_ffn2"][0])
    p = np.arange(128)
    par[:, 24] = np.asarray(inputs["q_norm"][0])[p % 64]
    par[:, 25] = np.asarray(inputs["k_norm"][0])[p % 64]
    aon = np.asarray(inputs["attn_out_norm"][0])
    for j in range(4):
        par[:, 26 + j] = aon[(j + 4 * (p // 64)) * 64 + (p % 64)]
    con = np.asarray(inputs["conv_out_norm"][0])
    cwv = np.asarray(inputs["conv_w"][0])
    for c in range(4):
        par[:, 30 + c] = con[c * 128 + p]
        for k in range(3):
            par[:, 34 + c * 3 + k] = cwv[k, c * 128 + p]
    par[:, 48:120] = np.asarray(inputs["b_ada"][0], np.float32).reshape(72, 128).T
    shared = {
        "w_ada": np.ascontiguousarray(inputs["w_ada"][0]),
        "b_ada": np.ascontiguousarray(np.asarray(inputs["b_ada"], np.float32).reshape(1, 9 * D)),
        "w1i": np.ascontiguousarray(inputs["w_ffn1_in"][0]),
        "w1o": np.ascontiguousarray(inputs["w_ffn1_out"][0]),
        "wmi": np.ascontiguousarray(inputs["w_mix_in"][0]),
        "wmo": np.ascontiguousarray(inputs["w_mix_out"][0]),
        "w2i": np.ascontiguousarray(inputs["w_ffn2_in"][0]),
        "w2o": np.ascontiguousarray(inputs["w_ffn2_out"][0]),
        "params": par,
        "fnorm": np.ascontiguousarray(np.asarray(inputs["final_norm"], np.float32).reshape(1, D)),
        "cmat": cm,
        "rope": rope,
    }
    maps = []
    cctx = np.asarray(inputs["c_ctx"], np.float32)
    for b in cores:
        cc = np.stack([_fm(inputs["c"][b]), _fm(cctx)], axis=1)
        m = dict(shared)
        m["x"] = np.ascontiguousarray(inputs["x"][b])
        m["ctx"] = np.ascontiguousarray(inputs["ctx"][b])
        m["cc"] = np.ascontiguousarray(cc)
        maps.append(m)
    return maps


_NC_CACHE = {}


def kernel(**inputs):
    inputs = {k: np.asarray(v) for k, v in inputs.items()}
    if "nc" not in _NC_CACHE:
        _NC_CACHE["nc"] = build_program()
    nc = _NC_CACHE["nc"]
    maps = make_in_maps(inputs, list(range(8)))
    res = run_bass_kernel_spmd(nc, maps, core_ids=list(range(8)))
    out = np.stack([np.asarray(r["out"], np.float32) for r in res.results], axis=0)
    return out
```
